# Optimizing a Trainium2 kernel written in Bass

```python
import math
import jax, jax.numpy as jnp
from jax import lax
import numpy as np

D_MODEL = 1024
BATCH = 4
SEQ = 8192
DEPTH = 4

CHUNK = 64
N_META = 16
PAD_FRONT = 128 - N_META
Q_BLOCK = 128
NORM_EPS = 1e-6
NEG_INF = -1e30
A_HEADS = 8
A_DIM = 64
IDX_HEADS = 8
IDX_DIM = 64
IDX_SCALE = (IDX_DIM ** -0.5) * (IDX_HEADS ** -0.5)
TOPK_MAX = 256
B_HEADS = 4
B_DK = 64
B_DV = 128
GLA_GATE_RANK = 16
GLA_TAU = 16.0
C_HEADS = 4
C_DK = 64
C_DV = 128
ROPE_BASE = 10000.0
D_HEADS = 8
D_DIM = 64
REL_BUCKETS = 32
REL_MAX_DIST = 128
SPLIT_AB = (A_HEADS * A_DIM, A_HEADS * A_DIM, A_HEADS * A_DIM, A_HEADS * A_DIM,
            IDX_HEADS * IDX_DIM, IDX_DIM, IDX_HEADS,
            B_HEADS * B_DK, B_HEADS * B_DK, B_HEADS * B_DV, B_HEADS * B_DV, GLA_GATE_RANK)
SPLIT_CD = (C_HEADS * C_DK, C_HEADS * C_DK, C_HEADS * C_DV, C_HEADS * C_DV,
            D_HEADS * D_DIM, D_HEADS * D_DIM, D_HEADS * D_DIM, D_HEADS * D_DIM)
W_AB = sum(SPLIT_AB)
W_CD = sum(SPLIT_CD)
MIX_AB = A_HEADS * A_DIM + B_HEADS * B_DV
MIX_CD = C_HEADS * C_DV + D_HEADS * D_DIM

kernel_name = 'hybrid_dsa_gla_retnet_stickbreak'


def rms_norm(x, g):
    xf = x.astype(jnp.float32)
    y = xf * lax.rsqrt(jnp.mean(xf * xf, -1, keepdims=True) + NORM_EPS)
    return (y * g.astype(jnp.float32)).astype(x.dtype)


def head_rms(o):
    return o * lax.rsqrt(jnp.mean(o * o, -1, keepdims=True) + NORM_EPS)


def head_layer_norm(o):
    o = o - jnp.mean(o, -1, keepdims=True)
    return o * lax.rsqrt(jnp.mean(o * o, -1, keepdims=True) + NORM_EPS)


def split_cols(t, sizes):
    return jnp.split(t, np.cumsum(sizes)[:-1].tolist(), axis=-1)


def rel_bucket(rel):
    half = REL_BUCKETS // 2
    max_exact = half // 2
    n = -rel
    ret = jnp.where(n < 0, half, 0)
    n = jnp.abs(n)
    nf = jnp.maximum(n, 1).astype(jnp.float32)
    large = max_exact + (jnp.log(nf / max_exact) / math.log(REL_MAX_DIST / max_exact)
                         * (half - max_exact)).astype(jnp.int32)
    large = jnp.minimum(large, half - 1)
    return ret + jnp.where(n < max_exact, n, large)


def rotary(x, pos):
    half = x.shape[-1] // 2
    inv = ROPE_BASE ** (-jnp.arange(half, dtype=jnp.float32) / half)
    ang = pos.astype(jnp.float32)[:, None] * inv[None, :]
    cos = jnp.cos(ang)[None, :, None, :]
    sin = jnp.sin(ang)[None, :, None, :]
    x1, x2 = x[..., :half], x[..., half:]
    return jnp.concatenate([x1 * cos - x2 * sin, x1 * sin + x2 * cos], -1)


def to_chunks(t):
    b, p, h, d = t.shape
    return t.reshape(b, p // CHUNK, CHUNK, h, d).transpose(0, 3, 1, 2, 4)


def from_chunks(t):
    b, h, n, c, e = t.shape
    return t.transpose(0, 2, 3, 1, 4).reshape(b, n * c, h, e)


def dsa_attention(q, k, v, iq, ik, iw, rel_bias, chunk, valid, topk):
    bsz, P, H, dh = q.shape
    take = jax.vmap(lambda t, ii: t[ii])

    def block(i):
        s0 = i * Q_BLOCK
        qb = lax.dynamic_slice_in_dim(q, s0, Q_BLOCK, axis=1)
        iqb = lax.dynamic_slice_in_dim(iq, s0, Q_BLOCK, axis=1)
        iwb = lax.dynamic_slice_in_dim(iw, s0, Q_BLOCK, axis=1)
        qpos = s0 + jnp.arange(Q_BLOCK, dtype=jnp.int32)
        qchunk = qpos // CHUNK
        adm = valid[None, :] & (chunk[None, :] <= qchunk[:, None])
        sc = jax.nn.relu(jnp.einsum('bqhd,bsd->bqhs', iqb, ik).astype(jnp.float32))
        score = jnp.einsum('bqhs,bqh->bqs', sc, iwb.astype(jnp.float32)) * IDX_SCALE
        score = jnp.where(adm[None], score, NEG_INF)
        _, idx = lax.top_k(score, topk)
        kg = take(k, idx)
        vg = take(v, idx)
        logits = jnp.einsum('bqhd,bqkhd->bhqk', qb, kg).astype(jnp.float32) * dh ** -0.5
        bias = rel_bias[rel_bucket(idx - qpos[None, :, None])]
        logits = logits + jnp.transpose(bias, (0, 3, 1, 2)).astype(jnp.float32)
        ok = valid[idx] & (chunk[idx] <= qchunk[None, :, None])
        logits = jnp.where(ok[:, None], logits, NEG_INF)
        p = jax.nn.softmax(logits, axis=-1).astype(v.dtype)
        return jnp.einsum('bhqk,bqkhd->bqhd', p, vg)

    out = lax.map(block, jnp.arange(P // Q_BLOCK))
    return jnp.transpose(out, (1, 0, 2, 3, 4)).reshape(bsz, P, H, dh)


def gla_chunked(q, k, v, log_a):
    dk = q.shape[-1]
    q, k, v, log_a = to_chunks(q) * dk ** -0.5, to_chunks(k), to_chunks(v), to_chunks(log_a)
    bcum = jnp.cumsum(log_a, axis=3)
    b_last = bcum[:, :, :, -1:, :]
    q_t = q * jnp.exp(bcum)
    k_t = k * jnp.exp(-bcum)
    causal = jnp.tril(jnp.ones((CHUNK, CHUNK), dtype=bool))
    att = jnp.where(causal, jnp.einsum('bhncd,bhnsd->bhncs', q_t, k_t), 0.0)
    o_intra = jnp.einsum('bhncs,bhnse->bhnce', att, v)
    contrib = jnp.einsum('bhncd,bhnce->bhnde', k * jnp.exp(b_last - bcum), v)
    decay = jnp.exp(b_last[:, :, :, 0, :])

    def step(S, inp):
        dec, con = inp
        return S * dec[..., None] + con, S

    S0 = jnp.zeros(contrib.shape[:2] + contrib.shape[3:], jnp.float32)
    _, S_before = lax.scan(step, S0, (jnp.moveaxis(decay, 2, 0), jnp.moveaxis(contrib, 2, 0)))
    S_before = jnp.moveaxis(S_before, 0, 2)
    o_inter = jnp.einsum('bhncd,bhnde->bhnce', q_t, S_before)
    return from_chunks(o_intra + o_inter)


def retention_chunked(q, k, v, log_gamma):
    dk = q.shape[-1]
    q, k, v = to_chunks(q), to_chunks(k) * dk ** -0.5, to_chunks(v)
    i = jnp.arange(CHUNK, dtype=jnp.float32)
    diff = i[:, None] - i[None, :]
    dmat = jnp.where(diff >= 0, jnp.exp(log_gamma[:, None, None] * jnp.maximum(diff, 0.0)), 0.0)
    att = jnp.einsum('bhncd,bhnsd->bhncs', q, k) * dmat[None, :, None]
    o_intra = jnp.einsum('bhncs,bhnse->bhnce', att, v)
    zeta = jnp.exp(log_gamma[:, None] * (CHUNK - 1 - i))
    xi = jnp.exp(log_gamma[:, None] * (i + 1))
    contrib = jnp.einsum('bhncd,bhnce->bhnde', k * zeta[None, :, None, :, None], v)
    chunk_decay = jnp.exp(log_gamma * CHUNK)[None, :, None, None]

    def step(S, con):
        return S * chunk_decay + con, S

    S0 = jnp.zeros(contrib.shape[:2] + contrib.shape[3:], jnp.float32)
    _, S_before = lax.scan(step, S0, jnp.moveaxis(contrib, 2, 0))
    S_before = jnp.moveaxis(S_before, 0, 2)
    o_inter = jnp.einsum('bhncd,bhnde->bhnce', q * xi[None, :, None, :, None], S_before)
    return from_chunks(o_intra + o_inter)


def stick_breaking(q, k, v, valid):
    bsz, P, H, dh = q.shape
    kpos = jnp.arange(P, dtype=jnp.int32)

    def block(i):
        s0 = i * Q_BLOCK
        qb = lax.dynamic_slice_in_dim(q, s0, Q_BLOCK, axis=1)
        qpos = s0 + jnp.arange(Q_BLOCK, dtype=jnp.int32)
        z = jnp.einsum('bqhd,bshd->bhqs', qb, k).astype(jnp.float32) * dh ** -0.5
        ok = ((kpos[None, :] < qpos[:, None]) & valid[None, :])[None, None]
        log_1m = jnp.where(ok, -jax.nn.softplus(z), 0.0)
        after = lax.cumsum(log_1m, axis=3, reverse=True) - log_1m
        w = jnp.where(ok, jnp.exp(jax.nn.log_sigmoid(z) + after), 0.0)
        return jnp.einsum('bhqs,bshd->bqhd', w.astype(v.dtype), v)

    out = lax.map(block, jnp.arange(P // Q_BLOCK))
    return jnp.transpose(out, (1, 0, 2, 3, 4)).reshape(bsz, P, H, dh)


def layer_ab(h, w_in, gate_w2, gate_b, w_out, rel_bias, chunk, valid, topk):
    bsz, P, _ = h.shape
    aq, ak, av, ag, iq, ik, iw, bq, bk, bv, bg, ba = split_cols(h @ w_in, SPLIT_AB)
    hd = lambda t, nh: t.reshape(bsz, P, nh, -1)
    oa = dsa_attention(hd(aq, A_HEADS), hd(ak, A_HEADS), hd(av, A_HEADS),
                       hd(iq, IDX_HEADS), ik, iw, rel_bias, chunk, valid, topk)
    oa = oa.reshape(bsz, P, -1) * jax.nn.silu(ag)
    log_a = jax.nn.log_sigmoid((ba @ gate_w2 + gate_b).astype(jnp.float32)) / GLA_TAU
    f = lambda t, nh: hd(t, nh).astype(jnp.float32)
    ob = gla_chunked(f(bq, B_HEADS), f(bk, B_HEADS), f(bv, B_HEADS), hd(log_a, B_HEADS))
    ob = head_rms(ob).reshape(bsz, P, -1).astype(h.dtype) * jax.nn.silu(bg)
    return jnp.concatenate([oa, ob], axis=-1) @ w_out


def layer_cd(h, w_in, w_out, log_gamma, pos, valid):
    bsz, P, _ = h.shape
    cq, ck, cv, cg, dq, dk, dv, dg = split_cols(h @ w_in, SPLIT_CD)
    hd = lambda t, nh: t.reshape(bsz, P, nh, -1)
    f = lambda t, nh: hd(t, nh).astype(jnp.float32)
    oc = retention_chunked(rotary(f(cq, C_HEADS), pos), rotary(f(ck, C_HEADS), pos),
                           f(cv, C_HEADS), log_gamma)
    oc = head_layer_norm(oc).reshape(bsz, P, -1).astype(h.dtype) * jax.nn.silu(cg)
    od = stick_breaking(hd(dq, D_HEADS), hd(dk, D_HEADS), hd(dv, D_HEADS), valid)
    od = od.reshape(bsz, P, -1) * jax.nn.silu(dg)
    return jnp.concatenate([oc, od], axis=-1) @ w_out


def setup_inputs(seed: int = 0) -> dict:
    key = jax.random.key(seed)
    ks = jax.random.split(key, 11)
    n_even = (DEPTH + 1) // 2
    n_odd = DEPTH // 2
    f32 = jnp.float32
    nrm = lambda k, shape, fan: jax.random.normal(k, shape, f32) * fan ** -0.5
    return {
        'x': jax.random.normal(ks[0], (BATCH, SEQ, D_MODEL), f32),
        'meta_tokens': jax.random.normal(ks[1], (N_META, D_MODEL), f32),
        'rel_bias': 0.1 * jax.random.normal(ks[2], (REL_BUCKETS, A_HEADS), f32),
        'norm_g': 1.0 + 0.02 * jax.random.normal(ks[3], (DEPTH, D_MODEL), f32),
        'final_g': 1.0 + 0.02 * jax.random.normal(ks[4], (D_MODEL,), f32),
        'w_in_ab': nrm(ks[5], (n_even, D_MODEL, W_AB), D_MODEL),
        'gla_gate_w2': nrm(ks[6], (n_even, GLA_GATE_RANK, B_HEADS * B_DK), GLA_GATE_RANK),
        'gla_gate_b': 0.1 * jax.random.normal(ks[7], (n_even, B_HEADS * B_DK), f32),
        'w_out_ab': nrm(ks[8], (n_even, MIX_AB, D_MODEL), MIX_AB),
        'w_in_cd': nrm(ks[9], (n_odd, D_MODEL, W_CD), D_MODEL),
        'w_out_cd': nrm(ks[10], (n_odd, MIX_CD, D_MODEL), MIX_CD),
    }


def reference(x, meta_tokens, rel_bias, norm_g, final_g, w_in_ab, gla_gate_w2, gla_gate_b,
              w_out_ab, w_in_cd, w_out_cd):
    bsz, seq, d = x.shape
    P = seq + PAD_FRONT + N_META
    h = jnp.concatenate([jnp.zeros((bsz, PAD_FRONT, d), x.dtype),
                         jnp.broadcast_to(meta_tokens.astype(x.dtype)[None], (bsz, N_META, d)),
                         x], axis=1)
    pos = jnp.arange(P, dtype=jnp.int32)
    chunk = pos // CHUNK
    valid = pos >= PAD_FRONT
    topk = min(TOPK_MAX, seq // 4)
    log_gamma = jnp.log(1.0 - jnp.exp2(-5.0 - jnp.arange(C_HEADS, dtype=jnp.float32)))
    for layer in range(DEPTH):
        hn = rms_norm(h, norm_g[layer])
        j = layer // 2
        if layer % 2 == 0:
            out = layer_ab(hn, w_in_ab[j], gla_gate_w2[j], gla_gate_b[j], w_out_ab[j],
                           rel_bias, chunk, valid, topk)
        else:
            out = layer_cd(hn, w_in_cd[j], w_out_cd[j], log_gamma, pos, valid)
        h = h + jnp.where(valid[None, :, None], out, 0.0).astype(h.dtype)
    return rms_norm(h[:, PAD_FRONT + N_META:], final_g)
```

```python
import contextlib
import numpy as np
import concourse.bass as bass
import concourse.mybir as mybir
from concourse.bass_utils import run_bass_kernel_spmd

F32 = mybir.dt.float32
BF16 = mybir.dt.bfloat16
ALU = mybir.AluOpType
AF = mybir.ActivationFunctionType
AX = mybir.AxisListType

D_MODEL = 1024
BATCH = 4
SEQ = 8192
PADF = 112
NMETA = 16
PTOK = SEQ + 128
NBLK = 66
PPAD = NBLK * 128
NCORES = 8
W_AB = 4184
W_CD = 3584
TOPK = 256
NEG = -1.0e30
LDBG = 9
EPS = 1e-6
SEM_ROLL = 30000

ENGS = ("pe", "act", "dve", "pool", "sp")


class Prog:
    def __init__(self, num_devices=None):
        if num_devices:
            self.nc = bass.Bass("TRN2", target_bir_lowering=False, num_devices=num_devices)
        else:
            self.nc = bass.Bass("TRN2", target_bir_lowering=False)
        self.q = {e: [] for e in ENGS}
        self.cnt = {e: 0 for e in ENGS}
        self.esem = {e: None for e in ENGS}
        self.keys = {}
        self.waited = {e: {} for e in ENGS}
        self.dsem = {}
        self.all_dma = []
        self.nsem = 0
        self.allsems = []
        self.fused = False
        self.prefix = ""
        self.sb_off = 16640
        self.banks = None
        self.bank_i = 0

    def new_sem(self, name):
        self.nsem += 1
        s = self.nc.alloc_semaphore(f"{name}_{self.nsem}")
        self.allsems.append(s)
        return s

    def dram(self, name, shape, dtype, kind="Internal"):
        return self.nc.dram_tensor(name, list(shape), dtype, kind=kind)

    def sb(self, name, shape, dtype):
        if not self.fused:
            return self.nc.alloc_sbuf_tensor(name, list(shape), dtype)
        nbytes = int(np.prod(shape[1:])) * (2 if dtype is BF16 else 4)
        nbytes = (nbytes + 31) // 32 * 32
        off = self.sb_off
        self.sb_off += nbytes
        assert self.sb_off <= 229120, (name, self.sb_off)
        self.sb_max = max(getattr(self, 'sb_max', 0), self.sb_off)
        return self.nc.alloc_sbuf_tensor_at(self.prefix + name, list(shape), dtype, offset=off)

    def ps(self, name, shape, dtype=F32):
        if not self.fused:
            return self.nc.alloc_psum_tensor(name, list(shape), dtype)
        if self.banks is None:
            self.banks = [self.nc.alloc_psum_tensor(f"bank{i}", [128, 512], F32) for i in range(8)]
        b = self.banks[self.bank_i]
        self.bank_i += 1
        assert self.bank_i <= 8
        return b

    def phase(self, name):
        toks = []
        for e in ENGS:
            if self.esem[e] is not None and self.cnt[e] > 0:
                toks.append((self.esem[e], self.cnt[e]))
        for ent in self.dsem.values():
            toks.append((ent[0], ent[1]))
        for e in ENGS:
            waits = []
            for sem, val in toks:
                if e == "pe" and sem is self.esem["pe"]:
                    continue
                if self.waited[e].get(id(sem), 0) >= val:
                    continue
                self.waited[e][id(sem)] = val
                waits.append((sem, val))
            if waits:
                self.q[e].append((waits, None, None, 0))
        self.prefix = name + "_"
        self.sb_off = 16640
        self.bank_i = 0

    def _deps(self, eng, reads, writes):
        deps = []
        for k in reads:
            st = self.keys.get(k)
            if st and st["w"]:
                deps.append(st["w"])
        for k in writes:
            st = self.keys.get(k)
            if st:
                if st["w"]:
                    deps.append(st["w"])
                deps.extend(st["r"])
        best = {}
        for sem, val in deps:
            if eng == "pe" and sem is self.esem["pe"]:
                continue
            sid = id(sem)
            if sid not in best or best[sid][1] < val:
                best[sid] = (sem, val)
        out = []
        for sid, (sem, val) in best.items():
            if self.waited[eng].get(sid, 0) >= val:
                continue
            self.waited[eng][sid] = val
            out.append((sem, val))
        return out

    def _mark(self, reads, writes, tok):
        for k in reads:
            st = self.keys.setdefault(k, {"w": None, "r": []})
            st["r"].append(tok)
        for k in writes:
            self.keys[k] = {"w": tok, "r": []}

    def op(self, eng, fn, reads=(), writes=()):
        waits = self._deps(eng, reads, writes)
        if self.esem[eng] is None or self.cnt[eng] >= SEM_ROLL:
            self.esem[eng] = self.new_sem("e" + eng)
            self.cnt[eng] = 0
        self.cnt[eng] += 1
        tok = (self.esem[eng], self.cnt[eng])
        self.q[eng].append((waits, fn, tok[0], 1))
        self._mark(reads, writes, tok)

    def dma(self, eng, out_ap, in_ap, reads=(), writes=(), skey=None):
        waits = self._deps(eng, reads, writes)
        if skey is None:
            skey = (list(writes) + list(reads))[0]
        ent = self.dsem.get(skey)
        if ent is None:
            ent = [self.new_sem("d"), 0]
            self.dsem[skey] = ent
        ent[1] += 16
        tok = (ent[0], ent[1])
        self.q[eng].append((waits, lambda e: e.dma_start(out=out_ap, in_=in_ap), tok[0], 16))
        self._mark(reads, writes, tok)

    def finish(self):
        fin = [(ent[0], ent[1]) for ent in self.dsem.values()]
        self.q["pool"].append((fin, None, None, 0))
        self.q["sp"].append((fin, None, None, 0))

    def emit(self):
        nc = self.nc

        def run(e, lst):
            for waits, fn, sem, inc in lst:
                for s, v in waits:
                    e.wait_ge(s, v)
                if fn is not None:
                    fn(e).then_inc(sem, inc)

        with nc.Block() as block:
            @block.tensor
            def _(e):
                run(e, self.q["pe"])

            @block.scalar
            def _(e):
                run(e, self.q["act"])

            @block.vector
            def _(e):
                run(e, self.q["dve"])

            @block.gpsimd
            def _(e):
                run(e, self.q["pool"])

            @block.sync
            def _(e):
                run(e, self.q["sp"])
        return nc


def build_P(T, WT, fchunks_bf, fchunks_f32, tgroups_bf, tgroups_f32, with_out, final=False, p=None, io=None,
            tdst=None):
    own = p is None
    NT = 384
    assert T % NT == 0
    ntile = T // NT
    KC = 8
    tdst = tdst or {}
    if own:
        p = Prog()
        A = {}
        A["hT"] = p.dram("hT", [D_MODEL, T], F32, kind="ExternalInput").ap()
        A["g"] = p.dram("g", [128, KC], F32, kind="ExternalInput").ap()
        if not final:
            A["w"] = p.dram("w", [D_MODEL, WT], F32, kind="ExternalInput").ap()
        if with_out:
            A["mixT"] = p.dram("mixT", [D_MODEL, T], BF16, kind="ExternalInput").ap()
            A["wo"] = p.dram("wo", [D_MODEL, D_MODEL], F32, kind="ExternalInput").ap()
            A["hT_new"] = p.dram("hT_new", [D_MODEL, T], F32, kind="ExternalOutput").ap()
        if final:
            A["y"] = p.dram("y", [D_MODEL, T], F32, kind="ExternalOutput").ap()
        else:
            nfb, nff = len(fchunks_bf), len(fchunks_f32)
            ctb = sum(n for _, n in tgroups_bf)
            ctf = sum(n for _, n in tgroups_f32)
            A["of_bf"] = p.dram("of_bf", [max(nfb, 1) * 128, T], BF16, kind="ExternalOutput").ap()
            A["of_f32"] = p.dram("of_f32", [max(nff, 1) * 128, T], F32, kind="ExternalOutput").ap()
            A["ot_bf"] = p.dram("ot_bf", [T, max(ctb, 1)], BF16, kind="ExternalOutput").ap()
            A["ot_f32"] = p.dram("ot_f32", [T, max(ctf, 1)], F32, kind="ExternalOutput").ap()
    else:
        A = io
    of_bf, of_f32, ot_bf, ot_f32 = A.get("of_bf"), A.get("of_f32"), A.get("ot_bf"), A.get("ot_f32")

    g_sb = p.sb("g_sb", [128, KC], F32)
    ones = p.sb("ones", [128, 128], F32)
    if not final:
        w_sb = p.sb("w_sb", [128, KC, WT], BF16)
        stg = [p.sb(f"stg{i}", [128, KC, 512], F32) for i in range(2)]
    if with_out:
        wo_sb = p.sb("wo_sb", [128, KC, D_MODEL], BF16)
        mix_sb = [p.sb(f"mix{i}", [128, KC, NT], BF16) for i in range(2)]
        if final:
            stg = [p.sb(f"stg{i}", [128, KC, 512], F32) for i in range(2)]
    h_sb = [p.sb(f"h{i}", [128, KC, NT], F32) for i in range(2)]
    sq_sb = p.sb("sq", [128, NT], F32)
    rstd = p.sb("rstd", [128, NT], F32)
    hn_sb = [p.sb(f"hn{i}", [128, KC, NT], BF16) for i in range(2)]
    NEV = 4
    ev_bf = [p.sb(f"evb{i}", [128, 512], BF16) for i in range(NEV)]
    ev_f = [p.sb(f"evf{i}", [128, 512], F32) for i in range(NEV)]
    pst = [p.ps(f"ps{i}", [128, 512], F32) for i in range(6)]
    ps_ss = p.ps("ps_ss", [128, 512], F32)

    p.dma("sp", g_sb[:, :], A["g"], writes=["g_sb"])
    p.op("pool", lambda e: e.memset(ones[:, :], 1.0), writes=["ones"])

    def load_w(dram_w, sb_w, ncols, tag):
        view = dram_w.rearrange("(c p) w -> p c w", p=128)
        npc = (ncols + 511) // 512
        for i in range(npc):
            a = i * 512
            b = min(ncols, a + 512)
            s = stg[i % 2]
            sk = f"stg{i % 2}"
            p.dma("sp", s[:, :, 0:b - a], view[:, :, a:b], writes=[sk])
            eng = "dve" if i % 2 == 0 else "pool"
            p.op(eng, lambda e, s=s, a=a, b=b: e.tensor_copy(out=sb_w[:, :, a:b], in_=s[:, :, 0:b - a]),
                 reads=[sk], writes=[f"{tag}_{i}"])
        return [f"{tag}_{i}" for i in range(npc)]

    wkeys = []
    if with_out:
        wokeys = load_w(A["wo"], wo_sb, D_MODEL, "wo")
    if not final:
        wkeys = load_w(A["w"], w_sb, WT, "w")

    hview = A["hT"].rearrange("(c p) t -> p c t", p=128)
    if with_out:
        mview = A["mixT"].rearrange("(c p) t -> p c t", p=128)
        if not final:
            hnview = A["hT_new"].rearrange("(c p) t -> p c t", p=128)
    if final:
        yview = A["y"].rearrange("(c p) t -> p c t", p=128)

    evi = [0]
    psi = [0]

    def next_ps():
        i = psi[0] % len(pst)
        psi[0] += 1
        return pst[i], f"ps{i}"

    def evac(ps_ap, pkey, nparts, ncols, dtype, dram_ap):
        i = evi[0] % NEV
        evi[0] += 1
        if dtype is BF16:
            t, tk = ev_bf[i], f"evb{i}"
        else:
            t, tk = ev_f[i], f"evf{i}"
        if evi[0] % 2 == 0:
            p.op("act", lambda e: e.activation(out=t[0:nparts, 0:ncols], in_=ps_ap, func=AF.Copy),
                 reads=[pkey], writes=[tk])
        else:
            p.op("dve", lambda e: e.tensor_copy(out=t[0:nparts, 0:ncols], in_=ps_ap),
                 reads=[pkey], writes=[tk])
        src = t[0:nparts, 0:ncols]
        if len(dram_ap.shape) == 3:
            src = src.rearrange("p (h d) -> p h d", h=dram_ap.shape[1])
        p.dma("pool", dram_ap, src, reads=[tk], skey=tk)

    for ti in range(ntile):
        t0 = ti * NT
        hb, hk = h_sb[ti % 2], f"h{ti % 2}"
        hnb, hnk = hn_sb[ti % 2], f"hn{ti % 2}"
        p.dma("sp", hb[:, :, :], hview[:, :, t0:t0 + NT], writes=[hk])
        if with_out:
            mb, mk = mix_sb[ti % 2], f"mix{ti % 2}"
            p.dma("sp", mb[:, :, :], mview[:, :, t0:t0 + NT], writes=[mk])
            for oc in range(KC):
                ps, pk = next_ps()
                for c in range(KC):
                    p.op("pe", lambda e, ps=ps, c=c, oc=oc, mb=mb: e.matmul(
                        ps[:, 0:NT], lhsT=wo_sb[:, c, oc * 128:(oc + 1) * 128], rhs=mb[:, c, :],
                        start=(c == 0), stop=(c == KC - 1)),
                        reads=[mk] + wokeys, writes=[pk])
                p.op("dve", lambda e, ps=ps, oc=oc, hb=hb: e.tensor_tensor(
                    out=hb[:, oc, :], in0=hb[:, oc, :], in1=ps[:, 0:NT], op=ALU.add),
                    reads=[pk, hk], writes=[hk])
            if not final:
                p.dma("pool", hnview[:, :, t0:t0 + NT], hb[:, :, :], reads=[hk], skey=hk + "_st")
        for c in range(KC):
            p.op("act", lambda e, c=c, hb=hb: e.activation(out=sq_sb[:, :], in_=hb[:, c, :], func=AF.Square),
                 reads=[hk], writes=["sq"])
            p.op("pe", lambda e, c=c: e.matmul(ps_ss[:, 0:NT], lhsT=ones[:, :], rhs=sq_sb[:, :],
                                                 start=(c == 0), stop=(c == KC - 1)),
                 reads=["sq", "ones"], writes=["ps_ss"])
        p.op("act", lambda e: e.activation(out=rstd[:, :], in_=ps_ss[:, 0:NT], func=AF.Sqrt,
                                           bias=EPS, scale=1.0 / D_MODEL),
             reads=["ps_ss"], writes=["rstd"])
        p.op("dve", lambda e: e.reciprocal(out=rstd[:, :], in_=rstd[:, :]),
             reads=["rstd"], writes=["rstd"])
        if final:
            for c in range(KC):
                p.op("dve", lambda e, c=c, hb=hb: e.scalar_tensor_tensor(
                    out=hb[:, c, :], in0=hb[:, c, :], scalar=g_sb[:, c:c + 1], in1=rstd[:, :],
                    op0=ALU.mult, op1=ALU.mult), reads=[hk, "rstd", "g_sb"], writes=[hk])
            p.dma("pool", yview[:, :, t0:t0 + NT], hb[:, :, :], reads=[hk], skey=hk + "_st")
            continue
        for c in range(KC):
            p.op("dve", lambda e, c=c, hb=hb, hnb=hnb: e.scalar_tensor_tensor(
                out=hnb[:, c, :], in0=hb[:, c, :], scalar=g_sb[:, c:c + 1], in1=rstd[:, :],
                op0=ALU.mult, op1=ALU.mult), reads=[hk, "rstd", "g_sb"], writes=[hnk])
        for lst, dt_, dram_o in ((fchunks_bf, BF16, of_bf), (fchunks_f32, F32, of_f32)):
            for ci, (c0, ncol) in enumerate(lst):
                ps, pk = next_ps()
                for c in range(KC):
                    p.op("pe", lambda e, ps=ps, c=c, c0=c0, ncol=ncol, hnb=hnb: e.matmul(
                        ps[0:ncol, 0:NT], lhsT=w_sb[:, c, c0:c0 + ncol], rhs=hnb[:, c, :],
                        start=(c == 0), stop=(c == KC - 1)), reads=[hnk] + wkeys, writes=[pk])
                evac(ps[0:ncol, 0:NT], pk, ncol, NT, dt_, dram_o[ci * 128:ci * 128 + ncol, t0:t0 + NT])
        for blk in range(NT // 128):
            for lst, dt_, dram_o in ((tgroups_bf, BF16, ot_bf), (tgroups_f32, F32, ot_f32)):
                oc0 = 0
                for (c0, ncol) in lst:
                    ps, pk = next_ps()
                    for c in range(KC):
                        p.op("pe", lambda e, ps=ps, c=c, c0=c0, ncol=ncol, hnb=hnb, blk=blk: e.matmul(
                            ps[:, 0:ncol], lhsT=hnb[:, c, blk * 128:(blk + 1) * 128], rhs=w_sb[:, c, c0:c0 + ncol],
                            start=(c == 0), stop=(c == KC - 1)), reads=[hnk] + wkeys, writes=[pk])
                    if (dt_ is BF16) and (c0 in tdst):
                        dst_ap = tdst[c0](ti * (NT // 128) + blk)
                    else:
                        dst_ap = dram_o[t0 + blk * 128:t0 + (blk + 1) * 128, oc0:oc0 + ncol]
                    evac(ps[:, 0:ncol], pk, 128, ncol, dt_, dst_ap)
                    oc0 += ncol
    if own:
        p.finish()
        return p.emit()


def sb_list():
    sbs = []
    b = 0
    while b < NBLK:
        nb = min(4, NBLK - b)
        sbs.append((b, nb))
        b += nb
    return sbs


def build_D(p=None, io=None):
    own = p is None
    NH = 4
    if own:
        p = Prog()
        qT = p.dram("qT", [NH, 64, PPAD], BF16, kind="ExternalInput")
        kT = p.dram("kT", [NH, 64, PPAD], BF16, kind="ExternalInput")
        v = p.dram("v", [PPAD, NH * 64], BF16, kind="ExternalInput")
        gT = p.dram("gT", [NH, 64, PPAD], F32, kind="ExternalInput")
        masks = p.dram("masks", [6, 128, 512], BF16, kind="ExternalInput")
        ut = p.dram("ut", [128, 256], BF16, kind="ExternalInput")
        oT = p.dram("oT", [NH, 64, PPAD], BF16, kind="ExternalOutput")
        A_ = {"qT": qT.ap(), "kT": kT.ap(), "v": v.ap(), "gT": gT.ap(), "masks": masks.ap(), "ut": ut.ap(),
              "oT": oT.ap()}
    else:
        A_ = io

    k_sb = p.sb("k_sb", [64, NH, PPAD], BF16)
    v_sb = p.sb("v_sb", [128, NBLK, NH * 64], BF16)
    m_sb = p.sb("m_sb", [128, 6, 512], BF16)
    u_sb = p.sb("u_sb", [128, 256], BF16)
    q_sb = [p.sb(f"q{i}", [64, NH, 512], BF16) for i in range(2)]
    g_sb = [p.sb(f"g{i}", [64, 512], F32) for i in range(2)]
    e1 = [p.sb(f"e1_{i}", [128, 512], F32) for i in range(2)]
    DEP = 3
    NSP = DEP + 2
    NA = DEP + 3
    sp = [p.sb(f"sp{i}", [128, 512], BF16) for i in range(NSP)]
    NW = 3
    wt = [p.sb(f"wt{i}", [128, 512], BF16) for i in range(NW)]
    acc = p.sb("acc", [128, 512], BF16)
    sg = p.sb("sg", [64, 512], F32)
    ob = [p.sb(f"ob{i}", [64, 512], BF16) for i in range(2)]
    psA = [p.ps(f"psA{i}", [128, 512]) for i in range(NA)]
    psC = [p.ps(f"psC{i}", [128, 512]) for i in range(2)]

    for h in range(NH):
        p.dma("sp", k_sb[:, h, :], A_["kT"][h], writes=[f"k{h}"])
    p.dma("sp", v_sb[:, :, :], A_["v"].rearrange("(b p) f -> p b f", p=128), writes=["v_sb"])
    p.dma("sp", m_sb[:, :, :], A_["masks"].rearrange("m p t -> p m t"), writes=["m_sb"])
    p.dma("sp", u_sb[:, :], A_["ut"], writes=["u_sb"])

    sbs = sb_list()
    its = []
    for si, (b0, nb) in enumerate(sbs):
        for h in range(NH):
            kend = b0 + nb - 1
            for kb in range(kend, -1, -1):
                its.append((si, b0, nb, h, kb, kend))
    n_it = len(its)
    cnt = {"ph": -1}

    def mask_idx(b0, kb):
        if kb >= b0:
            m = kb - b0
            if kb == 0:
                return 5
            return m
        if kb == 0:
            return 4
        return None

    def stage1(i):
        si, b0, nb, h, kb, kend = its[i]
        N = nb * 128
        qb, qk = q_sb[si % 2], f"q{si % 2}"
        if h == 0 and kb == kend:
            p.dma("sp", qb[:, :, 0:N], A_["qT"][:, :, b0 * 128:b0 * 128 + N].rearrange("h d t -> d h t"),
                  writes=[qk])
        A, ak = psA[i % NA], f"psA{i % NA}"
        p.op("pe", lambda e: e.matmul(A[:, 0:N], lhsT=k_sb[:, h, kb * 128:(kb + 1) * 128], rhs=qb[:, h, 0:N],
                                      start=True, stop=False), reads=[f"k{h}", qk], writes=[ak])
        E, ek = e1[i % 2], f"e1_{i % 2}"
        p.op("act", lambda e: e.activation(out=E[:, 0:N], in_=A[:, 0:N], func=AF.Exp, scale=0.125),
             reads=[ak], writes=[ek])
        S, sk = sp[i % NSP], f"sp{i % NSP}"
        p.op("act", lambda e: e.activation(out=S[:, 0:N], in_=E[:, 0:N], func=AF.Ln, bias=1.0, scale=1.0),
             reads=[ek], writes=[sk])
        mi = mask_idx(b0, kb)
        if mi is not None:
            p.op("pool", lambda e: e.tensor_tensor(out=S[:, 0:N], in0=S[:, 0:N], in1=m_sb[:, mi, 0:N], op=ALU.mult),
                 reads=[sk, "m_sb"], writes=[sk])

    def stage2(i):
        si, b0, nb, h, kb, kend = its[i]
        N = nb * 128
        qb, qk = q_sb[si % 2], f"q{si % 2}"
        S, sk = sp[i % NSP], f"sp{i % NSP}"
        B, bk = psA[i % NA], f"psA{i % NA}"
        hi = si * NH + h
        C, ck = psC[hi % 2], f"psC{hi % 2}"
        first = (kb == kend)
        p.op("pe", lambda e: e.matmul(B[:, 0:N], lhsT=u_sb[:, 0:128], rhs=S[:, 0:N],
                                      start=False, stop=first), reads=[sk, "u_sb"], writes=[bk])
        if not first:
            p.op("pe", lambda e: e.matmul(B[:, 0:N], lhsT=u_sb[:, 128:256], rhs=acc[:, 0:N],
                                          start=False, stop=True), reads=["acc", "u_sb"], writes=[bk])
        W, wk = wt[i % NW], f"wt{i % NW}"
        p.op("act", lambda e: e.activation(out=W[:, 0:N], in_=B[:, 0:N], func=AF.Exp, scale=0.125),
             reads=[bk], writes=[wk])
        mi = mask_idx(b0, kb)
        if mi is not None:
            p.op("pool", lambda e: e.tensor_tensor(out=W[:, 0:N], in0=W[:, 0:N], in1=m_sb[:, mi, 0:N], op=ALU.mult),
                 reads=[wk, "m_sb"], writes=[wk])
        if first:
            p.op("dve", lambda e: e.tensor_copy(out=acc[:, 0:N], in_=S[:, 0:N]), reads=[sk], writes=["acc"])
        elif kb > 0:
            p.op("dve", lambda e: e.tensor_tensor(out=acc[:, 0:N], in0=acc[:, 0:N], in1=S[:, 0:N], op=ALU.add),
                 reads=[sk, "acc"], writes=["acc"])

    def stage2b(i):
        si, b0, nb, h, kb, kend = its[i]
        N = nb * 128
        hi = si * NH + h
        C, ck = psC[hi % 2], f"psC{hi % 2}"
        first = (kb == kend)
        W, wk = wt[i % NW], f"wt{i % NW}"
        p.op("pe", lambda e: e.matmul(C[0:64, 0:N], lhsT=v_sb[:, kb, h * 64:(h + 1) * 64], rhs=W[:, 0:N],
                                      start=first, stop=(kb == 0)), reads=[wk, "v_sb"], writes=[ck])
        if kb == 0:
            G, gk = g_sb[hi % 2], f"g{hi % 2}"
            p.dma("sp", G[:, 0:N], A_["gT"][h, :, b0 * 128:b0 * 128 + N], writes=[gk])
            p.op("act", lambda e: e.activation(out=sg[:, 0:N], in_=G[:, 0:N], func=AF.Exp, scale=-1.0),
                 reads=[gk], writes=["sg"])
            p.op("dve", lambda e: e.tensor_scalar(out=sg[:, 0:N], in0=sg[:, 0:N], scalar1=1.0, scalar2=None,
                                                  op0=ALU.add), reads=["sg"], writes=["sg"])
            p.op("dve", lambda e: e.reciprocal(out=sg[:, 0:N], in_=sg[:, 0:N]), reads=["sg"], writes=["sg"])
            p.op("dve", lambda e: e.tensor_tensor(out=sg[:, 0:N], in0=sg[:, 0:N], in1=G[:, 0:N], op=ALU.mult),
                 reads=["sg", gk], writes=["sg"])
            O, ok_ = ob[hi % 2], f"ob{hi % 2}"
            p.op("dve", lambda e: e.tensor_tensor(out=O[:, 0:N], in0=C[0:64, 0:N], in1=sg[:, 0:N], op=ALU.mult),
                 reads=[ck, "sg"], writes=[ok_])
            p.dma("pool", A_["oT"][h, :, b0 * 128:b0 * 128 + N], O[:, 0:N], reads=[ok_], skey=ok_)

    for i in range(n_it + DEP + 1):
        if i < n_it:
            stage1(i)
        if 0 <= i - DEP < n_it:
            stage2(i - DEP)
        if 0 <= i - DEP - 1 < n_it:
            stage2b(i - DEP - 1)
    if own:
        p.finish()
        return p.emit()


def d_consts():
    import ml_dtypes
    masks = np.zeros((6, 128, 512), np.float32)
    s = np.arange(128)[:, None]
    t = np.arange(512)[None, :]
    for m in range(4):
        j = t // 128
        tl = t % 128
        masks[m] = np.where(j > m, 1.0, np.where(j == m, (tl > s).astype(np.float32), 0.0))
    masks[4] = (s >= PADF).astype(np.float32) * np.ones((128, 512), np.float32)
    masks[5] = masks[0] * masks[4]
    ut = np.zeros((128, 256), np.float32)
    jj = np.arange(128)[:, None]
    ss = np.arange(128)[None, :]
    ut[:, 0:128] = np.where(jj >= ss, -8.0, 0.0)
    ut[:, 128:256] = -8.0
    return masks.astype(ml_dtypes.bfloat16), ut.astype(ml_dtypes.bfloat16)


def build_L(kind, p=None, io=None):
    own = p is None
    gla = (kind == "gla")
    if own:
        p = Prog()
        A_ = {}
        A_["qT"] = p.dram("qT", [128, PPAD], BF16, kind="ExternalInput").ap()
        A_["kT"] = p.dram("kT", [128, PPAD], BF16, kind="ExternalInput").ap()
        A_["v"] = p.dram("v", [PPAD, 256], BF16, kind="ExternalInput").ap()
        A_["gT"] = p.dram("gT", [256, PPAD], F32, kind="ExternalInput").ap()
        A_["cst"] = p.dram("cst", [128, 3, 256], F32, kind="ExternalInput").ap()
        if gla:
            A_["baT"] = p.dram("baT", [16, PPAD], F32, kind="ExternalInput").ap()
            A_["w2a"] = p.dram("w2a", [17, 128], F32, kind="ExternalInput").ap()
        else:
            A_["qsT"] = p.dram("qsT", [128, PPAD], BF16, kind="ExternalInput").ap()
            A_["ksT"] = p.dram("ksT", [128, PPAD], BF16, kind="ExternalInput").ap()
            A_["tabs"] = p.dram("tabs", [4, 128, PPAD], F32, kind="ExternalInput").ap()
            A_["decc"] = p.dram("decc", [64, 2], F32, kind="ExternalInput").ap()
        A_["oT"] = p.dram("oT", [256, PPAD], BF16, kind="ExternalOutput").ap()
    else:
        A_ = io

    cst_sb = p.sb("cst_sb", [128, 3, 256], F32)
    idb = p.sb("idb", [64, 64], BF16)
    m4 = p.sb("m4", [64, 256], BF16)
    ones = p.sb("ones", [128, 128], F32)
    S = p.sb("S", [64, 2, 128], F32)
    Sb = p.sb("Sb", [64, 2, 128], BF16)
    St = p.sb("St", [64, 2, 128], F32)
    NB2 = 2
    NBI = 4
    q_in = [p.sb(f"q_in{i}", [64, 2, 128], BF16) for i in range(NBI)]
    k_in = [p.sb(f"k_in{i}", [64, 2, 128], BF16) for i in range(NBI)]
    v_in = [p.sb(f"v_in{i}", [64, 2, 256], BF16) for i in range(NBI)]
    g_in = [p.sb(f"g_in{i}", [128, 2, 128], F32) for i in range(NBI)]
    qt = [p.sb(f"qt{i}", [64, 2, 128], BF16) for i in range(NB2)]
    kt = [p.sb(f"kt{i}", [64, 2, 128], BF16) for i in range(NB2)]
    ktt = [p.sb(f"ktt{i}", [64, 256], BF16) for i in range(NB2)]
    dec = [p.sb(f"dec{i}", [64, 4], F32) for i in range(NB2)]
    attm = [p.sb(f"attm{i}", [64, 256], BF16) for i in range(NB2)]
    if gla:
        ba_in = [p.sb(f"ba_in{i}", [17, 128], F32) for i in range(NBI)]
        w2_sb = p.sb("w2_sb", [17, 128], F32)
        la = p.sb("la", [128, 128], F32)
        ep = p.sb("ep", [64, 256], F32)
        em = p.sb("em", [64, 256], F32)
    else:
        qs_in = [p.sb(f"qs_in{i}", [64, 2, 128], BF16) for i in range(NBI)]
        ks_in = [p.sb(f"ks_in{i}", [64, 2, 128], BF16) for i in range(NBI)]
        tb_in = [p.sb(f"tb_in{i}", [64, 4, 2, 128], F32) for i in range(NBI)]
        t1 = p.sb("t1", [64, 256], F32)
        t2 = p.sb("t2", [64, 256], F32)
        dec_c = p.sb("dec_c", [64, 2], F32)
    ofs = [p.sb(f"of{i}", [128, 256], F32) for i in range(2)]
    osq = p.sb("osq", [128, 256], F32)
    rs = p.sb("rs", [128, 256], F32)
    sg = p.sb("sg", [128, 256], F32)
    yb = [p.sb(f"yb{i}", [128, 2, 128], BF16) for i in range(2)]
    ln8t = p.sb("ln8t", [128, 1], F32)
    epst = p.sb("epst", [128, 1], F32)

    ps_x = p.ps("ps_x", [128, 512])
    ps_c = p.ps("ps_c", [128, 512])
    ps_t = p.ps("ps_t", [128, 512])
    ps_a = p.ps("ps_a", [128, 512])
    ps_o = p.ps("ps_o", [128, 512])
    ps_kv = p.ps("ps_kv", [128, 512])
    ps_n = p.ps("ps_n", [128, 512])

    p.dma("sp", cst_sb[:, :, :], A_["cst"], writes=["cst"])
    p.op("dve", lambda e: e.tensor_copy(out=m4[:, :], in_=cst_sb[0:64, 1, :]), reads=["cst"], writes=["m4"])
    p.op("dve", lambda e: e.tensor_copy(out=idb[:, :], in_=cst_sb[0:64, 2, 0:64]), reads=["cst"], writes=["idb"])
    p.op("pool", lambda e: e.memset(ones[:, :], 1.0 / 128.0), writes=["ones"])
    p.op("pool", lambda e: e.memset(S[:, :, :], 0.0), writes=["S"])
    p.op("pool", lambda e: e.memset(Sb[:, :, :], 0.0), writes=["Sb"])
    p.op("pool", lambda e: e.memset(ln8t[:, :], float(np.log(0.125))), writes=["ln8"])
    p.op("pool", lambda e: e.memset(epst[:, :], EPS), writes=["epsn"])
    if gla:
        if isinstance(A_["w2a"], tuple):
            p.dma("sp", w2_sb[0:16, :], A_["w2a"][0], writes=["w2"])
            p.dma("sp", w2_sb[16:17, :], A_["w2a"][1], writes=["w2"])
        else:
            p.dma("sp", w2_sb[:, :], A_["w2a"], writes=["w2"])
        for i in range(NBI):
            p.op("pool", lambda e, i=i: e.memset(ba_in[i][:, :], 1.0), writes=[f"ba_in{i}"])
    else:
        p.dma("sp", dec_c[:, :], A_["decc"], writes=["dec_c"])

    gview = A_["gT"].rearrange("(h e) t -> e h t", h=2)
    oview = A_["oT"].rearrange("(h e) t -> e h t", h=2)
    hd = lambda ap_: ap_.rearrange("(h d) t -> d h t", h=2)

    def load(b):
        i = b % NBI
        c0 = b * 128
        p.dma("sp", q_in[i][:, :, :], hd(A_["qT"])[:, :, c0:c0 + 128], writes=[f"q_in{i}"])
        p.dma("sp", k_in[i][:, :, :], hd(A_["kT"])[:, :, c0:c0 + 128], writes=[f"k_in{i}"])
        p.dma("sp", v_in[i][:, :, :], A_["v"][c0:c0 + 128, :].rearrange("(n s) f -> s n f", n=2), writes=[f"v_in{i}"])
        p.dma("sp", g_in[i][:, :, :], gview[:, :, c0:c0 + 128], writes=[f"g_in{i}"])
        if gla:
            p.dma("sp", ba_in[i][0:16, :], A_["baT"][:, c0:c0 + 128], writes=[f"ba_in{i}"])
        else:
            p.dma("sp", qs_in[i][:, :, :], hd(A_["qsT"])[:, :, c0:c0 + 128], writes=[f"qs_in{i}"])
            p.dma("sp", ks_in[i][:, :, :], hd(A_["ksT"])[:, :, c0:c0 + 128], writes=[f"ks_in{i}"])
            for f in range(4):
                p.dma("sp", tb_in[i][:, f, :, :], hd(A_["tabs"][f])[:, :, c0:c0 + 128], writes=[f"tb_in{i}"])

    fl = lambda t_: t_.rearrange("p h t -> p (h t)")

    def prep(b):
        i = b % NB2
        ii = b % NBI
        if gla:
            p.op("pe", lambda e: e.matmul(ps_x[:, 0:128], lhsT=ba_in[ii][:, :], rhs=w2_sb[:, :], start=True, stop=True),
                 reads=[f"ba_in{ii}", "w2"], writes=["ps_x"])
            p.op("act", lambda e: e.activation(out=la[:, :], in_=ps_x[:, 0:128], func=AF.Exp, scale=-1.0),
                 reads=["ps_x"], writes=["la"])
            p.op("act", lambda e: e.activation(out=la[:, :], in_=la[:, :], func=AF.Ln, bias=1.0),
                 reads=["la"], writes=["la"])
            for h in range(2):
                p.op("pe", lambda e, h=h: e.matmul(ps_c[0:64, h * 128:(h + 1) * 128], lhsT=la[:, h * 64:(h + 1) * 64],
                                                   rhs=cst_sb[:, 0, 0:128], start=True, stop=True),
                     reads=["la", "cst"], writes=["ps_c"])
            p.op("act", lambda e: e.activation(out=ep[:, :], in_=ps_c[0:64, 0:256], func=AF.Exp, bias=ln8t[0:64, 0:1]),
                 reads=["ps_c", "ln8"], writes=["ep"])
            p.op("act", lambda e: e.activation(out=em[:, :], in_=ps_c[0:64, 0:256], func=AF.Exp, scale=-1.0),
                 reads=["ps_c"], writes=["em"])
            p.op("act", lambda e: e.activation(out=dec[i][:, :], in_=ps_c[0:64, 63:256:64], func=AF.Exp),
                 reads=["ps_c"], writes=[f"dec{i}"])
            p.op("dve", lambda e: e.tensor_tensor(out=fl(qt[i][:, :, :]), in0=fl(q_in[ii][:, :, :]), in1=ep[:, :], op=ALU.mult),
                 reads=[f"q_in{ii}", "ep"], writes=[f"qt{i}"])
            p.op("dve", lambda e: e.tensor_tensor(out=fl(kt[i][:, :, :]), in0=fl(k_in[ii][:, :, :]), in1=em[:, :], op=ALU.mult),
                 reads=[f"k_in{ii}", "em"], writes=[f"kt{i}"])
        else:
            for (a_in, s_in, fa, fs, dst, dk_) in ((q_in, qs_in, 0, 1, qt, "qt"), (k_in, ks_in, 2, 3, kt, "kt")):
                p.op("dve", lambda e, a_in=a_in, fa=fa: e.tensor_tensor(
                    out=t1[:, :], in0=fl(a_in[ii][:, :, :]), in1=fl(tb_in[ii][:, fa, :, :]), op=ALU.mult),
                    reads=[f"q_in{ii}", f"k_in{ii}", f"tb_in{ii}"], writes=["t1"])
                p.op("pool", lambda e, s_in=s_in, fs=fs: e.tensor_tensor(
                    out=t2[:, :], in0=fl(s_in[ii][:, :, :]), in1=fl(tb_in[ii][:, fs, :, :]), op=ALU.mult),
                    reads=[f"qs_in{ii}", f"ks_in{ii}", f"tb_in{ii}"], writes=["t2"])
                p.op("dve", lambda e, dst=dst: e.tensor_tensor(out=fl(dst[i][:, :, :]), in0=t1[:, :], in1=t2[:, :], op=ALU.add),
                     reads=["t1", "t2"], writes=[f"{dk_}{i}"])
        for h in range(2):
            for n in range(2):
                j = h * 2 + n
                p.op("pe", lambda e, h=h, n=n, j=j: e.matmul(ps_t[0:64, j * 64:(j + 1) * 64], lhsT=kt[i][:, h, n * 64:(n + 1) * 64],
                                                            rhs=idb[:, :], start=True, stop=True),
                     reads=[f"kt{i}", "idb"], writes=["ps_t"])
        p.op("act", lambda e: e.activation(out=ktt[i][:, :], in_=ps_t[0:64, 0:256], func=AF.Copy),
             reads=["ps_t"], writes=[f"ktt{i}"])

    def core(b):
        i = b % NB2
        ii = b % NBI
        c0 = b * 128
        for h in range(2):
            for n in range(2):
                j = h * 2 + n
                p.op("pe", lambda e, h=h, n=n, j=j: e.matmul(
                    ps_a[0:64, j * 64:(j + 1) * 64], lhsT=kt[i][:, h, n * 64:(n + 1) * 64], rhs=qt[i][:, h, n * 64:(n + 1) * 64],
                    start=True, stop=True), reads=[f"kt{i}", f"qt{i}"], writes=["ps_a"])
        p.op("dve", lambda e: e.tensor_tensor(out=attm[i][:, :], in0=ps_a[0:64, 0:256], in1=m4[:, :], op=ALU.mult),
             reads=["ps_a", "m4"], writes=[f"attm{i}"])
        for n in range(2):
            for h in range(2):
                j = h * 2 + n
                oc = slice(h * 128 + n * 64, h * 128 + (n + 1) * 64)
                p.op("pe", lambda e, n=n, h=h, j=j, oc=oc: e.matmul(
                    ps_o[:, oc], lhsT=v_in[ii][:, n, h * 128:(h + 1) * 128], rhs=attm[i][:, j * 64:(j + 1) * 64],
                    start=True, stop=False), reads=[f"v_in{ii}", f"attm{i}"], writes=["ps_o"])
                p.op("pe", lambda e, n=n, h=h, oc=oc: e.matmul(
                    ps_o[:, oc], lhsT=Sb[:, h, :], rhs=qt[i][:, h, n * 64:(n + 1) * 64], start=False, stop=True),
                    reads=["Sb", f"qt{i}"], writes=["ps_o"])
            for h in range(2):
                j = h * 2 + n
                p.op("pe", lambda e, n=n, h=h, j=j: e.matmul(
                    ps_kv[0:64, h * 128:(h + 1) * 128], lhsT=ktt[i][:, j * 64:(j + 1) * 64], rhs=v_in[ii][:, n, h * 128:(h + 1) * 128],
                    start=True, stop=True), reads=[f"ktt{i}", f"v_in{ii}"], writes=["ps_kv"])
            for h in range(2):
                j = h * 2 + n
                dsc = dec[i][:, j:j + 1] if gla else dec_c[:, h:h + 1]
                dkey = f"dec{i}" if gla else "dec_c"
                p.op("pool", lambda e, dsc=dsc, h=h: e.tensor_scalar(out=St[:, h, :], in0=S[:, h, :], scalar1=dsc, scalar2=None,
                                                                     op0=ALU.mult), reads=["S", dkey], writes=["St"])
                p.op("dve", lambda e, dsc=dsc, h=h: e.scalar_tensor_tensor(
                    out=S[:, h, :], in0=ps_kv[0:64, h * 128:(h + 1) * 128], scalar=dsc, in1=St[:, h, :],
                    op0=ALU.mult, op1=ALU.add), reads=["ps_kv", "St", dkey], writes=["S"])
            p.op("act", lambda e: e.activation(out=fl(Sb[:, :, :]), in_=fl(S[:, :, :]), func=AF.Copy), reads=["S"], writes=["Sb"])
        OF, ofk = ofs[b % 2], f"of{b % 2}"
        p.op("act", lambda e: e.activation(out=OF[:, :], in_=ps_o[:, 0:256], func=AF.Copy), reads=["ps_o"], writes=[ofk])

    def tail(b):
        ii = b % NBI
        c0 = b * 128
        of, ofk = ofs[b % 2], f"of{b % 2}"
        if not gla:
            p.op("pe", lambda e: e.matmul(ps_n[:, 0:256], lhsT=ones[:, :], rhs=of[:, :], start=True, stop=True),
                 reads=[ofk, "ones"], writes=["ps_n"])
            p.op("dve", lambda e: e.tensor_tensor(out=of[:, :], in0=of[:, :], in1=ps_n[:, 0:256], op=ALU.subtract),
                 reads=[ofk, "ps_n"], writes=[ofk])
        p.op("act", lambda e: e.activation(out=osq[:, :], in_=of[:, :], func=AF.Square), reads=[ofk], writes=["osq"])
        p.op("pe", lambda e: e.matmul(ps_n[:, 256:512], lhsT=ones[:, :], rhs=osq[:, :], start=True, stop=True),
             reads=["osq", "ones"], writes=["ps_n"])
        p.op("act", lambda e: e.activation(out=rs[:, :], in_=ps_n[:, 256:512], func=AF.Sqrt, bias=epst[:, 0:1], scale=1.0),
             reads=["ps_n", "epsn"], writes=["rs"])
        p.op("dve", lambda e: e.reciprocal(out=rs[:, :], in_=rs[:, :]), reads=["rs"], writes=["rs"])
        p.op("dve", lambda e: e.tensor_tensor(out=of[:, :], in0=of[:, :], in1=rs[:, :], op=ALU.mult),
             reads=[ofk, "rs"], writes=[ofk])
        G = fl(g_in[ii][:, :, :])
        p.op("act", lambda e: e.activation(out=sg[:, :], in_=G, func=AF.Exp, scale=-1.0), reads=[f"g_in{ii}"], writes=["sg"])
        p.op("dve", lambda e: e.tensor_scalar(out=sg[:, :], in0=sg[:, :], scalar1=1.0, scalar2=None, op0=ALU.add),
             reads=["sg"], writes=["sg"])
        p.op("dve", lambda e: e.reciprocal(out=sg[:, :], in_=sg[:, :]), reads=["sg"], writes=["sg"])
        p.op("pool", lambda e: e.tensor_tensor(out=sg[:, :], in0=sg[:, :], in1=G, op=ALU.mult),
             reads=["sg", f"g_in{ii}"], writes=["sg"])
        Y, yk = yb[b % 2], f"yb{b % 2}"
        p.op("dve", lambda e: e.tensor_tensor(out=fl(Y[:, :, :]), in0=of[:, :], in1=sg[:, :], op=ALU.mult),
             reads=[ofk, "sg"], writes=[yk])
        p.dma("pool", oview[:, :, c0:c0 + 128], Y[:, :, :], reads=[yk], skey=yk)

    load(0)
    if NBLK > 1:
        load(1)
    prep(0)
    for b in range(NBLK):
        if b + 2 < NBLK:
            load(b + 2)
        if b + 1 < NBLK:
            prep(b + 1)
        core(b)
        if b >= 1:
            tail(b - 1)
    tail(NBLK - 1)
    if own:
        p.finish()
        return p.emit()


def l_consts():
    t = np.arange(128)
    same = (t[:, None] // 64) == (t[None, :] // 64)
    tri = np.where(same & (t[:, None] <= t[None, :]), -1.0 / 16.0, 0.0)
    cst = np.zeros((128, 3, 256), np.float32)
    cst[:, 0, 0:128] = tri
    s_ = np.arange(64)
    m64 = (s_[:, None] <= s_[None, :]).astype(np.float32)
    cst[0:64, 1, :] = np.tile(m64, (1, 4))
    cst[0:64, 2, :] = np.tile(np.eye(64, dtype=np.float32), (1, 4))
    return cst


def ret_tables(r):
    pos = np.arange(PPAD)
    inv = (10000.0 ** (-np.arange(32, dtype=np.float32) / 32)).astype(np.float32)
    ang = (pos.astype(np.float32)[:, None] * inv[None, :]).astype(np.float32).astype(np.float64)
    cos = np.cos(ang).T
    sin = np.sin(ang).T
    Cos = np.concatenate([cos, cos], 0)
    SinS = np.concatenate([-sin, sin], 0)
    c = (pos % 64).astype(np.float64)
    tabs = np.zeros((4, 128, PPAD), np.float64)
    decc = np.zeros((64, 2), np.float64)
    for hl in range(2):
        h = 2 * r + hl
        gam = 1.0 - 2.0 ** (-5.0 - h)
        xi = gam ** (c + 1.0)
        kf = gam ** (-(c + 1.0)) / 8.0
        sl = slice(hl * 64, (hl + 1) * 64)
        tabs[0, sl] = Cos * xi
        tabs[1, sl] = SinS * xi
        tabs[2, sl] = Cos * kf
        tabs[3, sl] = SinS * kf
        decc[:, hl] = gam ** 64
    return tabs.astype(np.float32), decc.astype(np.float32)


NQB = 33
NQ = NQB * 128


def a_groups():
    gs = []
    jj = 0
    while jj < NQB:
        n = min(4, NQB - jj)
        gs.append((jj, n))
        jj += n
    return gs


def build_A(p=None, io=None):
    own = p is None
    NH = 8
    if own:
        p = Prog()
        A_ = {}
        A_["aqT"] = p.dram("aqT", [NH, 64, NQ], BF16, kind="ExternalInput").ap().rearrange("h d (j t) -> h d j t", t=128)
        A_["agT"] = p.dram("agT", [NH, 64, NQ], F32, kind="ExternalInput").ap().rearrange("h d (j t) -> h d j t", t=128)
        A_["iqT"] = p.dram("iqT", [NH, 64, NQ], BF16, kind="ExternalInput").ap().rearrange("h d (j t) -> h d j t", t=128)
        A_["iw"] = p.dram("iw", [128, NQB, 8], F32, kind="ExternalInput").ap()
        A_["akT"] = p.dram("akT", [NH, 64, PPAD], BF16, kind="ExternalInput").ap()
        A_["avh"] = p.dram("avh", [NH, 128, NBLK, 64], BF16, kind="ExternalInput").ap()
        A_["ikT"] = p.dram("ikT", [64, PPAD], BF16, kind="ExternalInput").ap()
        A_["mskd"] = p.dram("mskd", [128, 3, 128], F32, kind="ExternalInput").ap()
        A_["btd"] = p.dram("btd", [128, 3, NH, 128], F32, kind="ExternalInput").ap()
        A_["b15d"] = p.dram("b15d", [128, NH], F32, kind="ExternalInput").ap()
        A_["cstd"] = p.dram("cstd", [128, 160], F32, kind="ExternalInput").ap()
        A_["oT"] = p.dram("oT", [NH, 64, NQ], BF16, kind="ExternalOutput").ap().rearrange("h d (j t) -> h d j t", t=128)
    else:
        A_ = io

    NIT = 26
    MAXKB = 64
    ik_sb = p.sb("ik_sb", [64, PPAD], BF16)
    k_sb = p.sb("k_sb", [64, PPAD], BF16)
    v_sb = p.sb("v_sb", [128, NBLK, 64], BF16)
    mT = p.sb("mT", [128, MAXKB * 512], BF16)
    score = p.sb("score", [128, PPAD], F32)
    junk = p.sb("junk", [128, PPAD], BF16)
    iq_sb = p.sb("iq_sb", [64, NH, 512], BF16)
    aq_sb = p.sb("aq_sb", [64, NH, 512], BF16)
    iw_sb = p.sb("iw_sb", [128, NQB, 8], F32)
    msk = p.sb("msk", [128, 3, 128], F32)
    bt = p.sb("bt", [128, 3, NH, 128], F32)
    b15 = p.sb("b15", [128, NH], F32)
    cst = p.sb("cst", [128, 160], F32)
    idb = p.sb("idb", [128, 128], BF16)
    onesb = p.sb("onesb", [128, 64], BF16)
    R = [p.sb(f"R{i}", [128, 512], F32) for i in range(2)]
    mk = [p.sb(f"mk{i}", [128, 512], BF16) for i in range(2)]
    PT = [p.sb(f"PT{i}", [128, 512], BF16) for i in range(3)]
    st = p.sb("st", [128, 8], F32)
    W = p.sb("W", [128, NIT], F32)
    g_sb = p.sb("g_sb", [64, 512], F32)
    sg = p.sb("sg", [64, 512], F32)
    rd = p.sb("rd", [64, 512], F32)
    ob = [p.sb(f"ob{i}", [64, 512], BF16) for i in range(2)]

    psI = [p.ps(f"psI{i}", [128, 512]) for i in range(2)]
    psT = p.ps("psT", [128, 512])
    psS = [p.ps(f"psS{i}", [128, 512]) for i in range(3)]
    psO = p.ps("psO", [128, 512])
    psD = p.ps("psD", [128, 512])

    p.dma("sp", ik_sb[:, :], A_["ikT"], writes=["ik"])
    p.dma("sp", iw_sb[:, :, :], A_["iw"], writes=["iw"])
    p.dma("sp", msk[:, :, :], A_["mskd"], writes=["msk"])
    p.dma("sp", bt[:, :, :, :], A_["btd"], writes=["bt"])
    p.dma("sp", b15[:, :], A_["b15d"], writes=["b15"])
    p.dma("sp", cst[:, :], A_["cstd"], writes=["cst"])
    p.op("dve", lambda e: e.tensor_copy(out=idb[:, :], in_=cst[:, 0:128]), reads=["cst"], writes=["idb"])
    p.op("pool", lambda e: e.memset(onesb[:, :], 1.0), writes=["onesb"])
    p.op("pool", lambda e: e.memset(st[:, 6:7], float(TOPK) - 0.5), writes=["k255"])
    for h in range(NH):
        for w_ in range(3):
            p.op("dve", lambda e, h=h, w_=w_: e.tensor_scalar(out=bt[:, w_, h, :], in0=bt[:, w_, h, :], scalar1=b15[:, h:h + 1],
                                                              scalar2=None, op0=ALU.subtract), reads=["bt", "b15"], writes=["bt"])
    p.op("dve", lambda e: e.tensor_scalar(out=bt[:, :, :, :].rearrange("p a h t -> p (a h t)"),
                                          in0=bt[:, :, :, :].rearrange("p a h t -> p (a h t)"), scalar1=8.0, scalar2=None,
                                          op0=ALU.mult), reads=["bt"], writes=["bt"])

    evi = [0]
    for gi, (jj0, nq) in enumerate(a_groups()):
        N = nq * 128
        g8 = 8 * gi
        nkb = min(8 * gi + 8, NBLK) if nq == 4 else NBLK
        q0 = jj0 * 128
        for hh in range(NH):
            p.dma("sp", iq_sb[:, hh, 0:N].rearrange("d (j t) -> d j t", t=128), A_["iqT"][hh, :, jj0:jj0 + nq, :], writes=["iq"])
            p.dma("sp", aq_sb[:, hh, 0:N].rearrange("d (j t) -> d j t", t=128), A_["aqT"][hh, :, jj0:jj0 + nq, :], writes=["aq"])
        p.op("pool", lambda e, nkb=nkb, N=N: e.memset(mT[:, 0:nkb * N], 0.0), writes=["mT"])
        mTv = mT[:, 0:nkb * N].rearrange("p (k t) -> p k t", t=N)
        for j in range(nq):
            jj = jj0 + j
            nk = g8 + 2 * j + 2
            nkeys = nk * 128
            ntl = (nkeys + 511) // 512
            for kt_ in range(ntl):
                c0 = kt_ * 512
                w_ = min(512, nkeys - c0)
                for h in range(NH):
                    ii = evi[0] % 2
                    evi[0] += 1
                    p.op("pe", lambda e, ii=ii, h=h, j=j, c0=c0, w_=w_: e.matmul(
                        psI[ii][:, 0:w_], lhsT=iq_sb[:, h, j * 128:(j + 1) * 128], rhs=ik_sb[:, c0:c0 + w_],
                        start=True, stop=True), reads=["iq", "ik"], writes=[f"psI{ii}"])
                    p.op("act", lambda e, ii=ii, w_=w_: e.activation(out=R[ii][:, 0:w_], in_=psI[ii][:, 0:w_], func=AF.Relu),
                         reads=[f"psI{ii}"], writes=[f"R{ii}"])
                    if h == 0:
                        p.op("dve", lambda e, ii=ii, c0=c0, w_=w_, jj=jj: e.tensor_scalar(
                            out=score[:, c0:c0 + w_], in0=R[ii][:, 0:w_], scalar1=iw_sb[:, jj, 0:1], scalar2=None,
                            op0=ALU.mult), reads=[f"R{ii}", "iw"], writes=["score"])
                    else:
                        p.op("dve", lambda e, ii=ii, c0=c0, w_=w_, jj=jj, h=h: e.scalar_tensor_tensor(
                            out=score[:, c0:c0 + w_], in0=R[ii][:, 0:w_], scalar=iw_sb[:, jj, h:h + 1],
                            in1=score[:, c0:c0 + w_], op0=ALU.mult, op1=ALU.add),
                            reads=[f"R{ii}", "iw", "score"], writes=["score"])
            p.op("dve", lambda e, nkeys=nkeys: e.reduce_max(out=st[:, 0:1], in_=score[:, 0:nkeys], axis=AX.X,
                                                            apply_absolute_value=True), reads=["score"], writes=["stB"])
            p.op("dve", lambda e, nk=nk: e.tensor_tensor(out=score[:, (nk - 2) * 128:(nk - 1) * 128],
                                                         in0=score[:, (nk - 2) * 128:(nk - 1) * 128], in1=msk[:, 0, :], op=ALU.add),
                 reads=["score", "msk"], writes=["score"])
            p.op("dve", lambda e, nk=nk: e.tensor_tensor(out=score[:, (nk - 1) * 128:nk * 128],
                                                         in0=score[:, (nk - 1) * 128:nk * 128], in1=msk[:, 1, :], op=ALU.add),
                 reads=["score", "msk"], writes=["score"])
            p.op("dve", lambda e: e.tensor_tensor(out=score[:, 0:128], in0=score[:, 0:128], in1=msk[:, 2, :], op=ALU.add),
                 reads=["score", "msk"], writes=["score"])
            p.op("dve", lambda e: e.tensor_scalar(out=st[:, 1:2], in0=st[:, 0:1], scalar1=2.002, scalar2=1e-6,
                                                  op0=ALU.mult, op1=ALU.add), reads=["stB"], writes=["stw0"])
            p.op("dve", lambda e: e.tensor_scalar(out=st[:, 2:3], in0=st[:, 1:2], scalar1=-0.5, scalar2=None, op0=ALU.mult),
                 reads=["stw0"], writes=["stlo"])
            p.op("dve", lambda e: e.tensor_scalar(out=W[:, :], in0=cst[:, 128:128 + NIT], scalar1=st[:, 1:2], scalar2=None,
                                                  op0=ALU.mult), reads=["cst", "stw0"], writes=["W"])
            for it in range(NIT):
                p.op("dve", lambda e, it=it: e.tensor_tensor(out=st[:, 3:4], in0=st[:, 2:3], in1=W[:, it:it + 1], op=ALU.add),
                     reads=["stlo", "W"], writes=["stmid"])
                p.op("dve", lambda e, nkeys=nkeys: e.tensor_scalar(out=junk[:, 0:nkeys], in0=score[:, 0:nkeys],
                                                                   scalar1=st[:, 3:4], scalar2=0.0, op0=ALU.is_ge, op1=ALU.add,
                                                                   accum_out=st[:, 4:5]),
                     reads=["score", "stmid"], writes=["junk", "stcnt"])
                p.op("dve", lambda e, it=it: e.tensor_scalar(out=st[:, 5:6], in0=st[:, 4:5], scalar1=st[:, 6:7],
                                                             scalar2=W[:, it:it + 1], op0=ALU.is_gt, op1=ALU.mult),
                     reads=["stcnt", "k255", "W"], writes=["stt"])
                p.op("dve", lambda e: e.tensor_tensor(out=st[:, 2:3], in0=st[:, 2:3], in1=st[:, 5:6], op=ALU.add),
                     reads=["stlo", "stt"], writes=["stlo"])
            p.op("dve", lambda e: e.tensor_scalar(out=st[:, 7:8], in0=st[:, 2:3], scalar1=-1.0e29, scalar2=None, op0=ALU.max),
                 reads=["stlo"], writes=["stthr"])
            for kt_ in range(ntl):
                c0 = kt_ * 512
                w_ = min(512, nkeys - c0)
                nb_ = w_ // 128
                mi = kt_ % 2
                p.op("dve", lambda e, mi=mi, c0=c0, w_=w_: e.tensor_scalar(out=mk[mi][:, 0:w_], in0=score[:, c0:c0 + w_],
                                                                           scalar1=st[:, 7:8], scalar2=None, op0=ALU.is_ge),
                     reads=["score", "stthr"], writes=[f"mk{mi}"])
                for bb in range(nb_):
                    p.op("pe", lambda e, mi=mi, bb=bb: e.matmul(psT[:, bb * 128:(bb + 1) * 128], lhsT=mk[mi][:, bb * 128:(bb + 1) * 128],
                                                                rhs=idb[:, :], start=True, stop=True),
                         reads=[f"mk{mi}", "idb"], writes=["psT"])
                p.op("act", lambda e, kt_=kt_, nb_=nb_, j=j, w_=w_, mTv=mTv: e.activation(
                    out=mTv[:, kt_ * 4:kt_ * 4 + nb_, j * 128:(j + 1) * 128],
                    in_=psT[:, 0:w_].rearrange("p (k t) -> p k t", t=128), func=AF.Copy),
                    reads=["psT"], writes=["mT"])
        for h in range(NH):
            p.dma("sp", k_sb[:, 0:nkb * 128], A_["akT"][h, :, 0:nkb * 128], writes=["k_sb"])
            p.dma("sp", v_sb[:, 0:nkb, :], A_["avh"][h, :, 0:nkb, :], writes=["v_sb"])
            def s1(kb, h=h, N=N, nkb=nkb, mTv=mTv, nq=nq, g8=g8):
                si = kb % 3
                p.op("pe", lambda e: e.matmul(
                    psS[si][:, 0:N], lhsT=k_sb[:, kb * 128:(kb + 1) * 128], rhs=aq_sb[:, h, 0:N], start=True, stop=True),
                    reads=["k_sb", "aq"], writes=[f"psS{si}"])
                for j in range(nq):
                    which = kb - (g8 + 2 * j - 1)
                    if 0 <= which <= 2:
                        p.op("dve", lambda e, j=j, which=which: e.tensor_tensor(
                            out=psS[si][:, j * 128:(j + 1) * 128], in0=psS[si][:, j * 128:(j + 1) * 128],
                            in1=bt[:, which, h, :], op=ALU.add), reads=[f"psS{si}", "bt"], writes=[f"psS{si}"])
                p.op("act", lambda e: e.activation(out=PT[si][:, 0:N], in_=psS[si][:, 0:N], func=AF.Exp, scale=0.125),
                     reads=[f"psS{si}"], writes=[f"PT{si}"])
                p.op("dve", lambda e: e.tensor_tensor(out=PT[si][:, 0:N], in0=PT[si][:, 0:N], in1=mTv[:, kb, :], op=ALU.mult),
                     reads=[f"PT{si}", "mT"], writes=[f"PT{si}"])

            def s2(kb, h=h, N=N, nkb=nkb):
                si = kb % 3
                p.op("pe", lambda e: e.matmul(
                    psO[0:64, 0:N], lhsT=v_sb[:, kb, :], rhs=PT[si][:, 0:N], start=(kb == 0), stop=(kb == nkb - 1)),
                    reads=["v_sb", f"PT{si}"], writes=["psO"])
                p.op("pe", lambda e: e.matmul(
                    psD[0:64, 0:N], lhsT=onesb[:, :], rhs=PT[si][:, 0:N], start=(kb == 0), stop=(kb == nkb - 1)),
                    reads=["onesb", f"PT{si}"], writes=["psD"])

            for kb in range(nkb + 2):
                if kb < nkb:
                    s1(kb)
                if kb - 2 >= 0:
                    s2(kb - 2)
            p.dma("sp", g_sb[:, 0:N].rearrange("d (j t) -> d j t", t=128), A_["agT"][h, :, jj0:jj0 + nq, :], writes=["g_sb"])
            p.op("dve", lambda e, N=N: e.tensor_scalar(out=rd[:, 0:N], in0=psD[0:64, 0:N], scalar1=1e-30, scalar2=None,
                                                       op0=ALU.max), reads=["psD"], writes=["rd"])
            p.op("dve", lambda e, N=N: e.reciprocal(out=rd[:, 0:N], in_=rd[:, 0:N]), reads=["rd"], writes=["rd"])
            p.op("act", lambda e, N=N: e.activation(out=sg[:, 0:N], in_=g_sb[:, 0:N], func=AF.Exp, scale=-1.0),
                 reads=["g_sb"], writes=["sg"])
            p.op("dve", lambda e, N=N: e.tensor_scalar(out=sg[:, 0:N], in0=sg[:, 0:N], scalar1=1.0, scalar2=None, op0=ALU.add),
                 reads=["sg"], writes=["sg"])
            p.op("dve", lambda e, N=N: e.reciprocal(out=sg[:, 0:N], in_=sg[:, 0:N]), reads=["sg"], writes=["sg"])
            p.op("pool", lambda e, N=N: e.tensor_tensor(out=sg[:, 0:N], in0=sg[:, 0:N], in1=g_sb[:, 0:N], op=ALU.mult),
                 reads=["sg", "g_sb"], writes=["sg"])
            p.op("pool", lambda e, N=N: e.tensor_tensor(out=sg[:, 0:N], in0=sg[:, 0:N], in1=rd[:, 0:N], op=ALU.mult),
                 reads=["sg", "rd"], writes=["sg"])
            O, ok_ = ob[h % 2], f"ob{h % 2}"
            p.op("dve", lambda e, N=N, O=O: e.tensor_tensor(out=O[:, 0:N], in0=psO[0:64, 0:N], in1=sg[:, 0:N], op=ALU.mult),
                 reads=["psO", "sg"], writes=[ok_])
            p.dma("pool", A_["oT"][h, :, jj0:jj0 + nq, :], O[:, 0:N].rearrange("d (j t) -> d j t", t=128), reads=[ok_], skey=ok_)
    if own:
        p.finish()
        return p.emit()


def rel_bucket_np(rel):
    n = -rel
    ret = np.where(n < 0, 16, 0)
    n = np.abs(n)
    nf = np.maximum(n, 1).astype(np.float32)
    large = 8 + (np.log(nf / np.float32(8)) / np.float32(np.log(16.0)) * np.float32(8)).astype(np.int32)
    large = np.minimum(large, 15)
    return ret + np.where(n < 8, n, large)


def a_consts(r, rel_bias):
    s = np.arange(128)
    t = np.arange(128)
    diag = np.where((s[None, :] < 64) | (t[:, None] >= 64), 0.0, NEG)
    if r == 0:
        mA, mB = diag, np.full((128, 128), NEG)
    else:
        mA, mB = np.zeros((128, 128)), diag
    padm = np.where(s[None, :] >= PADF, 0.0, NEG) * np.ones((128, 1))
    mskd = np.stack([mA, mB, padm], axis=1).astype(np.float32)
    btd = np.zeros((128, 3, 8, 128), np.float32)
    for which in range(3):
        dblk = (r + 1 - which)
        rel = (s[:, None] - (t[None, :] + dblk * 128)).astype(np.int32)
        bk = rel_bucket_np(rel)
        btd[:, which, :, :] = np.transpose(rel_bias[bk], (0, 2, 1))
    b15d = np.broadcast_to(rel_bias[15][None, :], (128, 8)).astype(np.float32).copy()
    cstd = np.zeros((128, 160), np.float32)
    cstd[:, 0:128] = np.eye(128)
    cstd[:, 128:128 + 26] = (2.0 ** -(np.arange(26) + 1.0))[None, :]
    return mskd, btd, b15d, cstd


def a_pack(aqT, agT, iqT, iw, akT, av, ikT, r, rel_bias):
    blks = np.arange(NQB) * 2 + r
    cols = (blks[:, None] * 128 + np.arange(128)[None, :]).reshape(-1)
    mskd, btd, b15d, cstd = a_consts(r, rel_bias)
    return {
        "aqT": np.ascontiguousarray(aqT[:, :, cols]),
        "agT": np.ascontiguousarray(agT[:, :, cols]),
        "iqT": np.ascontiguousarray(iqT[:, :, cols]),
        "iw": np.ascontiguousarray(iw[cols].reshape(NQB, 128, 8).transpose(1, 0, 2)),
        "akT": np.ascontiguousarray(akT),
        "avh": np.ascontiguousarray(av.reshape(NBLK, 128, 8, 64).transpose(2, 1, 0, 3)),
        "ikT": np.ascontiguousarray(ikT),
        "mskd": mskd, "btd": btd, "b15d": b15d, "cstd": cstd,
    }


T_CORE = PPAD // 2

AB_FB = ([(0 + i * 128, 128) for i in range(4)] + [(512 + i * 128, 128) for i in range(4)]
         + [(2048 + i * 128, 128) for i in range(4)] + [(2560, 64)]
         + [(2632, 128), (2760, 128), (2888, 128), (3016, 128)])
AB_FF = [(1536 + i * 128, 128) for i in range(4)] + [(3656 + i * 128, 128) for i in range(4)] + [(4168, 16)]
AB_TB = [(1024, 512), (3144, 512)]
AB_TF = [(2624, 8)]
W_CDX = W_CD + 512
CD_FB = ([(0, 128), (128, 128), (256, 128), (384, 128), (3584, 128), (3712, 128), (3840, 128), (3968, 128)]
         + [(1536 + i * 128, 128) for i in range(4)] + [(2048 + i * 128, 128) for i in range(4)])
CD_FF = [(1024 + i * 128, 128) for i in range(4)] + [(3072 + i * 128, 128) for i in range(4)]
CD_TB = [(512, 512), (2560, 512)]
CD_TF = []

_PROGS = {}


def _prog(name):
    if name not in _PROGS:
        if name == "P_AB0":
            _PROGS[name] = build_P(T_CORE, W_AB, AB_FB, AB_FF, AB_TB, AB_TF, with_out=False)
        elif name == "P_AB":
            _PROGS[name] = build_P(T_CORE, W_AB, AB_FB, AB_FF, AB_TB, AB_TF, with_out=True)
        elif name == "P_CD":
            _PROGS[name] = build_P(T_CORE, W_CDX, CD_FB, CD_FF, CD_TB, CD_TF, with_out=True)
        elif name == "F":
            _PROGS[name] = build_P(T_CORE, 0, [], [], [], [], with_out=True, final=True)
        elif name == "A":
            _PROGS[name] = build_A()
        elif name == "D":
            _PROGS[name] = build_D()
        elif name == "LG":
            _PROGS[name] = build_L("gla")
        elif name == "LR":
            _PROGS[name] = build_L("ret")
    return _PROGS[name]


def _run(name, maps):
    res = run_bass_kernel_spmd(_prog(name), maps, core_ids=list(range(NCORES)))
    return res.results


def _g_layout(g):
    return np.ascontiguousarray(np.asarray(g, np.float32).reshape(8, 128).T)


def _seq(results, key, axis):
    return [np.concatenate([results[2 * b][key], results[2 * b + 1][key]], axis=axis) for b in range(BATCH)]


def _cd_weights(w):
    w = np.asarray(w, np.float32)
    d = np.arange(64)
    sw = (d + 32) % 64
    cq_sw = np.concatenate([h * 64 + sw for h in range(4)])
    ck_sw = 256 + cq_sw
    return np.ascontiguousarray(np.concatenate([w, w[:, cq_sw], w[:, ck_sw]], axis=1))


def kernel(x, meta_tokens, rel_bias, norm_g, final_g, w_in_ab, gla_gate_w2, gla_gate_b,
           w_out_ab, w_in_cd, w_out_cd):
    x = np.asarray(x, np.float32)
    rel_bias = np.asarray(rel_bias, np.float32)
    h = np.zeros((BATCH, PPAD, D_MODEL), np.float32)
    h[:, PADF:128] = np.asarray(meta_tokens, np.float32)[None]
    h[:, 128:128 + SEQ] = x
    hT = [np.ascontiguousarray(h[b].T) for b in range(BATCH)]
    del h
    mixT = None
    T = T_CORE
    dmasks, dut = d_consts()
    lcst = l_consts()
    for layer in range(4):
        j = layer // 2
        ab = (layer % 2 == 0)
        maps = []
        for c in range(NCORES):
            b, r = divmod(c, 2)
            m = {"hT": np.ascontiguousarray(hT[b][:, r * T:(r + 1) * T]), "g": _g_layout(norm_g[layer])}
            if ab:
                m["w"] = np.ascontiguousarray(np.asarray(w_in_ab[j], np.float32))
            else:
                m["w"] = _cd_weights(w_in_cd[j])
            if layer > 0:
                m["mixT"] = np.ascontiguousarray(mixT[b][:, r * T:(r + 1) * T])
                m["wo"] = np.ascontiguousarray(np.asarray(w_out_cd[j - 1] if ab else w_out_ab[j], np.float32))
            maps.append(m)
        pname = "P_AB0" if layer == 0 else ("P_AB" if ab else "P_CD")
        res = _run(pname, maps)
        del maps
        if layer > 0:
            hT = _seq(res, "hT_new", 1)
        of_bf = _seq(res, "of_bf", 1)
        of_f32 = _seq(res, "of_f32", 1)
        ot_bf = _seq(res, "ot_bf", 0)
        ot_f32 = _seq(res, "ot_f32", 0)
        del res
        import ml_dtypes
        mixT = [np.zeros((D_MODEL, PPAD), ml_dtypes.bfloat16) for _ in range(BATCH)]
        if ab:
            maps = []
            for c in range(NCORES):
                b, r = divmod(c, 2)
                fb, ff = of_bf[b], of_f32[b]
                maps.append(a_pack(fb[0:512].reshape(8, 64, PPAD), ff[0:512].reshape(8, 64, PPAD),
                                   fb[1024:1536].reshape(8, 64, PPAD), ot_f32[b], fb[512:1024].reshape(8, 64, PPAD),
                                   ot_bf[b][:, 0:512], fb[1536:1600], r, rel_bias))
            res = _run("A", maps)
            del maps
            for c in range(NCORES):
                b, r = divmod(c, 2)
                o = res[c]["oT"].reshape(512, NQB, 128)
                blks = np.arange(NQB) * 2 + r
                keep = blks < NBLK - (1 if r == 1 else 0) if False else blks < NBLK
                mv = mixT[b][0:512].reshape(512, NBLK, 128)
                mv[:, blks[keep], :] = o[:, keep, :]
            del res
            maps = []
            for c in range(NCORES):
                b, r = divmod(c, 2)
                fb, ff = of_bf[b], of_f32[b]
                w2a = np.concatenate([np.asarray(gla_gate_w2[j], np.float32)[:, r * 128:(r + 1) * 128],
                                      np.asarray(gla_gate_b[j], np.float32)[None, r * 128:(r + 1) * 128]], axis=0)
                maps.append({"qT": np.ascontiguousarray(fb[(13 + r) * 128:(14 + r) * 128]),
                             "kT": np.ascontiguousarray(fb[(15 + r) * 128:(16 + r) * 128]),
                             "v": np.ascontiguousarray(ot_bf[b][:, 512 + r * 256:512 + (r + 1) * 256]),
                             "gT": np.ascontiguousarray(ff[512 + r * 256:512 + (r + 1) * 256]),
                             "baT": np.ascontiguousarray(ff[1024:1040]), "w2a": np.ascontiguousarray(w2a), "cst": lcst})
            res = _run("LG", maps)
            del maps
            for c in range(NCORES):
                b, r = divmod(c, 2)
                mixT[b][512 + r * 256:512 + (r + 1) * 256] = res[c]["oT"]
            del res
        else:
            maps = []
            for c in range(NCORES):
                b, r = divmod(c, 2)
                fb, ff = of_bf[b], of_f32[b]
                tabs, decc = ret_tables(r)
                maps.append({"qT": np.ascontiguousarray(fb[(0 + r) * 128:(1 + r) * 128]),
                             "kT": np.ascontiguousarray(fb[(2 + r) * 128:(3 + r) * 128]),
                             "qsT": np.ascontiguousarray(fb[(4 + r) * 128:(5 + r) * 128]),
                             "ksT": np.ascontiguousarray(fb[(6 + r) * 128:(7 + r) * 128]),
                             "v": np.ascontiguousarray(ot_bf[b][:, r * 256:(r + 1) * 256]),
                             "gT": np.ascontiguousarray(ff[r * 256:(r + 1) * 256]),
                             "tabs": tabs, "decc": decc, "cst": lcst})
            res = _run("LR", maps)
            del maps
            for c in range(NCORES):
                b, r = divmod(c, 2)
                mixT[b][r * 256:(r + 1) * 256] = res[c]["oT"]
            del res
            maps = []
            for c in range(NCORES):
                b, r = divmod(c, 2)
                fb, ff = of_bf[b], of_f32[b]
                maps.append({"qT": np.ascontiguousarray(fb[8 * 128 + r * 256:8 * 128 + (r + 1) * 256].reshape(4, 64, PPAD)),
                             "kT": np.ascontiguousarray(fb[12 * 128 + r * 256:12 * 128 + (r + 1) * 256].reshape(4, 64, PPAD)),
                             "v": np.ascontiguousarray(ot_bf[b][:, 512 + r * 256:512 + (r + 1) * 256]),
                             "gT": np.ascontiguousarray(ff[512 + r * 256:512 + (r + 1) * 256].reshape(4, 64, PPAD)),
                             "masks": dmasks, "ut": dut})
            res = _run("D", maps)
            del maps
            for c in range(NCORES):
                b, r = divmod(c, 2)
                mixT[b][512 + r * 256:512 + (r + 1) * 256] = res[c]["oT"].reshape(256, PPAD)
            del res
        del of_bf, of_f32, ot_bf, ot_f32
    maps = []
    for c in range(NCORES):
        b, r = divmod(c, 2)
        maps.append({"hT": np.ascontiguousarray(hT[b][:, r * T:(r + 1) * T]), "g": _g_layout(final_g),
                     "mixT": np.ascontiguousarray(mixT[b][:, r * T:(r + 1) * T]),
                     "wo": np.ascontiguousarray(np.asarray(w_out_cd[1], np.float32))})
    res = _run("F", maps)
    yT = _seq(res, "y", 1)
    out = np.stack([np.ascontiguousarray(yT[b][:, 128:128 + SEQ].T) for b in range(BATCH)], axis=0)
    return out.astype(np.float32)


def build_fused():
    p = Prog()
    p.fused = True
    EI = "ExternalInput"
    hT0 = p.dram("hT0", [D_MODEL, PPAD], F32, kind=EI)
    gs = p.dram("gs", [5, 128, 8], F32, kind=EI)
    w_ab = p.dram("w_ab", [2, D_MODEL, W_AB], F32, kind=EI)
    w_cdx = p.dram("w_cdx", [2, D_MODEL, W_CDX], F32, kind=EI)
    wo_ab = p.dram("wo_ab", [2, D_MODEL, D_MODEL], F32, kind=EI)
    wo_cd = p.dram("wo_cd", [2, D_MODEL, D_MODEL], F32, kind=EI)
    w2 = p.dram("w2", [2, 16, 256], F32, kind=EI)
    gb = p.dram("gb", [2, 1, 256], F32, kind=EI)
    mskd = p.dram("mskd", [2, 128, 3, 128], F32, kind=EI)
    btd = p.dram("btd", [2, 128, 3, 8, 128], F32, kind=EI)
    b15d = p.dram("b15d", [128, 8], F32, kind=EI)
    cstd = p.dram("cstd", [128, 160], F32, kind=EI)
    lcst = p.dram("lcst", [128, 3, 256], F32, kind=EI)
    tabs = p.dram("tabs", [2, 4, 128, PPAD], F32, kind=EI)
    decc = p.dram("decc", [2, 64, 2], F32, kind=EI)
    dmasks = p.dram("dmasks", [6, 128, 512], BF16, kind=EI)
    dut = p.dram("dut", [128, 256], BF16, kind=EI)
    y = p.dram("y", [D_MODEL, PPAD], F32, kind="ExternalOutput")
    of_bf = p.dram("s_of_bf", [17 * 128, PPAD], BF16)
    of_f32 = p.dram("s_of_f32", [9 * 128, PPAD], F32)
    ot_bf = p.dram("s_ot_bf", [PPAD, 1024], BF16)
    ot_f32 = p.dram("s_ot_f32", [PPAD, 8], F32)
    avh = p.dram("s_avh", [8, 128, NBLK, 64], BF16)
    mixT = p.dram("s_mixT", [D_MODEL, PPAD], BF16)
    hA = p.dram("s_hA", [D_MODEL, PPAD], F32)
    hB = p.dram("s_hB", [D_MODEL, PPAD], F32)

    fb = of_bf.ap()
    ff = of_f32.ap()
    hd3 = lambda ap_: ap_.rearrange("(h d) t -> h d t", d=64)

    def qsel(ap_, r):
        return ap_.rearrange("(h d) (j r t) -> h d j r t", d=64, r=2, t=128)[:, :, :, r, :]

    h_cur = hT0.ap()
    h_bufs = [hA.ap(), hB.ap()]
    for layer in range(4):
        j = layer // 2
        ab = (layer % 2 == 0)
        p.phase(f"P{layer}")
        io = {"hT": h_cur, "g": gs.ap()[layer], "of_bf": fb, "of_f32": ff, "ot_bf": ot_bf.ap(), "ot_f32": ot_f32.ap()}
        io["w"] = w_ab.ap()[j] if ab else w_cdx.ap()[j]
        if layer > 0:
            io["mixT"] = mixT.ap()
            io["wo"] = wo_cd.ap()[j - 1] if ab else wo_ab.ap()[j]
            io["hT_new"] = h_bufs[(layer - 1) % 2]
        if ab:
            tdst = {1024: (lambda blk: avh.ap()[:, :, blk, :].rearrange("h s d -> s h d"))}
            build_P(PPAD, W_AB, AB_FB, AB_FF, AB_TB, AB_TF, with_out=(layer > 0), p=p, io=io, tdst=tdst)
        else:
            build_P(PPAD, W_CDX, CD_FB, CD_FF, CD_TB, CD_TF, with_out=True, p=p, io=io)
        if layer > 0:
            h_cur = h_bufs[(layer - 1) % 2]
        if ab:
            for r in range(2):
                p.phase(f"A{layer}{r}")
                build_A(p=p, io={
                    "aqT": qsel(fb[0:512], r), "agT": qsel(ff[0:512], r), "iqT": qsel(fb[1024:1536], r),
                    "iw": ot_f32.ap().rearrange("(j r t) c -> t j r c", r=2, t=128)[:, :, r, :],
                    "akT": hd3(fb[512:1024]), "avh": avh.ap(), "ikT": fb[1536:1600],
                    "mskd": mskd.ap()[r], "btd": btd.ap()[r], "b15d": b15d.ap(), "cstd": cstd.ap(),
                    "oT": qsel(mixT.ap()[0:512], r)})
            for r in range(2):
                p.phase(f"G{layer}{r}")
                build_L("gla", p=p, io={
                    "qT": fb[(13 + r) * 128:(14 + r) * 128], "kT": fb[(15 + r) * 128:(16 + r) * 128],
                    "v": ot_bf.ap()[:, 512 + r * 256:512 + (r + 1) * 256],
                    "gT": ff[512 + r * 256:512 + (r + 1) * 256], "baT": ff[1024:1040],
                    "w2a": (w2.ap()[j][:, r * 128:(r + 1) * 128], gb.ap()[j][:, r * 128:(r + 1) * 128]),
                    "cst": lcst.ap(), "oT": mixT.ap()[512 + r * 256:512 + (r + 1) * 256]})
        else:
            for r in range(2):
                p.phase(f"R{layer}{r}")
                build_L("ret", p=p, io={
                    "qT": fb[(0 + r) * 128:(1 + r) * 128], "kT": fb[(2 + r) * 128:(3 + r) * 128],
                    "qsT": fb[(4 + r) * 128:(5 + r) * 128], "ksT": fb[(6 + r) * 128:(7 + r) * 128],
                    "v": ot_bf.ap()[:, r * 256:(r + 1) * 256], "gT": ff[r * 256:(r + 1) * 256],
                    "tabs": tabs.ap()[r], "decc": decc.ap()[r], "cst": lcst.ap(),
                    "oT": mixT.ap()[r * 256:(r + 1) * 256]})
            for r in range(2):
                p.phase(f"D{layer}{r}")
                build_D(p=p, io={
                    "qT": hd3(fb[8 * 128 + r * 256:8 * 128 + (r + 1) * 256]),
                    "kT": hd3(fb[12 * 128 + r * 256:12 * 128 + (r + 1) * 256]),
                    "v": ot_bf.ap()[:, 512 + r * 256:512 + (r + 1) * 256],
                    "gT": hd3(ff[512 + r * 256:512 + (r + 1) * 256]),
                    "masks": dmasks.ap(), "ut": dut.ap(),
                    "oT": hd3(mixT.ap()[512 + r * 256:512 + (r + 1) * 256])})
    p.phase("F")
    build_P(PPAD, 0, [], [], [], [], with_out=True, final=True, p=p,
            io={"hT": h_cur, "g": gs.ap()[4], "mixT": mixT.ap(), "wo": wo_cd.ap()[1], "hT_new": None, "y": y.ap()})
    p.finish()
    return p.emit()


def kernel_unfused(**kw):
    return _kernel_unfused(**kw)


_kernel_unfused = kernel


def kernel(x, meta_tokens, rel_bias, norm_g, final_g, w_in_ab, gla_gate_w2, gla_gate_b,
           w_out_ab, w_in_cd, w_out_cd):
    f32 = lambda a: np.ascontiguousarray(np.asarray(a, np.float32))
    x = f32(x)
    rel_bias = f32(rel_bias)
    if "FUSED" not in _PROGS:
        _PROGS["FUSED"] = build_fused()
    nc = _PROGS["FUSED"]
    gs = np.stack([_g_layout(norm_g[l]) for l in range(4)] + [_g_layout(final_g)], axis=0)
    ac = [a_consts(r, rel_bias) for r in range(2)]
    rt = [ret_tables(r) for r in range(2)]
    dmasks, dut = d_consts()
    shared = {
        "gs": gs, "w_ab": f32(w_in_ab), "w_cdx": np.stack([_cd_weights(w_in_cd[j]) for j in range(2)], 0),
        "wo_ab": f32(w_out_ab), "wo_cd": f32(w_out_cd), "w2": f32(gla_gate_w2),
        "gb": f32(gla_gate_b).reshape(2, 1, 256),
        "mskd": np.stack([ac[0][0], ac[1][0]], 0), "btd": np.stack([ac[0][1], ac[1][1]], 0),
        "b15d": ac[0][2], "cstd": ac[0][3], "lcst": l_consts(),
        "tabs": np.stack([rt[0][0], rt[1][0]], 0), "decc": np.stack([rt[0][1], rt[1][1]], 0),
        "dmasks": dmasks, "dut": dut,
    }
    maps = []
    for c in range(NCORES):
        b = c // 2
        h = np.zeros((PPAD, D_MODEL), np.float32)
        h[PADF:128] = np.asarray(meta_tokens, np.float32)
        h[128:128 + SEQ] = x[b]
        m = dict(shared)
        m["hT0"] = np.ascontiguousarray(h.T)
        maps.append(m)
    res = run_bass_kernel_spmd(nc, maps, core_ids=list(range(NCORES))).results
    out = np.stack([np.ascontiguousarray(res[2 * b]["y"][:, 128:128 + SEQ].T) for b in range(BATCH)], axis=0)
    return out.astype(np.float32)
```

```python
import contextlib
import numpy as np
import concourse.bass as bass
import concourse.mybir as mybir
from concourse.bass_utils import run_bass_kernel_spmd

F32 = mybir.dt.float32
BF16 = mybir.dt.bfloat16
ALU = mybir.AluOpType
AF = mybir.ActivationFunctionType
AX = mybir.AxisListType

D_MODEL = 1024
BATCH = 4
SEQ = 8192
PADF = 112
NMETA = 16
PTOK = SEQ + 128
NBLK = 66
PPAD = NBLK * 128
NCORES = 8
W_AB = 4184
W_CD = 3584
TOPK = 256
NEG = -1.0e30
LDBG = 9
EPS = 1e-6
SEM_ROLL = 30000

ENGS = ("pe", "act", "dve", "pool", "sp")


class Prog:
    def __init__(self, num_devices=None):
        if num_devices:
            self.nc = bass.Bass("TRN2", target_bir_lowering=False, num_devices=num_devices)
        else:
            self.nc = bass.Bass("TRN2", target_bir_lowering=False)
        self.q = {e: [] for e in ENGS}
        self.cnt = {e: 0 for e in ENGS}
        self.esem = {e: None for e in ENGS}
        self.keys = {}
        self.waited = {e: {} for e in ENGS}
        self.dsem = {}
        self.all_dma = []
        self.nsem = 0
        self.allsems = []
        self.fused = False
        self.prefix = ""
        self.sb_off = 16640
        self.banks = None
        self.bank_i = 0

    def new_sem(self, name):
        self.nsem += 1
        s = self.nc.alloc_semaphore(f"{name}_{self.nsem}")
        self.allsems.append(s)
        return s

    def dram(self, name, shape, dtype, kind="Internal"):
        return self.nc.dram_tensor(name, list(shape), dtype, kind=kind)

    def sb(self, name, shape, dtype):
        if not self.fused:
            return self.nc.alloc_sbuf_tensor(name, list(shape), dtype)
        nbytes = int(np.prod(shape[1:])) * (2 if dtype is BF16 else 4)
        nbytes = (nbytes + 31) // 32 * 32
        off = self.sb_off
        self.sb_off += nbytes
        assert self.sb_off <= 229120, (name, self.sb_off)
        self.sb_max = max(getattr(self, 'sb_max', 0), self.sb_off)
        return self.nc.alloc_sbuf_tensor_at(self.prefix + name, list(shape), dtype, offset=off)

    def ps(self, name, shape, dtype=F32):
        if not self.fused:
            return self.nc.alloc_psum_tensor(name, list(shape), dtype)
        if self.banks is None:
            self.banks = [self.nc.alloc_psum_tensor(f"bank{i}", [128, 512], F32) for i in range(8)]
        b = self.banks[self.bank_i]
        self.bank_i += 1
        assert self.bank_i <= 8
        return b

    def phase(self, name):
        toks = []
        for e in ENGS:
            if self.esem[e] is not None and self.cnt[e] > 0:
                toks.append((self.esem[e], self.cnt[e]))
        for ent in self.dsem.values():
            toks.append((ent[0], ent[1]))
        for e in ENGS:
            waits = []
            for sem, val in toks:
                if e == "pe" and sem is self.esem["pe"]:
                    continue
                if self.waited[e].get(id(sem), 0) >= val:
                    continue
                self.waited[e][id(sem)] = val
                waits.append((sem, val))
            if waits:
                self.q[e].append((waits, None, None, 0))
        self.prefix = name + "_"
        self.sb_off = 16640
        self.bank_i = 0

    def _deps(self, eng, reads, writes):
        deps = []
        for k in reads:
            st = self.keys.get(k)
            if st and st["w"]:
                deps.append(st["w"])
        for k in writes:
            st = self.keys.get(k)
            if st:
                if st["w"]:
                    deps.append(st["w"])
                deps.extend(st["r"])
        best = {}
        for sem, val in deps:
            if eng == "pe" and sem is self.esem["pe"]:
                continue
            sid = id(sem)
            if sid not in best or best[sid][1] < val:
                best[sid] = (sem, val)
        out = []
        for sid, (sem, val) in best.items():
            if self.waited[eng].get(sid, 0) >= val:
                continue
            self.waited[eng][sid] = val
            out.append((sem, val))
        return out

    def _mark(self, reads, writes, tok):
        for k in reads:
            st = self.keys.setdefault(k, {"w": None, "r": []})
            st["r"].append(tok)
        for k in writes:
            self.keys[k] = {"w": tok, "r": []}

    def op(self, eng, fn, reads=(), writes=()):
        waits = self._deps(eng, reads, writes)
        if self.esem[eng] is None or self.cnt[eng] >= SEM_ROLL:
            self.esem[eng] = self.new_sem("e" + eng)
            self.cnt[eng] = 0
        self.cnt[eng] += 1
        tok = (self.esem[eng], self.cnt[eng])
        self.q[eng].append((waits, fn, tok[0], 1))
        self._mark(reads, writes, tok)

    def dma(self, eng, out_ap, in_ap, reads=(), writes=(), skey=None):
        waits = self._deps(eng, reads, writes)
        if skey is None:
            skey = (list(writes) + list(reads))[0]
        ent = self.dsem.get(skey)
        if ent is None:
            ent = [self.new_sem("d"), 0]
            self.dsem[skey] = ent
        ent[1] += 16
        tok = (ent[0], ent[1])
        self.q[eng].append((waits, lambda e: e.dma_start(out=out_ap, in_=in_ap), tok[0], 16))
        self._mark(reads, writes, tok)

    def finish(self):
        fin = [(ent[0], ent[1]) for ent in self.dsem.values()]
        self.q["pool"].append((fin, None, None, 0))
        self.q["sp"].append((fin, None, None, 0))

    def emit(self):
        nc = self.nc

        def run(e, lst):
            for waits, fn, sem, inc in lst:
                for s, v in waits:
                    e.wait_ge(s, v)
                if fn is not None:
                    fn(e).then_inc(sem, inc)

        with nc.Block() as block:
            @block.tensor
            def _(e):
                run(e, self.q["pe"])

            @block.scalar
            def _(e):
                run(e, self.q["act"])

            @block.vector
            def _(e):
                run(e, self.q["dve"])

            @block.gpsimd
            def _(e):
                run(e, self.q["pool"])

            @block.sync
            def _(e):
                run(e, self.q["sp"])
        return nc


def build_P(T, WT, fchunks_bf, fchunks_f32, tgroups_bf, tgroups_f32, with_out, final=False, p=None, io=None,
            tdst=None):
    own = p is None
    NT = 384
    assert T % NT == 0
    ntile = T // NT
    KC = 8
    tdst = tdst or {}
    if own:
        p = Prog()
        A = {}
        A["hT"] = p.dram("hT", [D_MODEL, T], F32, kind="ExternalInput").ap()
        A["g"] = p.dram("g", [128, KC], F32, kind="ExternalInput").ap()
        if not final:
            A["w"] = p.dram("w", [D_MODEL, WT], F32, kind="ExternalInput").ap()
        if with_out:
            A["mixT"] = p.dram("mixT", [D_MODEL, T], BF16, kind="ExternalInput").ap()
            A["wo"] = p.dram("wo", [D_MODEL, D_MODEL], F32, kind="ExternalInput").ap()
            A["hT_new"] = p.dram("hT_new", [D_MODEL, T], F32, kind="ExternalOutput").ap()
        if final:
            A["y"] = p.dram("y", [D_MODEL, T], F32, kind="ExternalOutput").ap()
        else:
            nfb, nff = len(fchunks_bf), len(fchunks_f32)
            ctb = sum(n for _, n in tgroups_bf)
            ctf = sum(n for _, n in tgroups_f32)
            A["of_bf"] = p.dram("of_bf", [max(nfb, 1) * 128, T], BF16, kind="ExternalOutput").ap()
            A["of_f32"] = p.dram("of_f32", [max(nff, 1) * 128, T], F32, kind="ExternalOutput").ap()
            A["ot_bf"] = p.dram("ot_bf", [T, max(ctb, 1)], BF16, kind="ExternalOutput").ap()
            A["ot_f32"] = p.dram("ot_f32", [T, max(ctf, 1)], F32, kind="ExternalOutput").ap()
    else:
        A = io
    of_bf, of_f32, ot_bf, ot_f32 = A.get("of_bf"), A.get("of_f32"), A.get("ot_bf"), A.get("ot_f32")

    g_sb = p.sb("g_sb", [128, KC], F32)
    ones = p.sb("ones", [128, 128], F32)
    if not final:
        w_sb = p.sb("w_sb", [128, KC, WT], BF16)
        stg = [p.sb(f"stg{i}", [128, KC, 512], F32) for i in range(2)]
    if with_out:
        wo_sb = p.sb("wo_sb", [128, KC, D_MODEL], BF16)
        mix_sb = [p.sb(f"mix{i}", [128, KC, NT], BF16) for i in range(2)]
        if final:
            stg = [p.sb(f"stg{i}", [128, KC, 512], F32) for i in range(2)]
    h_sb = [p.sb(f"h{i}", [128, KC, NT], F32) for i in range(2)]
    sq_sb = p.sb("sq", [128, NT], F32)
    rstd = p.sb("rstd", [128, NT], F32)
    hn_sb = [p.sb(f"hn{i}", [128, KC, NT], BF16) for i in range(2)]
    NEV = 4
    ev_bf = [p.sb(f"evb{i}", [128, 512], BF16) for i in range(NEV)]
    ev_f = [p.sb(f"evf{i}", [128, 512], F32) for i in range(NEV)]
    pst = [p.ps(f"ps{i}", [128, 512], F32) for i in range(6)]
    ps_ss = p.ps("ps_ss", [128, 512], F32)

    p.dma("sp", g_sb[:, :], A["g"], writes=["g_sb"])
    p.op("pool", lambda e: e.memset(ones[:, :], 1.0), writes=["ones"])

    def load_w(dram_w, sb_w, ncols, tag):
        view = dram_w.rearrange("(c p) w -> p c w", p=128)
        npc = (ncols + 511) // 512
        for i in range(npc):
            a = i * 512
            b = min(ncols, a + 512)
            s = stg[i % 2]
            sk = f"stg{i % 2}"
            p.dma("sp", s[:, :, 0:b - a], view[:, :, a:b], writes=[sk])
            eng = "dve" if i % 2 == 0 else "pool"
            p.op(eng, lambda e, s=s, a=a, b=b: e.tensor_copy(out=sb_w[:, :, a:b], in_=s[:, :, 0:b - a]),
                 reads=[sk], writes=[f"{tag}_{i}"])
        return [f"{tag}_{i}" for i in range(npc)]

    wkeys = []
    if with_out:
        wokeys = load_w(A["wo"], wo_sb, D_MODEL, "wo")
    if not final:
        wkeys = load_w(A["w"], w_sb, WT, "w")

    hview = A["hT"].rearrange("(c p) t -> p c t", p=128)
    if with_out:
        mview = A["mixT"].rearrange("(c p) t -> p c t", p=128)
        if not final:
            hnview = A["hT_new"].rearrange("(c p) t -> p c t", p=128)
    if final:
        yview = A["y"].rearrange("(c p) t -> p c t", p=128)

    evi = [0]
    psi = [0]

    def next_ps():
        i = psi[0] % len(pst)
        psi[0] += 1
        return pst[i], f"ps{i}"

    def evac(ps_ap, pkey, nparts, ncols, dtype, dram_ap):
        i = evi[0] % NEV
        evi[0] += 1
        if dtype is BF16:
            t, tk = ev_bf[i], f"evb{i}"
        else:
            t, tk = ev_f[i], f"evf{i}"
        if evi[0] % 2 == 0:
            p.op("act", lambda e: e.activation(out=t[0:nparts, 0:ncols], in_=ps_ap, func=AF.Copy),
                 reads=[pkey], writes=[tk])
        else:
            p.op("dve", lambda e: e.tensor_copy(out=t[0:nparts, 0:ncols], in_=ps_ap),
                 reads=[pkey], writes=[tk])
        src = t[0:nparts, 0:ncols]
        if len(dram_ap.shape) == 3:
            src = src.rearrange("p (h d) -> p h d", h=dram_ap.shape[1])
        p.dma("pool", dram_ap, src, reads=[tk], skey=tk)

    for ti in range(ntile):
        t0 = ti * NT
        hb, hk = h_sb[ti % 2], f"h{ti % 2}"
        hnb, hnk = hn_sb[ti % 2], f"hn{ti % 2}"
        p.dma("sp", hb[:, :, :], hview[:, :, t0:t0 + NT], writes=[hk])
        if with_out:
            mb, mk = mix_sb[ti % 2], f"mix{ti % 2}"
            p.dma("sp", mb[:, :, :], mview[:, :, t0:t0 + NT], writes=[mk])
            for oc in range(KC):
                ps, pk = next_ps()
                for c in range(KC):
                    p.op("pe", lambda e, ps=ps, c=c, oc=oc, mb=mb: e.matmul(
                        ps[:, 0:NT], lhsT=wo_sb[:, c, oc * 128:(oc + 1) * 128], rhs=mb[:, c, :],
                        start=(c == 0), stop=(c == KC - 1)),
                        reads=[mk] + wokeys, writes=[pk])
                p.op("dve", lambda e, ps=ps, oc=oc, hb=hb: e.tensor_tensor(
                    out=hb[:, oc, :], in0=hb[:, oc, :], in1=ps[:, 0:NT], op=ALU.add),
                    reads=[pk, hk], writes=[hk])
            if not final:
                p.dma("pool", hnview[:, :, t0:t0 + NT], hb[:, :, :], reads=[hk], skey=hk + "_st")
        for c in range(KC):
            p.op("act", lambda e, c=c, hb=hb: e.activation(out=sq_sb[:, :], in_=hb[:, c, :], func=AF.Square),
                 reads=[hk], writes=["sq"])
            p.op("pe", lambda e, c=c: e.matmul(ps_ss[:, 0:NT], lhsT=ones[:, :], rhs=sq_sb[:, :],
                                                 start=(c == 0), stop=(c == KC - 1)),
                 reads=["sq", "ones"], writes=["ps_ss"])
        p.op("act", lambda e: e.activation(out=rstd[:, :], in_=ps_ss[:, 0:NT], func=AF.Sqrt,
                                           bias=EPS, scale=1.0 / D_MODEL),
             reads=["ps_ss"], writes=["rstd"])
        p.op("dve", lambda e: e.reciprocal(out=rstd[:, :], in_=rstd[:, :]),
             reads=["rstd"], writes=["rstd"])
        if final:
            for c in range(KC):
                p.op("dve", lambda e, c=c, hb=hb: e.scalar_tensor_tensor(
                    out=hb[:, c, :], in0=hb[:, c, :], scalar=g_sb[:, c:c + 1], in1=rstd[:, :],
                    op0=ALU.mult, op1=ALU.mult), reads=[hk, "rstd", "g_sb"], writes=[hk])
            p.dma("pool", yview[:, :, t0:t0 + NT], hb[:, :, :], reads=[hk], skey=hk + "_st")
            continue
        for c in range(KC):
            p.op("dve", lambda e, c=c, hb=hb, hnb=hnb: e.scalar_tensor_tensor(
                out=hnb[:, c, :], in0=hb[:, c, :], scalar=g_sb[:, c:c + 1], in1=rstd[:, :],
                op0=ALU.mult, op1=ALU.mult), reads=[hk, "rstd", "g_sb"], writes=[hnk])
        for lst, dt_, dram_o in ((fchunks_bf, BF16, of_bf), (fchunks_f32, F32, of_f32)):
            for ci, (c0, ncol) in enumerate(lst):
                ps, pk = next_ps()
                for c in range(KC):
                    p.op("pe", lambda e, ps=ps, c=c, c0=c0, ncol=ncol, hnb=hnb: e.matmul(
                        ps[0:ncol, 0:NT], lhsT=w_sb[:, c, c0:c0 + ncol], rhs=hnb[:, c, :],
                        start=(c == 0), stop=(c == KC - 1)), reads=[hnk] + wkeys, writes=[pk])
                evac(ps[0:ncol, 0:NT], pk, ncol, NT, dt_, dram_o[ci * 128:ci * 128 + ncol, t0:t0 + NT])
        for blk in range(NT // 128):
            for lst, dt_, dram_o in ((tgroups_bf, BF16, ot_bf), (tgroups_f32, F32, ot_f32)):
                oc0 = 0
                for (c0, ncol) in lst:
                    ps, pk = next_ps()
                    for c in range(KC):
                        p.op("pe", lambda e, ps=ps, c=c, c0=c0, ncol=ncol, hnb=hnb, blk=blk: e.matmul(
                            ps[:, 0:ncol], lhsT=hnb[:, c, blk * 128:(blk + 1) * 128], rhs=w_sb[:, c, c0:c0 + ncol],
                            start=(c == 0), stop=(c == KC - 1)), reads=[hnk] + wkeys, writes=[pk])
                    if (dt_ is BF16) and (c0 in tdst):
                        dst_ap = tdst[c0](ti * (NT // 128) + blk)
                    else:
                        dst_ap = dram_o[t0 + blk * 128:t0 + (blk + 1) * 128, oc0:oc0 + ncol]
                    evac(ps[:, 0:ncol], pk, 128, ncol, dt_, dst_ap)
                    oc0 += ncol
    if own:
        p.finish()
        return p.emit()


def sb_list():
    sbs = []
    b = 0
    while b < NBLK:
        nb = min(4, NBLK - b)
        sbs.append((b, nb))
        b += nb
    return sbs


def build_D(p=None, io=None):
    own = p is None
    NH = 4
    if own:
        p = Prog()
        qT = p.dram("qT", [NH, 64, PPAD], BF16, kind="ExternalInput")
        kT = p.dram("kT", [NH, 64, PPAD], BF16, kind="ExternalInput")
        v = p.dram("v", [PPAD, NH * 64], BF16, kind="ExternalInput")
        gT = p.dram("gT", [NH, 64, PPAD], F32, kind="ExternalInput")
        masks = p.dram("masks", [6, 128, 512], BF16, kind="ExternalInput")
        ut = p.dram("ut", [128, 256], BF16, kind="ExternalInput")
        oT = p.dram("oT", [NH, 64, PPAD], BF16, kind="ExternalOutput")
        A_ = {"qT": qT.ap(), "kT": kT.ap(), "v": v.ap(), "gT": gT.ap(), "masks": masks.ap(), "ut": ut.ap(),
              "oT": oT.ap()}
    else:
        A_ = io

    k_sb = p.sb("k_sb", [64, NH, PPAD], BF16)
    v_sb = p.sb("v_sb", [128, NBLK, NH * 64], BF16)
    m_sb = p.sb("m_sb", [128, 6, 512], BF16)
    u_sb = p.sb("u_sb", [128, 256], BF16)
    q_sb = [p.sb(f"q{i}", [64, NH, 512], BF16) for i in range(2)]
    g_sb = [p.sb(f"g{i}", [64, 512], F32) for i in range(2)]
    e1 = [p.sb(f"e1_{i}", [128, 512], F32) for i in range(2)]
    DEP = 3
    NSP = DEP + 2
    NA = DEP + 3
    sp = [p.sb(f"sp{i}", [128, 512], BF16) for i in range(NSP)]
    NW = 3
    wt = [p.sb(f"wt{i}", [128, 512], BF16) for i in range(NW)]
    acc = p.sb("acc", [128, 512], BF16)
    sg = p.sb("sg", [64, 512], F32)
    ob = [p.sb(f"ob{i}", [64, 512], BF16) for i in range(2)]
    psA = [p.ps(f"psA{i}", [128, 512]) for i in range(NA)]
    psC = [p.ps(f"psC{i}", [128, 512]) for i in range(2)]

    for h in range(NH):
        p.dma("sp", k_sb[:, h, :], A_["kT"][h], writes=[f"k{h}"])
    p.dma("sp", v_sb[:, :, :], A_["v"].rearrange("(b p) f -> p b f", p=128), writes=["v_sb"])
    p.dma("sp", m_sb[:, :, :], A_["masks"].rearrange("m p t -> p m t"), writes=["m_sb"])
    p.dma("sp", u_sb[:, :], A_["ut"], writes=["u_sb"])

    sbs = sb_list()
    its = []
    for si, (b0, nb) in enumerate(sbs):
        for h in range(NH):
            kend = b0 + nb - 1
            for kb in range(kend, -1, -1):
                its.append((si, b0, nb, h, kb, kend))
    n_it = len(its)
    cnt = {"ph": -1}

    def mask_idx(b0, kb):
        if kb >= b0:
            m = kb - b0
            if kb == 0:
                return 5
            return m
        if kb == 0:
            return 4
        return None

    def stage1(i):
        si, b0, nb, h, kb, kend = its[i]
        N = nb * 128
        qb, qk = q_sb[si % 2], f"q{si % 2}"
        if h == 0 and kb == kend:
            p.dma("sp", qb[:, :, 0:N], A_["qT"][:, :, b0 * 128:b0 * 128 + N].rearrange("h d t -> d h t"),
                  writes=[qk])
        A, ak = psA[i % NA], f"psA{i % NA}"
        p.op("pe", lambda e: e.matmul(A[:, 0:N], lhsT=k_sb[:, h, kb * 128:(kb + 1) * 128], rhs=qb[:, h, 0:N],
                                      start=True, stop=False), reads=[f"k{h}", qk], writes=[ak])
        E, ek = e1[i % 2], f"e1_{i % 2}"
        p.op("act", lambda e: e.activation(out=E[:, 0:N], in_=A[:, 0:N], func=AF.Exp, scale=0.125),
             reads=[ak], writes=[ek])
        S, sk = sp[i % NSP], f"sp{i % NSP}"
        p.op("act", lambda e: e.activation(out=S[:, 0:N], in_=E[:, 0:N], func=AF.Ln, bias=1.0, scale=1.0),
             reads=[ek], writes=[sk])
        mi = mask_idx(b0, kb)
        if mi is not None:
            p.op("pool", lambda e: e.tensor_tensor(out=S[:, 0:N], in0=S[:, 0:N], in1=m_sb[:, mi, 0:N], op=ALU.mult),
                 reads=[sk, "m_sb"], writes=[sk])

    def stage2(i):
        si, b0, nb, h, kb, kend = its[i]
        N = nb * 128
        qb, qk = q_sb[si % 2], f"q{si % 2}"
        S, sk = sp[i % NSP], f"sp{i % NSP}"
        B, bk = psA[i % NA], f"psA{i % NA}"
        hi = si * NH + h
        C, ck = psC[hi % 2], f"psC{hi % 2}"
        first = (kb == kend)
        p.op("pe", lambda e: e.matmul(B[:, 0:N], lhsT=u_sb[:, 0:128], rhs=S[:, 0:N],
                                      start=False, stop=first), reads=[sk, "u_sb"], writes=[bk])
        if not first:
            p.op("pe", lambda e: e.matmul(B[:, 0:N], lhsT=u_sb[:, 128:256], rhs=acc[:, 0:N],
                                          start=False, stop=True), reads=["acc", "u_sb"], writes=[bk])
        W, wk = wt[i % NW], f"wt{i % NW}"
        p.op("act", lambda e: e.activation(out=W[:, 0:N], in_=B[:, 0:N], func=AF.Exp, scale=0.125),
             reads=[bk], writes=[wk])
        mi = mask_idx(b0, kb)
        if mi is not None:
            p.op("pool", lambda e: e.tensor_tensor(out=W[:, 0:N], in0=W[:, 0:N], in1=m_sb[:, mi, 0:N], op=ALU.mult),
                 reads=[wk, "m_sb"], writes=[wk])
        if first:
            p.op("dve", lambda e: e.tensor_copy(out=acc[:, 0:N], in_=S[:, 0:N]), reads=[sk], writes=["acc"])
        elif kb > 0:
            p.op("dve", lambda e: e.tensor_tensor(out=acc[:, 0:N], in0=acc[:, 0:N], in1=S[:, 0:N], op=ALU.add),
                 reads=[sk, "acc"], writes=["acc"])

    def stage2b(i):
        si, b0, nb, h, kb, kend = its[i]
        N = nb * 128
        hi = si * NH + h
        C, ck = psC[hi % 2], f"psC{hi % 2}"
        first = (kb == kend)
        W, wk = wt[i % NW], f"wt{i % NW}"
        p.op("pe", lambda e: e.matmul(C[0:64, 0:N], lhsT=v_sb[:, kb, h * 64:(h + 1) * 64], rhs=W[:, 0:N],
                                      start=first, stop=(kb == 0)), reads=[wk, "v_sb"], writes=[ck])
        if kb == 0:
            G, gk = g_sb[hi % 2], f"g{hi % 2}"
            p.dma("sp", G[:, 0:N], A_["gT"][h, :, b0 * 128:b0 * 128 + N], writes=[gk])
            p.op("act", lambda e: e.activation(out=sg[:, 0:N], in_=G[:, 0:N], func=AF.Exp, scale=-1.0),
                 reads=[gk], writes=["sg"])
            p.op("dve", lambda e: e.tensor_scalar(out=sg[:, 0:N], in0=sg[:, 0:N], scalar1=1.0, scalar2=None,
                                                  op0=ALU.add), reads=["sg"], writes=["sg"])
            p.op("dve", lambda e: e.reciprocal(out=sg[:, 0:N], in_=sg[:, 0:N]), reads=["sg"], writes=["sg"])
            p.op("dve", lambda e: e.tensor_tensor(out=sg[:, 0:N], in0=sg[:, 0:N], in1=G[:, 0:N], op=ALU.mult),
                 reads=["sg", gk], writes=["sg"])
            O, ok_ = ob[hi % 2], f"ob{hi % 2}"
            p.op("dve", lambda e: e.tensor_tensor(out=O[:, 0:N], in0=C[0:64, 0:N], in1=sg[:, 0:N], op=ALU.mult),
                 reads=[ck, "sg"], writes=[ok_])
            p.dma("pool", A_["oT"][h, :, b0 * 128:b0 * 128 + N], O[:, 0:N], reads=[ok_], skey=ok_)

    for i in range(n_it + DEP + 1):
        if i < n_it:
            stage1(i)
        if 0 <= i - DEP < n_it:
            stage2(i - DEP)
        if 0 <= i - DEP - 1 < n_it:
            stage2b(i - DEP - 1)
    if own:
        p.finish()
        return p.emit()


def d_consts():
    import ml_dtypes
    masks = np.zeros((6, 128, 512), np.float32)
    s = np.arange(128)[:, None]
    t = np.arange(512)[None, :]
    for m in range(4):
        j = t // 128
        tl = t % 128
        masks[m] = np.where(j > m, 1.0, np.where(j == m, (tl > s).astype(np.float32), 0.0))
    masks[4] = (s >= PADF).astype(np.float32) * np.ones((128, 512), np.float32)
    masks[5] = masks[0] * masks[4]
    ut = np.zeros((128, 256), np.float32)
    jj = np.arange(128)[:, None]
    ss = np.arange(128)[None, :]
    ut[:, 0:128] = np.where(jj >= ss, -8.0, 0.0)
    ut[:, 128:256] = -8.0
    return masks.astype(ml_dtypes.bfloat16), ut.astype(ml_dtypes.bfloat16)


def build_L(kind, p=None, io=None):
    own = p is None
    gla = (kind == "gla")
    if own:
        p = Prog()
        A_ = {}
        A_["qT"] = p.dram("qT", [128, PPAD], BF16, kind="ExternalInput").ap()
        A_["kT"] = p.dram("kT", [128, PPAD], BF16, kind="ExternalInput").ap()
        A_["v"] = p.dram("v", [PPAD, 256], BF16, kind="ExternalInput").ap()
        A_["gT"] = p.dram("gT", [256, PPAD], F32, kind="ExternalInput").ap()
        A_["cst"] = p.dram("cst", [128, 3, 256], F32, kind="ExternalInput").ap()
        if gla:
            A_["baT"] = p.dram("baT", [16, PPAD], F32, kind="ExternalInput").ap()
            A_["w2a"] = p.dram("w2a", [17, 128], F32, kind="ExternalInput").ap()
        else:
            A_["qsT"] = p.dram("qsT", [128, PPAD], BF16, kind="ExternalInput").ap()
            A_["ksT"] = p.dram("ksT", [128, PPAD], BF16, kind="ExternalInput").ap()
            A_["tabs"] = p.dram("tabs", [4, 128, PPAD], F32, kind="ExternalInput").ap()
            A_["decc"] = p.dram("decc", [64, 2], F32, kind="ExternalInput").ap()
        A_["oT"] = p.dram("oT", [256, PPAD], BF16, kind="ExternalOutput").ap()
    else:
        A_ = io

    cst_sb = p.sb("cst_sb", [128, 3, 256], F32)
    idb = p.sb("idb", [64, 64], BF16)
    m4 = p.sb("m4", [64, 256], BF16)
    ones = p.sb("ones", [128, 128], F32)
    S = p.sb("S", [64, 2, 128], F32)
    Sb = p.sb("Sb", [64, 2, 128], BF16)
    SbX = [Sb, p.sb("Sb1", [64, 2, 128], BF16)]
    kvd = p.sb("kvd", [64, 4, 128], F32)
    NB2 = 2
    NBI = 4
    q_in = [p.sb(f"q_in{i}", [64, 2, 128], BF16) for i in range(NBI)]
    k_in = [p.sb(f"k_in{i}", [64, 2, 128], BF16) for i in range(NBI)]
    v_in = [p.sb(f"v_in{i}", [64, 2, 256], BF16) for i in range(NBI)]
    g_in = [p.sb(f"g_in{i}", [128, 2, 128], F32) for i in range(NBI)]
    qt = [p.sb(f"qt{i}", [64, 2, 128], BF16) for i in range(NB2)]
    kt = [p.sb(f"kt{i}", [64, 2, 128], BF16) for i in range(NB2)]
    ktt = [p.sb(f"ktt{i}", [64, 256], BF16) for i in range(NB2)]
    dec = [p.sb(f"dec{i}", [64, 4], F32) for i in range(NB2)]
    attm = [p.sb(f"attm{i}", [64, 256], BF16) for i in range(NB2)]
    if gla:
        ba_in = [p.sb(f"ba_in{i}", [17, 128], F32) for i in range(NBI)]
        w2_sb = p.sb("w2_sb", [17, 128], F32)
        la = p.sb("la", [128, 128], F32)
        ep = p.sb("ep", [64, 256], F32)
        em = p.sb("em", [64, 256], F32)
    else:
        qs_in = [p.sb(f"qs_in{i}", [64, 2, 128], BF16) for i in range(NBI)]
        ks_in = [p.sb(f"ks_in{i}", [64, 2, 128], BF16) for i in range(NBI)]
        tb_in = [p.sb(f"tb_in{i}", [64, 4, 2, 128], F32) for i in range(NBI)]
        t1 = p.sb("t1", [64, 256], F32)
        t2 = p.sb("t2", [64, 256], F32)
        dec_c = p.sb("dec_c", [64, 2], F32)
    ofs = [p.sb(f"of{i}", [128, 256], F32) for i in range(2)]
    osq = p.sb("osq", [128, 256], F32)
    rs = p.sb("rs", [128, 256], F32)
    sg = p.sb("sg", [128, 256], F32)
    yb = [p.sb(f"yb{i}", [128, 2, 128], BF16) for i in range(2)]
    ln8t = p.sb("ln8t", [128, 1], F32)
    epst = p.sb("epst", [128, 1], F32)

    ps_x = p.ps("ps_x", [128, 512])
    ps_c = p.ps("ps_c", [128, 512])
    ps_t = p.ps("ps_t", [128, 512])
    ps_a = p.ps("ps_a", [128, 512])
    ps_o = p.ps("ps_o", [128, 512])
    ps_kv = p.ps("ps_kv", [128, 512])
    ps_n = p.ps("ps_n", [128, 512])

    p.dma("sp", cst_sb[:, :, :], A_["cst"], writes=["cst"])
    p.op("dve", lambda e: e.tensor_copy(out=m4[:, :], in_=cst_sb[0:64, 1, :]), reads=["cst"], writes=["m4"])
    p.op("dve", lambda e: e.tensor_copy(out=idb[:, :], in_=cst_sb[0:64, 2, 0:64]), reads=["cst"], writes=["idb"])
    p.op("pool", lambda e: e.memset(ones[:, :], 1.0 / 128.0), writes=["ones"])
    p.op("pool", lambda e: e.memset(S[:, :, :], 0.0), writes=["S0", "S1"])
    p.op("pool", lambda e: e.memset(SbX[0][:, :, :], 0.0), writes=["Sb0"])
    p.op("pool", lambda e: e.memset(SbX[1][:, :, :], 0.0), writes=["Sb1"])
    p.op("pool", lambda e: e.memset(ln8t[:, :], float(np.log(0.125))), writes=["ln8"])
    p.op("pool", lambda e: e.memset(epst[:, :], EPS), writes=["epsn"])
    if gla:
        if isinstance(A_["w2a"], tuple):
            p.dma("sp", w2_sb[0:16, :], A_["w2a"][0], writes=["w2"])
            p.dma("sp", w2_sb[16:17, :], A_["w2a"][1], writes=["w2"])
        else:
            p.dma("sp", w2_sb[:, :], A_["w2a"], writes=["w2"])
        for i in range(NBI):
            p.op("pool", lambda e, i=i: e.memset(ba_in[i][:, :], 1.0), writes=[f"ba_in{i}"])
    else:
        p.dma("sp", dec_c[:, :], A_["decc"], writes=["dec_c"])

    gview = A_["gT"].rearrange("(h e) t -> e h t", h=2)
    oview = A_["oT"].rearrange("(h e) t -> e h t", h=2)
    hd = lambda ap_: ap_.rearrange("(h d) t -> d h t", h=2)

    def load(b):
        i = b % NBI
        c0 = b * 128
        p.dma("sp", q_in[i][:, :, :], hd(A_["qT"])[:, :, c0:c0 + 128], writes=[f"q_in{i}"])
        p.dma("sp", k_in[i][:, :, :], hd(A_["kT"])[:, :, c0:c0 + 128], writes=[f"k_in{i}"])
        p.dma("sp", v_in[i][:, :, :], A_["v"][c0:c0 + 128, :].rearrange("(n s) f -> s n f", n=2), writes=[f"v_in{i}"])
        p.dma("sp", g_in[i][:, :, :], gview[:, :, c0:c0 + 128], writes=[f"g_in{i}"])
        if gla:
            p.dma("sp", ba_in[i][0:16, :], A_["baT"][:, c0:c0 + 128], writes=[f"ba_in{i}"])
        else:
            p.dma("sp", qs_in[i][:, :, :], hd(A_["qsT"])[:, :, c0:c0 + 128], writes=[f"qs_in{i}"])
            p.dma("sp", ks_in[i][:, :, :], hd(A_["ksT"])[:, :, c0:c0 + 128], writes=[f"ks_in{i}"])
            for f in range(4):
                p.dma("sp", tb_in[i][:, f, :, :], hd(A_["tabs"][f])[:, :, c0:c0 + 128], writes=[f"tb_in{i}"])

    fl = lambda t_: t_.rearrange("p h t -> p (h t)")

    def prep(b):
        i = b % NB2
        ii = b % NBI
        if gla:
            p.op("pe", lambda e: e.matmul(ps_x[:, 0:128], lhsT=ba_in[ii][:, :], rhs=w2_sb[:, :], start=True, stop=True),
                 reads=[f"ba_in{ii}", "w2"], writes=["ps_x"])
            p.op("act", lambda e: e.activation(out=la[:, :], in_=ps_x[:, 0:128], func=AF.Exp, scale=-1.0),
                 reads=["ps_x"], writes=["la"])
            p.op("act", lambda e: e.activation(out=la[:, :], in_=la[:, :], func=AF.Ln, bias=1.0),
                 reads=["la"], writes=["la"])
            for h in range(2):
                p.op("pe", lambda e, h=h: e.matmul(ps_c[0:64, h * 128:(h + 1) * 128], lhsT=la[:, h * 64:(h + 1) * 64],
                                                   rhs=cst_sb[:, 0, 0:128], start=True, stop=True),
                     reads=["la", "cst"], writes=["ps_c"])
            p.op("act", lambda e: e.activation(out=ep[:, :], in_=ps_c[0:64, 0:256], func=AF.Exp, bias=ln8t[0:64, 0:1]),
                 reads=["ps_c", "ln8"], writes=["ep"])
            p.op("act", lambda e: e.activation(out=em[:, :], in_=ps_c[0:64, 0:256], func=AF.Exp, scale=-1.0),
                 reads=["ps_c"], writes=["em"])
            p.op("act", lambda e: e.activation(out=dec[i][:, :], in_=ps_c[0:64, 63:256:64], func=AF.Exp),
                 reads=["ps_c"], writes=[f"dec{i}"])
            p.op("dve", lambda e: e.tensor_tensor(out=fl(qt[i][:, :, :]), in0=fl(q_in[ii][:, :, :]), in1=ep[:, :], op=ALU.mult),
                 reads=[f"q_in{ii}", "ep"], writes=[f"qt{i}"])
            p.op("dve", lambda e: e.tensor_tensor(out=fl(kt[i][:, :, :]), in0=fl(k_in[ii][:, :, :]), in1=em[:, :], op=ALU.mult),
                 reads=[f"k_in{ii}", "em"], writes=[f"kt{i}"])
        else:
            for (a_in, s_in, fa, fs, dst, dk_) in ((q_in, qs_in, 0, 1, qt, "qt"), (k_in, ks_in, 2, 3, kt, "kt")):
                p.op("dve", lambda e, a_in=a_in, fa=fa: e.tensor_tensor(
                    out=t1[:, :], in0=fl(a_in[ii][:, :, :]), in1=fl(tb_in[ii][:, fa, :, :]), op=ALU.mult),
                    reads=[f"q_in{ii}", f"k_in{ii}", f"tb_in{ii}"], writes=["t1"])
                p.op("pool", lambda e, s_in=s_in, fs=fs: e.tensor_tensor(
                    out=t2[:, :], in0=fl(s_in[ii][:, :, :]), in1=fl(tb_in[ii][:, fs, :, :]), op=ALU.mult),
                    reads=[f"qs_in{ii}", f"ks_in{ii}", f"tb_in{ii}"], writes=["t2"])
                p.op("dve", lambda e, dst=dst: e.tensor_tensor(out=fl(dst[i][:, :, :]), in0=t1[:, :], in1=t2[:, :], op=ALU.add),
                     reads=["t1", "t2"], writes=[f"{dk_}{i}"])
        for h in range(2):
            for n in range(2):
                j = h * 2 + n
                p.op("pe", lambda e, h=h, n=n, j=j: e.matmul(ps_t[0:64, j * 64:(j + 1) * 64], lhsT=kt[i][:, h, n * 64:(n + 1) * 64],
                                                            rhs=idb[:, :], start=True, stop=True),
                     reads=[f"kt{i}", "idb"], writes=["ps_t"])
        p.op("act", lambda e: e.activation(out=ktt[i][:, :], in_=ps_t[0:64, 0:256], func=AF.Copy),
             reads=["ps_t"], writes=[f"ktt{i}"])

    def core(b):
        i = b % NB2
        ii = b % NBI
        c0 = b * 128
        for h in range(2):
            for n in range(2):
                j = h * 2 + n
                p.op("pe", lambda e, h=h, n=n, j=j: e.matmul(
                    ps_a[0:64, j * 64:(j + 1) * 64], lhsT=kt[i][:, h, n * 64:(n + 1) * 64], rhs=qt[i][:, h, n * 64:(n + 1) * 64],
                    start=True, stop=True), reads=[f"kt{i}", f"qt{i}"], writes=["ps_a"])
        p.op("dve", lambda e: e.tensor_tensor(out=attm[i][:, :], in0=ps_a[0:64, 0:256], in1=m4[:, :], op=ALU.mult),
             reads=["ps_a", "m4"], writes=[f"attm{i}"])
        for n in range(2):
            for h in range(2):
                j = h * 2 + n
                j4 = n * 2 + h
                p.op("pe", lambda e, n=n, h=h, j=j, j4=j4: e.matmul(
                    ps_kv[0:64, j4 * 128:(j4 + 1) * 128], lhsT=ktt[i][:, j * 64:(j + 1) * 64], rhs=v_in[ii][:, n, h * 128:(h + 1) * 128],
                    start=True, stop=True), reads=[f"ktt{i}", f"v_in{ii}"], writes=["ps_kv"])
        dkey = f"dec{i}" if gla else "dec_c"
        for n in range(2):
            for h in range(2):
                j = h * 2 + n
                j4 = n * 2 + h
                dsc = dec[i][:, j:j + 1] if gla else dec_c[:, h:h + 1]
                p.op("act", lambda e, dsc=dsc, j4=j4: e.activation(out=kvd[:, j4, :], in_=ps_kv[0:64, j4 * 128:(j4 + 1) * 128],
                                                                   func=AF.Copy, scale=dsc),
                     reads=["ps_kv", dkey], writes=[f"kvd{j4}"])
        for n in range(2):
            cn = 2 * b + n
            SBc, sbk = SbX[cn % 2], f"Sb{cn % 2}"
            SBn, sbnk = SbX[(cn + 1) % 2], f"Sb{(cn + 1) % 2}"
            for h in range(2):
                j = h * 2 + n
                oc = slice(h * 128 + n * 64, h * 128 + (n + 1) * 64)
                p.op("pe", lambda e, n=n, h=h, j=j, oc=oc: e.matmul(
                    ps_o[:, oc], lhsT=v_in[ii][:, n, h * 128:(h + 1) * 128], rhs=attm[i][:, j * 64:(j + 1) * 64],
                    start=True, stop=False), reads=[f"v_in{ii}", f"attm{i}"], writes=["ps_o"])
                p.op("pe", lambda e, n=n, h=h, oc=oc, SBc=SBc: e.matmul(
                    ps_o[:, oc], lhsT=SBc[:, h, :], rhs=qt[i][:, h, n * 64:(n + 1) * 64], start=False, stop=True),
                    reads=[sbk, f"qt{i}"], writes=["ps_o"])
            for h in range(2):
                j = h * 2 + n
                j4 = n * 2 + h
                dsc = dec[i][:, j:j + 1] if gla else dec_c[:, h:h + 1]
                p.op("dve", lambda e, dsc=dsc, h=h, j4=j4: e.scalar_tensor_tensor(
                    out=S[:, h, :], in0=S[:, h, :], scalar=dsc, in1=kvd[:, j4, :],
                    op0=ALU.mult, op1=ALU.add), reads=[f"S{h}", f"kvd{j4}", dkey], writes=[f"S{h}"])
            p.op("act", lambda e, SBn=SBn: e.activation(out=fl(SBn[:, :, :]), in_=fl(S[:, :, :]), func=AF.Copy),
                 reads=["S0", "S1"], writes=[sbnk])
        OF, ofk = ofs[b % 2], f"of{b % 2}"
        p.op("act", lambda e: e.activation(out=OF[:, :], in_=ps_o[:, 0:256], func=AF.Copy), reads=["ps_o"], writes=[ofk])

    def tail(b):
        ii = b % NBI
        c0 = b * 128
        of, ofk = ofs[b % 2], f"of{b % 2}"
        if not gla:
            p.op("pe", lambda e: e.matmul(ps_n[:, 0:256], lhsT=ones[:, :], rhs=of[:, :], start=True, stop=True),
                 reads=[ofk, "ones"], writes=["ps_n"])
            p.op("dve", lambda e: e.tensor_tensor(out=of[:, :], in0=of[:, :], in1=ps_n[:, 0:256], op=ALU.subtract),
                 reads=[ofk, "ps_n"], writes=[ofk])
        p.op("act", lambda e: e.activation(out=osq[:, :], in_=of[:, :], func=AF.Square), reads=[ofk], writes=["osq"])
        p.op("pe", lambda e: e.matmul(ps_n[:, 256:512], lhsT=ones[:, :], rhs=osq[:, :], start=True, stop=True),
             reads=["osq", "ones"], writes=["ps_n"])
        p.op("act", lambda e: e.activation(out=rs[:, :], in_=ps_n[:, 256:512], func=AF.Sqrt, bias=epst[:, 0:1], scale=1.0),
             reads=["ps_n", "epsn"], writes=["rs"])
        p.op("dve", lambda e: e.reciprocal(out=rs[:, :], in_=rs[:, :]), reads=["rs"], writes=["rs"])
        p.op("dve", lambda e: e.tensor_tensor(out=of[:, :], in0=of[:, :], in1=rs[:, :], op=ALU.mult),
             reads=[ofk, "rs"], writes=[ofk])
        G = fl(g_in[ii][:, :, :])
        p.op("act", lambda e: e.activation(out=sg[:, :], in_=G, func=AF.Exp, scale=-1.0), reads=[f"g_in{ii}"], writes=["sg"])
        p.op("dve", lambda e: e.tensor_scalar(out=sg[:, :], in0=sg[:, :], scalar1=1.0, scalar2=None, op0=ALU.add),
             reads=["sg"], writes=["sg"])
        p.op("dve", lambda e: e.reciprocal(out=sg[:, :], in_=sg[:, :]), reads=["sg"], writes=["sg"])
        p.op("pool", lambda e: e.tensor_tensor(out=sg[:, :], in0=sg[:, :], in1=G, op=ALU.mult),
             reads=["sg", f"g_in{ii}"], writes=["sg"])
        Y, yk = yb[b % 2], f"yb{b % 2}"
        p.op("dve", lambda e: e.tensor_tensor(out=fl(Y[:, :, :]), in0=of[:, :], in1=sg[:, :], op=ALU.mult),
             reads=[ofk, "sg"], writes=[yk])
        p.dma("pool", oview[:, :, c0:c0 + 128], Y[:, :, :], reads=[yk], skey=yk)

    load(0)
    if NBLK > 1:
        load(1)
    prep(0)
    for b in range(NBLK):
        if b + 2 < NBLK:
            load(b + 2)
        if b + 1 < NBLK:
            prep(b + 1)
        core(b)
        if b >= 1:
            tail(b - 1)
    tail(NBLK - 1)
    if own:
        p.finish()
        return p.emit()


def l_consts():
    t = np.arange(128)
    same = (t[:, None] // 64) == (t[None, :] // 64)
    tri = np.where(same & (t[:, None] <= t[None, :]), -1.0 / 16.0, 0.0)
    cst = np.zeros((128, 3, 256), np.float32)
    cst[:, 0, 0:128] = tri
    s_ = np.arange(64)
    m64 = (s_[:, None] <= s_[None, :]).astype(np.float32)
    cst[0:64, 1, :] = np.tile(m64, (1, 4))
    cst[0:64, 2, :] = np.tile(np.eye(64, dtype=np.float32), (1, 4))
    return cst


def ret_tables(r):
    pos = np.arange(PPAD)
    inv = (10000.0 ** (-np.arange(32, dtype=np.float32) / 32)).astype(np.float32)
    ang = (pos.astype(np.float32)[:, None] * inv[None, :]).astype(np.float32).astype(np.float64)
    cos = np.cos(ang).T
    sin = np.sin(ang).T
    Cos = np.concatenate([cos, cos], 0)
    SinS = np.concatenate([-sin, sin], 0)
    c = (pos % 64).astype(np.float64)
    tabs = np.zeros((4, 128, PPAD), np.float64)
    decc = np.zeros((64, 2), np.float64)
    for hl in range(2):
        h = 2 * r + hl
        gam = 1.0 - 2.0 ** (-5.0 - h)
        xi = gam ** (c + 1.0)
        kf = gam ** (-(c + 1.0)) / 8.0
        sl = slice(hl * 64, (hl + 1) * 64)
        tabs[0, sl] = Cos * xi
        tabs[1, sl] = SinS * xi
        tabs[2, sl] = Cos * kf
        tabs[3, sl] = SinS * kf
        decc[:, hl] = gam ** 64
    return tabs.astype(np.float32), decc.astype(np.float32)


NQB = 33
NQ = NQB * 128


def a_groups():
    gs = []
    jj = 0
    while jj < NQB:
        n = min(4, NQB - jj)
        gs.append((jj, n))
        jj += n
    return gs


def build_A(p=None, io=None):
    own = p is None
    NH = 8
    if own:
        p = Prog()
        A_ = {}
        A_["aqT"] = p.dram("aqT", [NH, 64, NQ], BF16, kind="ExternalInput").ap().rearrange("h d (j t) -> h d j t", t=128)
        A_["agT"] = p.dram("agT", [NH, 64, NQ], F32, kind="ExternalInput").ap().rearrange("h d (j t) -> h d j t", t=128)
        A_["iqT"] = p.dram("iqT", [NH, 64, NQ], BF16, kind="ExternalInput").ap().rearrange("h d (j t) -> h d j t", t=128)
        A_["iw"] = p.dram("iw", [128, NQB, 8], F32, kind="ExternalInput").ap()
        A_["akT"] = p.dram("akT", [NH, 64, PPAD], BF16, kind="ExternalInput").ap()
        A_["avh"] = p.dram("avh", [NH, 128, NBLK, 64], BF16, kind="ExternalInput").ap()
        A_["ikT"] = p.dram("ikT", [64, PPAD], BF16, kind="ExternalInput").ap()
        A_["mskd"] = p.dram("mskd", [128, 3, 128], F32, kind="ExternalInput").ap()
        A_["btd"] = p.dram("btd", [128, 3, NH, 128], F32, kind="ExternalInput").ap()
        A_["b15d"] = p.dram("b15d", [128, NH], F32, kind="ExternalInput").ap()
        A_["cstd"] = p.dram("cstd", [128, 160], F32, kind="ExternalInput").ap()
        A_["oT"] = p.dram("oT", [NH, 64, NQ], BF16, kind="ExternalOutput").ap().rearrange("h d (j t) -> h d j t", t=128)
    else:
        A_ = io

    NIT = 26
    MAXKB = 64
    ik_sb = p.sb("ik_sb", [64, PPAD], BF16)
    k_sb = p.sb("k_sb", [64, PPAD], BF16)
    v_sb = p.sb("v_sb", [128, NBLK, 64], BF16)
    mT = p.sb("mT", [128, MAXKB * 512], BF16)
    score = p.sb("score", [128, PPAD], F32)
    junk = p.sb("junk", [128, PPAD], BF16)
    iq_sb = p.sb("iq_sb", [64, NH, 512], BF16)
    aq_sb = p.sb("aq_sb", [64, NH, 512], BF16)
    iw_sb = p.sb("iw_sb", [128, NQB, 8], F32)
    msk = p.sb("msk", [128, 3, 128], F32)
    bt = p.sb("bt", [128, 3, NH, 128], F32)
    b15 = p.sb("b15", [128, NH], F32)
    cst = p.sb("cst", [128, 160], F32)
    idb = p.sb("idb", [128, 128], BF16)
    onesb = p.sb("onesb", [128, 64], BF16)
    R = [p.sb(f"R{i}", [128, 512], F32) for i in range(2)]
    mk = [p.sb(f"mk{i}", [128, 512], BF16) for i in range(2)]
    PT = [p.sb(f"PT{i}", [128, 512], BF16) for i in range(3)]
    st = p.sb("st", [128, 8], F32)
    W = p.sb("W", [128, NIT], F32)
    g_sb = p.sb("g_sb", [64, 512], F32)
    sg = p.sb("sg", [64, 512], F32)
    rd = p.sb("rd", [64, 512], F32)
    ob = [p.sb(f"ob{i}", [64, 512], BF16) for i in range(2)]

    psI = [p.ps(f"psI{i}", [128, 512]) for i in range(2)]
    psT = p.ps("psT", [128, 512])
    psS = [p.ps(f"psS{i}", [128, 512]) for i in range(3)]
    psO = p.ps("psO", [128, 512])
    psD = p.ps("psD", [128, 512])

    p.dma("sp", ik_sb[:, :], A_["ikT"], writes=["ik"])
    p.dma("sp", iw_sb[:, :, :], A_["iw"], writes=["iw"])
    p.dma("sp", msk[:, :, :], A_["mskd"], writes=["msk"])
    p.dma("sp", bt[:, :, :, :], A_["btd"], writes=["bt"])
    p.dma("sp", b15[:, :], A_["b15d"], writes=["b15"])
    p.dma("sp", cst[:, :], A_["cstd"], writes=["cst"])
    p.op("dve", lambda e: e.tensor_copy(out=idb[:, :], in_=cst[:, 0:128]), reads=["cst"], writes=["idb"])
    p.op("pool", lambda e: e.memset(onesb[:, :], 1.0), writes=["onesb"])
    p.op("pool", lambda e: e.memset(st[:, 6:7], float(TOPK) - 0.5), writes=["k255"])
    for h in range(NH):
        for w_ in range(3):
            p.op("dve", lambda e, h=h, w_=w_: e.tensor_scalar(out=bt[:, w_, h, :], in0=bt[:, w_, h, :], scalar1=b15[:, h:h + 1],
                                                              scalar2=None, op0=ALU.subtract), reads=["bt", "b15"], writes=["bt"])
    p.op("dve", lambda e: e.tensor_scalar(out=bt[:, :, :, :].rearrange("p a h t -> p (a h t)"),
                                          in0=bt[:, :, :, :].rearrange("p a h t -> p (a h t)"), scalar1=8.0, scalar2=None,
                                          op0=ALU.mult), reads=["bt"], writes=["bt"])

    evi = [0]
    for gi, (jj0, nq) in enumerate(a_groups()):
        N = nq * 128
        g8 = 8 * gi
        nkb = min(8 * gi + 8, NBLK) if nq == 4 else NBLK
        q0 = jj0 * 128
        for hh in range(NH):
            p.dma("sp", iq_sb[:, hh, 0:N].rearrange("d (j t) -> d j t", t=128), A_["iqT"][hh, :, jj0:jj0 + nq, :], writes=["iq"])
            p.dma("sp", aq_sb[:, hh, 0:N].rearrange("d (j t) -> d j t", t=128), A_["aqT"][hh, :, jj0:jj0 + nq, :], writes=["aq"])
        p.op("pool", lambda e, nkb=nkb, N=N: e.memset(mT[:, 0:nkb * N], 0.0), writes=["mT"])
        mTv = mT[:, 0:nkb * N].rearrange("p (k t) -> p k t", t=N)
        for j in range(nq):
            jj = jj0 + j
            nk = g8 + 2 * j + 2
            nkeys = nk * 128
            ntl = (nkeys + 511) // 512
            for kt_ in range(ntl):
                c0 = kt_ * 512
                w_ = min(512, nkeys - c0)
                for h in range(NH):
                    ii = evi[0] % 2
                    evi[0] += 1
                    p.op("pe", lambda e, ii=ii, h=h, j=j, c0=c0, w_=w_: e.matmul(
                        psI[ii][:, 0:w_], lhsT=iq_sb[:, h, j * 128:(j + 1) * 128], rhs=ik_sb[:, c0:c0 + w_],
                        start=True, stop=True), reads=["iq", "ik"], writes=[f"psI{ii}"])
                    p.op("act", lambda e, ii=ii, w_=w_: e.activation(out=R[ii][:, 0:w_], in_=psI[ii][:, 0:w_], func=AF.Relu),
                         reads=[f"psI{ii}"], writes=[f"R{ii}"])
                    if h == 0:
                        p.op("dve", lambda e, ii=ii, c0=c0, w_=w_, jj=jj: e.tensor_scalar(
                            out=score[:, c0:c0 + w_], in0=R[ii][:, 0:w_], scalar1=iw_sb[:, jj, 0:1], scalar2=None,
                            op0=ALU.mult), reads=[f"R{ii}", "iw"], writes=["score"])
                    else:
                        p.op("dve", lambda e, ii=ii, c0=c0, w_=w_, jj=jj, h=h: e.scalar_tensor_tensor(
                            out=score[:, c0:c0 + w_], in0=R[ii][:, 0:w_], scalar=iw_sb[:, jj, h:h + 1],
                            in1=score[:, c0:c0 + w_], op0=ALU.mult, op1=ALU.add),
                            reads=[f"R{ii}", "iw", "score"], writes=["score"])
            p.op("dve", lambda e, nkeys=nkeys: e.reduce_max(out=st[:, 0:1], in_=score[:, 0:nkeys], axis=AX.X,
                                                            apply_absolute_value=True), reads=["score"], writes=["stB"])
            p.op("dve", lambda e, nk=nk: e.tensor_tensor(out=score[:, (nk - 2) * 128:(nk - 1) * 128],
                                                         in0=score[:, (nk - 2) * 128:(nk - 1) * 128], in1=msk[:, 0, :], op=ALU.add),
                 reads=["score", "msk"], writes=["score"])
            p.op("dve", lambda e, nk=nk: e.tensor_tensor(out=score[:, (nk - 1) * 128:nk * 128],
                                                         in0=score[:, (nk - 1) * 128:nk * 128], in1=msk[:, 1, :], op=ALU.add),
                 reads=["score", "msk"], writes=["score"])
            p.op("dve", lambda e: e.tensor_tensor(out=score[:, 0:128], in0=score[:, 0:128], in1=msk[:, 2, :], op=ALU.add),
                 reads=["score", "msk"], writes=["score"])
            p.op("dve", lambda e: e.tensor_scalar(out=st[:, 1:2], in0=st[:, 0:1], scalar1=2.002, scalar2=1e-6,
                                                  op0=ALU.mult, op1=ALU.add), reads=["stB"], writes=["stw0"])
            p.op("dve", lambda e: e.tensor_scalar(out=st[:, 2:3], in0=st[:, 1:2], scalar1=-0.5, scalar2=None, op0=ALU.mult),
                 reads=["stw0"], writes=["stlo"])
            p.op("dve", lambda e: e.tensor_scalar(out=W[:, :], in0=cst[:, 128:128 + NIT], scalar1=st[:, 1:2], scalar2=None,
                                                  op0=ALU.mult), reads=["cst", "stw0"], writes=["W"])
            for it in range(NIT):
                p.op("dve", lambda e, it=it: e.tensor_tensor(out=st[:, 3:4], in0=st[:, 2:3], in1=W[:, it:it + 1], op=ALU.add),
                     reads=["stlo", "W"], writes=["stmid"])
                p.op("dve", lambda e, nkeys=nkeys: e.tensor_scalar(out=junk[:, 0:nkeys], in0=score[:, 0:nkeys],
                                                                   scalar1=st[:, 3:4], scalar2=0.0, op0=ALU.is_ge, op1=ALU.add,
                                                                   accum_out=st[:, 4:5]),
                     reads=["score", "stmid"], writes=["junk", "stcnt"])
                p.op("dve", lambda e, it=it: e.tensor_scalar(out=st[:, 5:6], in0=st[:, 4:5], scalar1=st[:, 6:7],
                                                             scalar2=W[:, it:it + 1], op0=ALU.is_gt, op1=ALU.mult),
                     reads=["stcnt", "k255", "W"], writes=["stt"])
                p.op("dve", lambda e: e.tensor_tensor(out=st[:, 2:3], in0=st[:, 2:3], in1=st[:, 5:6], op=ALU.add),
                     reads=["stlo", "stt"], writes=["stlo"])
            p.op("dve", lambda e: e.tensor_scalar(out=st[:, 7:8], in0=st[:, 2:3], scalar1=-1.0e29, scalar2=None, op0=ALU.max),
                 reads=["stlo"], writes=["stthr"])
            for kt_ in range(ntl):
                c0 = kt_ * 512
                w_ = min(512, nkeys - c0)
                nb_ = w_ // 128
                mi = kt_ % 2
                p.op("dve", lambda e, mi=mi, c0=c0, w_=w_: e.tensor_scalar(out=mk[mi][:, 0:w_], in0=score[:, c0:c0 + w_],
                                                                           scalar1=st[:, 7:8], scalar2=None, op0=ALU.is_ge),
                     reads=["score", "stthr"], writes=[f"mk{mi}"])
                for bb in range(nb_):
                    p.op("pe", lambda e, mi=mi, bb=bb: e.matmul(psT[:, bb * 128:(bb + 1) * 128], lhsT=mk[mi][:, bb * 128:(bb + 1) * 128],
                                                                rhs=idb[:, :], start=True, stop=True),
                         reads=[f"mk{mi}", "idb"], writes=["psT"])
                p.op("act", lambda e, kt_=kt_, nb_=nb_, j=j, w_=w_, mTv=mTv: e.activation(
                    out=mTv[:, kt_ * 4:kt_ * 4 + nb_, j * 128:(j + 1) * 128],
                    in_=psT[:, 0:w_].rearrange("p (k t) -> p k t", t=128), func=AF.Copy),
                    reads=["psT"], writes=["mT"])
        for h in range(NH):
            p.dma("sp", k_sb[:, 0:nkb * 128], A_["akT"][h, :, 0:nkb * 128], writes=["k_sb"])
            p.dma("sp", v_sb[:, 0:nkb, :], A_["avh"][h, :, 0:nkb, :], writes=["v_sb"])
            def s1(kb, h=h, N=N, nkb=nkb, mTv=mTv, nq=nq, g8=g8):
                si = kb % 3
                p.op("pe", lambda e: e.matmul(
                    psS[si][:, 0:N], lhsT=k_sb[:, kb * 128:(kb + 1) * 128], rhs=aq_sb[:, h, 0:N], start=True, stop=True),
                    reads=["k_sb", "aq"], writes=[f"psS{si}"])
                for j in range(nq):
                    which = kb - (g8 + 2 * j - 1)
                    if 0 <= which <= 2:
                        p.op("dve", lambda e, j=j, which=which: e.tensor_tensor(
                            out=psS[si][:, j * 128:(j + 1) * 128], in0=psS[si][:, j * 128:(j + 1) * 128],
                            in1=bt[:, which, h, :], op=ALU.add), reads=[f"psS{si}", "bt"], writes=[f"psS{si}"])
                p.op("act", lambda e: e.activation(out=PT[si][:, 0:N], in_=psS[si][:, 0:N], func=AF.Exp, scale=0.125),
                     reads=[f"psS{si}"], writes=[f"PT{si}"])
                p.op("dve", lambda e: e.tensor_tensor(out=PT[si][:, 0:N], in0=PT[si][:, 0:N], in1=mTv[:, kb, :], op=ALU.mult),
                     reads=[f"PT{si}", "mT"], writes=[f"PT{si}"])

            def s2(kb, h=h, N=N, nkb=nkb):
                si = kb % 3
                p.op("pe", lambda e: e.matmul(
                    psO[0:64, 0:N], lhsT=v_sb[:, kb, :], rhs=PT[si][:, 0:N], start=(kb == 0), stop=(kb == nkb - 1)),
                    reads=["v_sb", f"PT{si}"], writes=["psO"])
                p.op("pe", lambda e: e.matmul(
                    psD[0:64, 0:N], lhsT=onesb[:, :], rhs=PT[si][:, 0:N], start=(kb == 0), stop=(kb == nkb - 1)),
                    reads=["onesb", f"PT{si}"], writes=["psD"])

            for kb in range(nkb + 2):
                if kb < nkb:
                    s1(kb)
                if kb - 2 >= 0:
                    s2(kb - 2)
            p.dma("sp", g_sb[:, 0:N].rearrange("d (j t) -> d j t", t=128), A_["agT"][h, :, jj0:jj0 + nq, :], writes=["g_sb"])
            p.op("dve", lambda e, N=N: e.tensor_scalar(out=rd[:, 0:N], in0=psD[0:64, 0:N], scalar1=1e-30, scalar2=None,
                                                       op0=ALU.max), reads=["psD"], writes=["rd"])
            p.op("dve", lambda e, N=N: e.reciprocal(out=rd[:, 0:N], in_=rd[:, 0:N]), reads=["rd"], writes=["rd"])
            p.op("act", lambda e, N=N: e.activation(out=sg[:, 0:N], in_=g_sb[:, 0:N], func=AF.Exp, scale=-1.0),
                 reads=["g_sb"], writes=["sg"])
            p.op("dve", lambda e, N=N: e.tensor_scalar(out=sg[:, 0:N], in0=sg[:, 0:N], scalar1=1.0, scalar2=None, op0=ALU.add),
                 reads=["sg"], writes=["sg"])
            p.op("dve", lambda e, N=N: e.reciprocal(out=sg[:, 0:N], in_=sg[:, 0:N]), reads=["sg"], writes=["sg"])
            p.op("pool", lambda e, N=N: e.tensor_tensor(out=sg[:, 0:N], in0=sg[:, 0:N], in1=g_sb[:, 0:N], op=ALU.mult),
                 reads=["sg", "g_sb"], writes=["sg"])
            p.op("pool", lambda e, N=N: e.tensor_tensor(out=sg[:, 0:N], in0=sg[:, 0:N], in1=rd[:, 0:N], op=ALU.mult),
                 reads=["sg", "rd"], writes=["sg"])
            O, ok_ = ob[h % 2], f"ob{h % 2}"
            p.op("dve", lambda e, N=N, O=O: e.tensor_tensor(out=O[:, 0:N], in0=psO[0:64, 0:N], in1=sg[:, 0:N], op=ALU.mult),
                 reads=["psO", "sg"], writes=[ok_])
            p.dma("pool", A_["oT"][h, :, jj0:jj0 + nq, :], O[:, 0:N].rearrange("d (j t) -> d j t", t=128), reads=[ok_], skey=ok_)
    if own:
        p.finish()
        return p.emit()


def rel_bucket_np(rel):
    n = -rel
    ret = np.where(n < 0, 16, 0)
    n = np.abs(n)
    nf = np.maximum(n, 1).astype(np.float32)
    large = 8 + (np.log(nf / np.float32(8)) / np.float32(np.log(16.0)) * np.float32(8)).astype(np.int32)
    large = np.minimum(large, 15)
    return ret + np.where(n < 8, n, large)


def a_consts(r, rel_bias):
    s = np.arange(128)
    t = np.arange(128)
    diag = np.where((s[None, :] < 64) | (t[:, None] >= 64), 0.0, NEG)
    if r == 0:
        mA, mB = diag, np.full((128, 128), NEG)
    else:
        mA, mB = np.zeros((128, 128)), diag
    padm = np.where(s[None, :] >= PADF, 0.0, NEG) * np.ones((128, 1))
    mskd = np.stack([mA, mB, padm], axis=1).astype(np.float32)
    btd = np.zeros((128, 3, 8, 128), np.float32)
    for which in range(3):
        dblk = (r + 1 - which)
        rel = (s[:, None] - (t[None, :] + dblk * 128)).astype(np.int32)
        bk = rel_bucket_np(rel)
        btd[:, which, :, :] = np.transpose(rel_bias[bk], (0, 2, 1))
    b15d = np.broadcast_to(rel_bias[15][None, :], (128, 8)).astype(np.float32).copy()
    cstd = np.zeros((128, 160), np.float32)
    cstd[:, 0:128] = np.eye(128)
    cstd[:, 128:128 + 26] = (2.0 ** -(np.arange(26) + 1.0))[None, :]
    return mskd, btd, b15d, cstd


def a_pack(aqT, agT, iqT, iw, akT, av, ikT, r, rel_bias):
    blks = np.arange(NQB) * 2 + r
    cols = (blks[:, None] * 128 + np.arange(128)[None, :]).reshape(-1)
    mskd, btd, b15d, cstd = a_consts(r, rel_bias)
    return {
        "aqT": np.ascontiguousarray(aqT[:, :, cols]),
        "agT": np.ascontiguousarray(agT[:, :, cols]),
        "iqT": np.ascontiguousarray(iqT[:, :, cols]),
        "iw": np.ascontiguousarray(iw[cols].reshape(NQB, 128, 8).transpose(1, 0, 2)),
        "akT": np.ascontiguousarray(akT),
        "avh": np.ascontiguousarray(av.reshape(NBLK, 128, 8, 64).transpose(2, 1, 0, 3)),
        "ikT": np.ascontiguousarray(ikT),
        "mskd": mskd, "btd": btd, "b15d": b15d, "cstd": cstd,
    }


T_CORE = PPAD // 2

AB_FB = ([(0 + i * 128, 128) for i in range(4)] + [(512 + i * 128, 128) for i in range(4)]
         + [(2048 + i * 128, 128) for i in range(4)] + [(2560, 64)]
         + [(2632, 128), (2760, 128), (2888, 128), (3016, 128)])
AB_FF = [(1536 + i * 128, 128) for i in range(4)] + [(3656 + i * 128, 128) for i in range(4)] + [(4168, 16)]
AB_TB = [(1024, 512), (3144, 512)]
AB_TF = [(2624, 8)]
W_CDX = W_CD + 512
CD_FB = ([(0, 128), (128, 128), (256, 128), (384, 128), (3584, 128), (3712, 128), (3840, 128), (3968, 128)]
         + [(1536 + i * 128, 128) for i in range(4)] + [(2048 + i * 128, 128) for i in range(4)])
CD_FF = [(1024 + i * 128, 128) for i in range(4)] + [(3072 + i * 128, 128) for i in range(4)]
CD_TB = [(512, 512), (2560, 512)]
CD_TF = []

_PROGS = {}


def _prog(name):
    if name not in _PROGS:
        if name == "P_AB0":
            _PROGS[name] = build_P(T_CORE, W_AB, AB_FB, AB_FF, AB_TB, AB_TF, with_out=False)
        elif name == "P_AB":
            _PROGS[name] = build_P(T_CORE, W_AB, AB_FB, AB_FF, AB_TB, AB_TF, with_out=True)
        elif name == "P_CD":
            _PROGS[name] = build_P(T_CORE, W_CDX, CD_FB, CD_FF, CD_TB, CD_TF, with_out=True)
        elif name == "F":
            _PROGS[name] = build_P(T_CORE, 0, [], [], [], [], with_out=True, final=True)
        elif name == "A":
            _PROGS[name] = build_A()
        elif name == "D":
            _PROGS[name] = build_D()
        elif name == "LG":
            _PROGS[name] = build_L("gla")
        elif name == "LR":
            _PROGS[name] = build_L("ret")
    return _PROGS[name]


def _run(name, maps):
    res = run_bass_kernel_spmd(_prog(name), maps, core_ids=list(range(NCORES)))
    return res.results


def _g_layout(g):
    return np.ascontiguousarray(np.asarray(g, np.float32).reshape(8, 128).T)


def _seq(results, key, axis):
    return [np.concatenate([results[2 * b][key], results[2 * b + 1][key]], axis=axis) for b in range(BATCH)]


def _cd_weights(w):
    w = np.asarray(w, np.float32)
    d = np.arange(64)
    sw = (d + 32) % 64
    cq_sw = np.concatenate([h * 64 + sw for h in range(4)])
    ck_sw = 256 + cq_sw
    return np.ascontiguousarray(np.concatenate([w, w[:, cq_sw], w[:, ck_sw]], axis=1))


def kernel(x, meta_tokens, rel_bias, norm_g, final_g, w_in_ab, gla_gate_w2, gla_gate_b,
           w_out_ab, w_in_cd, w_out_cd):
    x = np.asarray(x, np.float32)
    rel_bias = np.asarray(rel_bias, np.float32)
    h = np.zeros((BATCH, PPAD, D_MODEL), np.float32)
    h[:, PADF:128] = np.asarray(meta_tokens, np.float32)[None]
    h[:, 128:128 + SEQ] = x
    hT = [np.ascontiguousarray(h[b].T) for b in range(BATCH)]
    del h
    mixT = None
    T = T_CORE
    dmasks, dut = d_consts()
    lcst = l_consts()
    for layer in range(4):
        j = layer // 2
        ab = (layer % 2 == 0)
        maps = []
        for c in range(NCORES):
            b, r = divmod(c, 2)
            m = {"hT": np.ascontiguousarray(hT[b][:, r * T:(r + 1) * T]), "g": _g_layout(norm_g[layer])}
            if ab:
                m["w"] = np.ascontiguousarray(np.asarray(w_in_ab[j], np.float32))
            else:
                m["w"] = _cd_weights(w_in_cd[j])
            if layer > 0:
                m["mixT"] = np.ascontiguousarray(mixT[b][:, r * T:(r + 1) * T])
                m["wo"] = np.ascontiguousarray(np.asarray(w_out_cd[j - 1] if ab else w_out_ab[j], np.float32))
            maps.append(m)
        pname = "P_AB0" if layer == 0 else ("P_AB" if ab else "P_CD")
        res = _run(pname, maps)
        del maps
        if layer > 0:
            hT = _seq(res, "hT_new", 1)
        of_bf = _seq(res, "of_bf", 1)
        of_f32 = _seq(res, "of_f32", 1)
        ot_bf = _seq(res, "ot_bf", 0)
        ot_f32 = _seq(res, "ot_f32", 0)
        del res
        import ml_dtypes
        mixT = [np.zeros((D_MODEL, PPAD), ml_dtypes.bfloat16) for _ in range(BATCH)]
        if ab:
            maps = []
            for c in range(NCORES):
                b, r = divmod(c, 2)
                fb, ff = of_bf[b], of_f32[b]
                maps.append(a_pack(fb[0:512].reshape(8, 64, PPAD), ff[0:512].reshape(8, 64, PPAD),
                                   fb[1024:1536].reshape(8, 64, PPAD), ot_f32[b], fb[512:1024].reshape(8, 64, PPAD),
                                   ot_bf[b][:, 0:512], fb[1536:1600], r, rel_bias))
            res = _run("A", maps)
            del maps
            for c in range(NCORES):
                b, r = divmod(c, 2)
                o = res[c]["oT"].reshape(512, NQB, 128)
                blks = np.arange(NQB) * 2 + r
                keep = blks < NBLK - (1 if r == 1 else 0) if False else blks < NBLK
                mv = mixT[b][0:512].reshape(512, NBLK, 128)
                mv[:, blks[keep], :] = o[:, keep, :]
            del res
            maps = []
            for c in range(NCORES):
                b, r = divmod(c, 2)
                fb, ff = of_bf[b], of_f32[b]
                w2a = np.concatenate([np.asarray(gla_gate_w2[j], np.float32)[:, r * 128:(r + 1) * 128],
                                      np.asarray(gla_gate_b[j], np.float32)[None, r * 128:(r + 1) * 128]], axis=0)
                maps.append({"qT": np.ascontiguousarray(fb[(13 + r) * 128:(14 + r) * 128]),
                             "kT": np.ascontiguousarray(fb[(15 + r) * 128:(16 + r) * 128]),
                             "v": np.ascontiguousarray(ot_bf[b][:, 512 + r * 256:512 + (r + 1) * 256]),
                             "gT": np.ascontiguousarray(ff[512 + r * 256:512 + (r + 1) * 256]),
                             "baT": np.ascontiguousarray(ff[1024:1040]), "w2a": np.ascontiguousarray(w2a), "cst": lcst})
            res = _run("LG", maps)
            del maps
            for c in range(NCORES):
                b, r = divmod(c, 2)
                mixT[b][512 + r * 256:512 + (r + 1) * 256] = res[c]["oT"]
            del res
        else:
            maps = []
            for c in range(NCORES):
                b, r = divmod(c, 2)
                fb, ff = of_bf[b], of_f32[b]
                tabs, decc = ret_tables(r)
                maps.append({"qT": np.ascontiguousarray(fb[(0 + r) * 128:(1 + r) * 128]),
                             "kT": np.ascontiguousarray(fb[(2 + r) * 128:(3 + r) * 128]),
                             "qsT": np.ascontiguousarray(fb[(4 + r) * 128:(5 + r) * 128]),
                             "ksT": np.ascontiguousarray(fb[(6 + r) * 128:(7 + r) * 128]),
                             "v": np.ascontiguousarray(ot_bf[b][:, r * 256:(r + 1) * 256]),
                             "gT": np.ascontiguousarray(ff[r * 256:(r + 1) * 256]),
                             "tabs": tabs, "decc": decc, "cst": lcst})
            res = _run("LR", maps)
            del maps
            for c in range(NCORES):
                b, r = divmod(c, 2)
                mixT[b][r * 256:(r + 1) * 256] = res[c]["oT"]
            del res
            maps = []
            for c in range(NCORES):
                b, r = divmod(c, 2)
                fb, ff = of_bf[b], of_f32[b]
                maps.append({"qT": np.ascontiguousarray(fb[8 * 128 + r * 256:8 * 128 + (r + 1) * 256].reshape(4, 64, PPAD)),
                             "kT": np.ascontiguousarray(fb[12 * 128 + r * 256:12 * 128 + (r + 1) * 256].reshape(4, 64, PPAD)),
                             "v": np.ascontiguousarray(ot_bf[b][:, 512 + r * 256:512 + (r + 1) * 256]),
                             "gT": np.ascontiguousarray(ff[512 + r * 256:512 + (r + 1) * 256].reshape(4, 64, PPAD)),
                             "masks": dmasks, "ut": dut})
            res = _run("D", maps)
            del maps
            for c in range(NCORES):
                b, r = divmod(c, 2)
                mixT[b][512 + r * 256:512 + (r + 1) * 256] = res[c]["oT"].reshape(256, PPAD)
            del res
        del of_bf, of_f32, ot_bf, ot_f32
    maps = []
    for c in range(NCORES):
        b, r = divmod(c, 2)
        maps.append({"hT": np.ascontiguousarray(hT[b][:, r * T:(r + 1) * T]), "g": _g_layout(final_g),
                     "mixT": np.ascontiguousarray(mixT[b][:, r * T:(r + 1) * T]),
                     "wo": np.ascontiguousarray(np.asarray(w_out_cd[1], np.float32))})
    res = _run("F", maps)
    yT = _seq(res, "y", 1)
    out = np.stack([np.ascontiguousarray(yT[b][:, 128:128 + SEQ].T) for b in range(BATCH)], axis=0)
    return out.astype(np.float32)


def build_fused():
    p = Prog()
    p.fused = True
    EI = "ExternalInput"
    hT0 = p.dram("hT0", [D_MODEL, PPAD], F32, kind=EI)
    gs = p.dram("gs", [5, 128, 8], F32, kind=EI)
    w_ab = p.dram("w_ab", [2, D_MODEL, W_AB], F32, kind=EI)
    w_cdx = p.dram("w_cdx", [2, D_MODEL, W_CDX], F32, kind=EI)
    wo_ab = p.dram("wo_ab", [2, D_MODEL, D_MODEL], F32, kind=EI)
    wo_cd = p.dram("wo_cd", [2, D_MODEL, D_MODEL], F32, kind=EI)
    w2 = p.dram("w2", [2, 16, 256], F32, kind=EI)
    gb = p.dram("gb", [2, 1, 256], F32, kind=EI)
    mskd = p.dram("mskd", [2, 128, 3, 128], F32, kind=EI)
    btd = p.dram("btd", [2, 128, 3, 8, 128], F32, kind=EI)
    b15d = p.dram("b15d", [128, 8], F32, kind=EI)
    cstd = p.dram("cstd", [128, 160], F32, kind=EI)
    lcst = p.dram("lcst", [128, 3, 256], F32, kind=EI)
    tabs = p.dram("tabs", [2, 4, 128, PPAD], F32, kind=EI)
    decc = p.dram("decc", [2, 64, 2], F32, kind=EI)
    dmasks = p.dram("dmasks", [6, 128, 512], BF16, kind=EI)
    dut = p.dram("dut", [128, 256], BF16, kind=EI)
    y = p.dram("y", [D_MODEL, PPAD], F32, kind="ExternalOutput")
    of_bf = p.dram("s_of_bf", [17 * 128, PPAD], BF16)
    of_f32 = p.dram("s_of_f32", [9 * 128, PPAD], F32)
    ot_bf = p.dram("s_ot_bf", [PPAD, 1024], BF16)
    ot_f32 = p.dram("s_ot_f32", [PPAD, 8], F32)
    avh = p.dram("s_avh", [8, 128, NBLK, 64], BF16)
    mixT = p.dram("s_mixT", [D_MODEL, PPAD], BF16)
    hA = p.dram("s_hA", [D_MODEL, PPAD], F32)
    hB = p.dram("s_hB", [D_MODEL, PPAD], F32)

    fb = of_bf.ap()
    ff = of_f32.ap()
    hd3 = lambda ap_: ap_.rearrange("(h d) t -> h d t", d=64)

    def qsel(ap_, r):
        return ap_.rearrange("(h d) (j r t) -> h d j r t", d=64, r=2, t=128)[:, :, :, r, :]

    h_cur = hT0.ap()
    h_bufs = [hA.ap(), hB.ap()]
    for layer in range(4):
        j = layer // 2
        ab = (layer % 2 == 0)
        p.phase(f"P{layer}")
        io = {"hT": h_cur, "g": gs.ap()[layer], "of_bf": fb, "of_f32": ff, "ot_bf": ot_bf.ap(), "ot_f32": ot_f32.ap()}
        io["w"] = w_ab.ap()[j] if ab else w_cdx.ap()[j]
        if layer > 0:
            io["mixT"] = mixT.ap()
            io["wo"] = wo_cd.ap()[j - 1] if ab else wo_ab.ap()[j]
            io["hT_new"] = h_bufs[(layer - 1) % 2]
        if ab:
            tdst = {1024: (lambda blk: avh.ap()[:, :, blk, :].rearrange("h s d -> s h d"))}
            build_P(PPAD, W_AB, AB_FB, AB_FF, AB_TB, AB_TF, with_out=(layer > 0), p=p, io=io, tdst=tdst)
        else:
            build_P(PPAD, W_CDX, CD_FB, CD_FF, CD_TB, CD_TF, with_out=True, p=p, io=io)
        if layer > 0:
            h_cur = h_bufs[(layer - 1) % 2]
        if ab:
            for r in range(2):
                p.phase(f"A{layer}{r}")
                build_A(p=p, io={
                    "aqT": qsel(fb[0:512], r), "agT": qsel(ff[0:512], r), "iqT": qsel(fb[1024:1536], r),
                    "iw": ot_f32.ap().rearrange("(j r t) c -> t j r c", r=2, t=128)[:, :, r, :],
                    "akT": hd3(fb[512:1024]), "avh": avh.ap(), "ikT": fb[1536:1600],
                    "mskd": mskd.ap()[r], "btd": btd.ap()[r], "b15d": b15d.ap(), "cstd": cstd.ap(),
                    "oT": qsel(mixT.ap()[0:512], r)})
            for r in range(2):
                p.phase(f"G{layer}{r}")
                build_L("gla", p=p, io={
                    "qT": fb[(13 + r) * 128:(14 + r) * 128], "kT": fb[(15 + r) * 128:(16 + r) * 128],
                    "v": ot_bf.ap()[:, 512 + r * 256:512 + (r + 1) * 256],
                    "gT": ff[512 + r * 256:512 + (r + 1) * 256], "baT": ff[1024:1040],
                    "w2a": (w2.ap()[j][:, r * 128:(r + 1) * 128], gb.ap()[j][:, r * 128:(r + 1) * 128]),
                    "cst": lcst.ap(), "oT": mixT.ap()[512 + r * 256:512 + (r + 1) * 256]})
        else:
            for r in range(2):
                p.phase(f"R{layer}{r}")
                build_L("ret", p=p, io={
                    "qT": fb[(0 + r) * 128:(1 + r) * 128], "kT": fb[(2 + r) * 128:(3 + r) * 128],
                    "qsT": fb[(4 + r) * 128:(5 + r) * 128], "ksT": fb[(6 + r) * 128:(7 + r) * 128],
                    "v": ot_bf.ap()[:, r * 256:(r + 1) * 256], "gT": ff[r * 256:(r + 1) * 256],
                    "tabs": tabs.ap()[r], "decc": decc.ap()[r], "cst": lcst.ap(),
                    "oT": mixT.ap()[r * 256:(r + 1) * 256]})
            for r in range(2):
                p.phase(f"D{layer}{r}")
                build_D(p=p, io={
                    "qT": hd3(fb[8 * 128 + r * 256:8 * 128 + (r + 1) * 256]),
                    "kT": hd3(fb[12 * 128 + r * 256:12 * 128 + (r + 1) * 256]),
                    "v": ot_bf.ap()[:, 512 + r * 256:512 + (r + 1) * 256],
                    "gT": hd3(ff[512 + r * 256:512 + (r + 1) * 256]),
                    "masks": dmasks.ap(), "ut": dut.ap(),
                    "oT": hd3(mixT.ap()[512 + r * 256:512 + (r + 1) * 256])})
    p.phase("F")
    build_P(PPAD, 0, [], [], [], [], with_out=True, final=True, p=p,
            io={"hT": h_cur, "g": gs.ap()[4], "mixT": mixT.ap(), "wo": wo_cd.ap()[1], "hT_new": None, "y": y.ap()})
    p.finish()
    return p.emit()


def kernel_unfused(**kw):
    return _kernel_unfused(**kw)


_kernel_unfused = kernel


def kernel(x, meta_tokens, rel_bias, norm_g, final_g, w_in_ab, gla_gate_w2, gla_gate_b,
           w_out_ab, w_in_cd, w_out_cd):
    f32 = lambda a: np.ascontiguousarray(np.asarray(a, np.float32))
    x = f32(x)
    rel_bias = f32(rel_bias)
    if "FUSED" not in _PROGS:
        _PROGS["FUSED"] = build_fused()
    nc = _PROGS["FUSED"]
    gs = np.stack([_g_layout(norm_g[l]) for l in range(4)] + [_g_layout(final_g)], axis=0)
    ac = [a_consts(r, rel_bias) for r in range(2)]
    rt = [ret_tables(r) for r in range(2)]
    dmasks, dut = d_consts()
    shared = {
        "gs": gs, "w_ab": f32(w_in_ab), "w_cdx": np.stack([_cd_weights(w_in_cd[j]) for j in range(2)], 0),
        "wo_ab": f32(w_out_ab), "wo_cd": f32(w_out_cd), "w2": f32(gla_gate_w2),
        "gb": f32(gla_gate_b).reshape(2, 1, 256),
        "mskd": np.stack([ac[0][0], ac[1][0]], 0), "btd": np.stack([ac[0][1], ac[1][1]], 0),
        "b15d": ac[0][2], "cstd": ac[0][3], "lcst": l_consts(),
        "tabs": np.stack([rt[0][0], rt[1][0]], 0), "decc": np.stack([rt[0][1], rt[1][1]], 0),
        "dmasks": dmasks, "dut": dut,
    }
    maps = []
    for c in range(NCORES):
        b = c // 2
        h = np.zeros((PPAD, D_MODEL), np.float32)
        h[PADF:128] = np.asarray(meta_tokens, np.float32)
        h[128:128 + SEQ] = x[b]
        m = dict(shared)
        m["hT0"] = np.ascontiguousarray(h.T)
        maps.append(m)
    res = run_bass_kernel_spmd(nc, maps, core_ids=list(range(NCORES))).results
    out = np.stack([np.ascontiguousarray(res[2 * b]["y"][:, 128:128 + SEQ].T) for b in range(BATCH)], axis=0)
    return out.astype(np.float32)
```

```python
import contextlib
import numpy as np
import concourse.bass as bass
import concourse.mybir as mybir
from concourse.bass_utils import run_bass_kernel_spmd

F32 = mybir.dt.float32
BF16 = mybir.dt.bfloat16
ALU = mybir.AluOpType
AF = mybir.ActivationFunctionType
AX = mybir.AxisListType

D_MODEL = 1024
BATCH = 4
SEQ = 8192
PADF = 112
NMETA = 16
PTOK = SEQ + 128
NBLK = 66
PPAD = NBLK * 128
NCORES = 8
W_AB = 4184
W_CD = 3584
TOPK = 256
NEG = -1.0e30
LDBG = 9
EPS = 1e-6
SEM_ROLL = 30000

ENGS = ("pe", "act", "dve", "pool", "sp")


class Prog:
    def __init__(self, num_devices=None):
        if num_devices:
            self.nc = bass.Bass("TRN2", target_bir_lowering=False, num_devices=num_devices)
        else:
            self.nc = bass.Bass("TRN2", target_bir_lowering=False)
        self.q = {e: [] for e in ENGS}
        self.cnt = {e: 0 for e in ENGS}
        self.esem = {e: None for e in ENGS}
        self.keys = {}
        self.waited = {e: {} for e in ENGS}
        self.dsem = {}
        self.all_dma = []
        self.nsem = 0
        self.allsems = []
        self.fused = False
        self.prefix = ""
        self.sb_off = 16640
        self.banks = None
        self.bank_i = 0

    def new_sem(self, name):
        self.nsem += 1
        s = self.nc.alloc_semaphore(f"{name}_{self.nsem}")
        self.allsems.append(s)
        return s

    def dram(self, name, shape, dtype, kind="Internal"):
        return self.nc.dram_tensor(name, list(shape), dtype, kind=kind)

    def sb(self, name, shape, dtype):
        if not self.fused:
            return self.nc.alloc_sbuf_tensor(name, list(shape), dtype)
        nbytes = int(np.prod(shape[1:])) * (2 if dtype is BF16 else 4)
        nbytes = (nbytes + 31) // 32 * 32
        off = self.sb_off
        self.sb_off += nbytes
        assert self.sb_off <= 229120, (name, self.sb_off)
        self.sb_max = max(getattr(self, 'sb_max', 0), self.sb_off)
        return self.nc.alloc_sbuf_tensor_at(self.prefix + name, list(shape), dtype, offset=off)

    def ps(self, name, shape, dtype=F32):
        if not self.fused:
            return self.nc.alloc_psum_tensor(name, list(shape), dtype)
        if self.banks is None:
            self.banks = [self.nc.alloc_psum_tensor(f"bank{i}", [128, 512], F32) for i in range(8)]
        b = self.banks[self.bank_i]
        self.bank_i += 1
        assert self.bank_i <= 8
        return b

    def phase(self, name):
        toks = []
        for e in ENGS:
            if self.esem[e] is not None and self.cnt[e] > 0:
                toks.append((self.esem[e], self.cnt[e]))
        for ent in self.dsem.values():
            toks.append((ent[0], ent[1]))
        for e in ENGS:
            waits = []
            for sem, val in toks:
                if e == "pe" and sem is self.esem["pe"]:
                    continue
                if self.waited[e].get(id(sem), 0) >= val:
                    continue
                self.waited[e][id(sem)] = val
                waits.append((sem, val))
            if waits:
                self.q[e].append((waits, None, None, 0))
        self.prefix = name + "_"
        self.sb_off = 16640
        self.bank_i = 0

    def _deps(self, eng, reads, writes):
        deps = []
        for k in reads:
            st = self.keys.get(k)
            if st and st["w"]:
                deps.append(st["w"])
        for k in writes:
            st = self.keys.get(k)
            if st:
                if st["w"]:
                    deps.append(st["w"])
                deps.extend(st["r"])
        best = {}
        for sem, val in deps:
            if eng == "pe" and sem is self.esem["pe"]:
                continue
            sid = id(sem)
            if sid not in best or best[sid][1] < val:
                best[sid] = (sem, val)
        out = []
        for sid, (sem, val) in best.items():
            if self.waited[eng].get(sid, 0) >= val:
                continue
            self.waited[eng][sid] = val
            out.append((sem, val))
        return out

    def _mark(self, reads, writes, tok):
        for k in reads:
            st = self.keys.setdefault(k, {"w": None, "r": []})
            st["r"].append(tok)
        for k in writes:
            self.keys[k] = {"w": tok, "r": []}

    def op(self, eng, fn, reads=(), writes=()):
        waits = self._deps(eng, reads, writes)
        if self.esem[eng] is None or self.cnt[eng] >= SEM_ROLL:
            self.esem[eng] = self.new_sem("e" + eng)
            self.cnt[eng] = 0
        self.cnt[eng] += 1
        tok = (self.esem[eng], self.cnt[eng])
        self.q[eng].append((waits, fn, tok[0], 1))
        self._mark(reads, writes, tok)

    def dma(self, eng, out_ap, in_ap, reads=(), writes=(), skey=None):
        waits = self._deps(eng, reads, writes)
        if skey is None:
            skey = (list(writes) + list(reads))[0]
        ent = self.dsem.get(skey)
        if ent is None:
            ent = [self.new_sem("d"), 0]
            self.dsem[skey] = ent
        ent[1] += 16
        tok = (ent[0], ent[1])
        self.q[eng].append((waits, lambda e: e.dma_start(out=out_ap, in_=in_ap), tok[0], 16))
        self._mark(reads, writes, tok)

    def finish(self):
        fin = [(ent[0], ent[1]) for ent in self.dsem.values()]
        self.q["pool"].append((fin, None, None, 0))
        self.q["sp"].append((fin, None, None, 0))

    def emit(self):
        nc = self.nc

        def run(e, lst):
            for waits, fn, sem, inc in lst:
                for s, v in waits:
                    e.wait_ge(s, v)
                if fn is not None:
                    fn(e).then_inc(sem, inc)

        with nc.Block() as block:
            @block.tensor
            def _(e):
                run(e, self.q["pe"])

            @block.scalar
            def _(e):
                run(e, self.q["act"])

            @block.vector
            def _(e):
                run(e, self.q["dve"])

            @block.gpsimd
            def _(e):
                run(e, self.q["pool"])

            @block.sync
            def _(e):
                run(e, self.q["sp"])
        return nc


def build_P(T, WT, fchunks_bf, fchunks_f32, tgroups_bf, tgroups_f32, with_out, final=False, p=None, io=None,
            tdst=None):
    own = p is None
    NT = 384
    assert T % NT == 0
    ntile = T // NT
    KC = 8
    tdst = tdst or {}
    if own:
        p = Prog()
        A = {}
        A["hT"] = p.dram("hT", [D_MODEL, T], F32, kind="ExternalInput").ap()
        A["g"] = p.dram("g", [128, KC], F32, kind="ExternalInput").ap()
        if not final:
            A["w"] = p.dram("w", [D_MODEL, WT], F32, kind="ExternalInput").ap()
        if with_out:
            A["mixT"] = p.dram("mixT", [D_MODEL, T], BF16, kind="ExternalInput").ap()
            A["wo"] = p.dram("wo", [D_MODEL, D_MODEL], F32, kind="ExternalInput").ap()
            A["hT_new"] = p.dram("hT_new", [D_MODEL, T], F32, kind="ExternalOutput").ap()
        if final:
            A["y"] = p.dram("y", [D_MODEL, T], F32, kind="ExternalOutput").ap()
        else:
            nfb, nff = len(fchunks_bf), len(fchunks_f32)
            ctb = sum(n for _, n in tgroups_bf)
            ctf = sum(n for _, n in tgroups_f32)
            A["of_bf"] = p.dram("of_bf", [max(nfb, 1) * 128, T], BF16, kind="ExternalOutput").ap()
            A["of_f32"] = p.dram("of_f32", [max(nff, 1) * 128, T], F32, kind="ExternalOutput").ap()
            A["ot_bf"] = p.dram("ot_bf", [T, max(ctb, 1)], BF16, kind="ExternalOutput").ap()
            A["ot_f32"] = p.dram("ot_f32", [T, max(ctf, 1)], F32, kind="ExternalOutput").ap()
    else:
        A = io
    of_bf, of_f32, ot_bf, ot_f32 = A.get("of_bf"), A.get("of_f32"), A.get("ot_bf"), A.get("ot_f32")

    g_sb = p.sb("g_sb", [128, KC], F32)
    ones = p.sb("ones", [128, 128], F32)
    if not final:
        w_sb = p.sb("w_sb", [128, KC, WT], BF16)
        stg = [p.sb(f"stg{i}", [128, KC, 512], F32) for i in range(2)]
    if with_out:
        wo_sb = p.sb("wo_sb", [128, KC, D_MODEL], BF16)
        mix_sb = [p.sb(f"mix{i}", [128, KC, NT], BF16) for i in range(2)]
        if final:
            stg = [p.sb(f"stg{i}", [128, KC, 512], F32) for i in range(2)]
    h_sb = [p.sb(f"h{i}", [128, KC, NT], F32) for i in range(2)]
    sq_sb = p.sb("sq", [128, NT], F32)
    rstd = p.sb("rstd", [128, NT], F32)
    hn_sb = [p.sb(f"hn{i}", [128, KC, NT], BF16) for i in range(2)]
    NEV = 4
    ev_bf = [p.sb(f"evb{i}", [128, 512], BF16) for i in range(NEV)]
    ev_f = [p.sb(f"evf{i}", [128, 512], F32) for i in range(NEV)]
    pst = [p.ps(f"ps{i}", [128, 512], F32) for i in range(6)]
    ps_ss = p.ps("ps_ss", [128, 512], F32)

    p.dma("sp", g_sb[:, :], A["g"], writes=["g_sb"])
    p.op("pool", lambda e: e.memset(ones[:, :], 1.0), writes=["ones"])

    def load_w(dram_w, sb_w, ncols, tag):
        view = dram_w.rearrange("(c p) w -> p c w", p=128)
        npc = (ncols + 511) // 512
        for i in range(npc):
            a = i * 512
            b = min(ncols, a + 512)
            s = stg[i % 2]
            sk = f"stg{i % 2}"
            p.dma("sp", s[:, :, 0:b - a], view[:, :, a:b], writes=[sk])
            eng = "dve" if i % 2 == 0 else "pool"
            p.op(eng, lambda e, s=s, a=a, b=b: e.tensor_copy(out=sb_w[:, :, a:b], in_=s[:, :, 0:b - a]),
                 reads=[sk], writes=[f"{tag}_{i}"])
        return [f"{tag}_{i}" for i in range(npc)]

    wkeys = []
    if with_out:
        wokeys = load_w(A["wo"], wo_sb, D_MODEL, "wo")
    if not final:
        wkeys = load_w(A["w"], w_sb, WT, "w")

    hview = A["hT"].rearrange("(c p) t -> p c t", p=128)
    if with_out:
        mview = A["mixT"].rearrange("(c p) t -> p c t", p=128)
        if not final:
            hnview = A["hT_new"].rearrange("(c p) t -> p c t", p=128)
    if final:
        yview = A["y"].rearrange("(c p) t -> p c t", p=128)

    evi = [0]
    psi = [0]

    def next_ps():
        i = psi[0] % len(pst)
        psi[0] += 1
        return pst[i], f"ps{i}"

    def evac(ps_ap, pkey, nparts, ncols, dtype, dram_ap):
        i = evi[0] % NEV
        evi[0] += 1
        if dtype is BF16:
            t, tk = ev_bf[i], f"evb{i}"
        else:
            t, tk = ev_f[i], f"evf{i}"
        if evi[0] % 2 == 0:
            p.op("act", lambda e: e.activation(out=t[0:nparts, 0:ncols], in_=ps_ap, func=AF.Copy),
                 reads=[pkey], writes=[tk])
        else:
            p.op("dve", lambda e: e.tensor_copy(out=t[0:nparts, 0:ncols], in_=ps_ap),
                 reads=[pkey], writes=[tk])
        src = t[0:nparts, 0:ncols]
        if len(dram_ap.shape) == 3:
            src = src.rearrange("p (h d) -> p h d", h=dram_ap.shape[1])
        p.dma("pool", dram_ap, src, reads=[tk], skey=tk)

    for ti in range(ntile):
        t0 = ti * NT
        hb, hk = h_sb[ti % 2], f"h{ti % 2}"
        hnb, hnk = hn_sb[ti % 2], f"hn{ti % 2}"
        p.dma("sp", hb[:, :, :], hview[:, :, t0:t0 + NT], writes=[hk])
        if with_out:
            mb, mk = mix_sb[ti % 2], f"mix{ti % 2}"
            p.dma("sp", mb[:, :, :], mview[:, :, t0:t0 + NT], writes=[mk])
            for oc in range(KC):
                ps, pk = next_ps()
                for c in range(KC):
                    p.op("pe", lambda e, ps=ps, c=c, oc=oc, mb=mb: e.matmul(
                        ps[:, 0:NT], lhsT=wo_sb[:, c, oc * 128:(oc + 1) * 128], rhs=mb[:, c, :],
                        start=(c == 0), stop=(c == KC - 1)),
                        reads=[mk] + wokeys, writes=[pk])
                p.op("dve", lambda e, ps=ps, oc=oc, hb=hb: e.tensor_tensor(
                    out=hb[:, oc, :], in0=hb[:, oc, :], in1=ps[:, 0:NT], op=ALU.add),
                    reads=[pk, hk], writes=[hk])
            if not final:
                p.dma("pool", hnview[:, :, t0:t0 + NT], hb[:, :, :], reads=[hk], skey=hk + "_st")
        for c in range(KC):
            p.op("act", lambda e, c=c, hb=hb: e.activation(out=sq_sb[:, :], in_=hb[:, c, :], func=AF.Square),
                 reads=[hk], writes=["sq"])
            p.op("pe", lambda e, c=c: e.matmul(ps_ss[:, 0:NT], lhsT=ones[:, :], rhs=sq_sb[:, :],
                                                 start=(c == 0), stop=(c == KC - 1)),
                 reads=["sq", "ones"], writes=["ps_ss"])
        p.op("act", lambda e: e.activation(out=rstd[:, :], in_=ps_ss[:, 0:NT], func=AF.Sqrt,
                                           bias=EPS, scale=1.0 / D_MODEL),
             reads=["ps_ss"], writes=["rstd"])
        p.op("dve", lambda e: e.reciprocal(out=rstd[:, :], in_=rstd[:, :]),
             reads=["rstd"], writes=["rstd"])
        if final:
            for c in range(KC):
                p.op("dve", lambda e, c=c, hb=hb: e.scalar_tensor_tensor(
                    out=hb[:, c, :], in0=hb[:, c, :], scalar=g_sb[:, c:c + 1], in1=rstd[:, :],
                    op0=ALU.mult, op1=ALU.mult), reads=[hk, "rstd", "g_sb"], writes=[hk])
            p.dma("pool", yview[:, :, t0:t0 + NT], hb[:, :, :], reads=[hk], skey=hk + "_st")
            continue
        for c in range(KC):
            p.op("dve", lambda e, c=c, hb=hb, hnb=hnb: e.scalar_tensor_tensor(
                out=hnb[:, c, :], in0=hb[:, c, :], scalar=g_sb[:, c:c + 1], in1=rstd[:, :],
                op0=ALU.mult, op1=ALU.mult), reads=[hk, "rstd", "g_sb"], writes=[hnk])
        for lst, dt_, dram_o in ((fchunks_bf, BF16, of_bf), (fchunks_f32, F32, of_f32)):
            for ci, (c0, ncol) in enumerate(lst):
                ps, pk = next_ps()
                for c in range(KC):
                    p.op("pe", lambda e, ps=ps, c=c, c0=c0, ncol=ncol, hnb=hnb: e.matmul(
                        ps[0:ncol, 0:NT], lhsT=w_sb[:, c, c0:c0 + ncol], rhs=hnb[:, c, :],
                        start=(c == 0), stop=(c == KC - 1)), reads=[hnk] + wkeys, writes=[pk])
                evac(ps[0:ncol, 0:NT], pk, ncol, NT, dt_, dram_o[ci * 128:ci * 128 + ncol, t0:t0 + NT])
        for blk in range(NT // 128):
            for lst, dt_, dram_o in ((tgroups_bf, BF16, ot_bf), (tgroups_f32, F32, ot_f32)):
                oc0 = 0
                for (c0, ncol) in lst:
                    ps, pk = next_ps()
                    for c in range(KC):
                        p.op("pe", lambda e, ps=ps, c=c, c0=c0, ncol=ncol, hnb=hnb, blk=blk: e.matmul(
                            ps[:, 0:ncol], lhsT=hnb[:, c, blk * 128:(blk + 1) * 128], rhs=w_sb[:, c, c0:c0 + ncol],
                            start=(c == 0), stop=(c == KC - 1)), reads=[hnk] + wkeys, writes=[pk])
                    if (dt_ is BF16) and (c0 in tdst):
                        dst_ap = tdst[c0](ti * (NT // 128) + blk)
                    else:
                        dst_ap = dram_o[t0 + blk * 128:t0 + (blk + 1) * 128, oc0:oc0 + ncol]
                    evac(ps[:, 0:ncol], pk, 128, ncol, dt_, dst_ap)
                    oc0 += ncol
    if own:
        p.finish()
        return p.emit()


def sb_list():
    sbs = []
    b = 0
    while b < NBLK:
        nb = min(4, NBLK - b)
        sbs.append((b, nb))
        b += nb
    return sbs


def build_D(p=None, io=None):
    own = p is None
    NH = 4
    if own:
        p = Prog()
        qT = p.dram("qT", [NH, 64, PPAD], BF16, kind="ExternalInput")
        kT = p.dram("kT", [NH, 64, PPAD], BF16, kind="ExternalInput")
        v = p.dram("v", [PPAD, NH * 64], BF16, kind="ExternalInput")
        gT = p.dram("gT", [NH, 64, PPAD], F32, kind="ExternalInput")
        masks = p.dram("masks", [6, 128, 512], BF16, kind="ExternalInput")
        ut = p.dram("ut", [128, 256], BF16, kind="ExternalInput")
        oT = p.dram("oT", [NH, 64, PPAD], BF16, kind="ExternalOutput")
        A_ = {"qT": qT.ap(), "kT": kT.ap(), "v": v.ap(), "gT": gT.ap(), "masks": masks.ap(), "ut": ut.ap(),
              "oT": oT.ap()}
    else:
        A_ = io

    k_sb = p.sb("k_sb", [64, NH, PPAD], BF16)
    v_sb = p.sb("v_sb", [128, NBLK, NH * 64], BF16)
    m_sb = p.sb("m_sb", [128, 6, 512], BF16)
    u_sb = p.sb("u_sb", [128, 256], BF16)
    q_sb = [p.sb(f"q{i}", [64, NH, 512], BF16) for i in range(2)]
    g_sb = [p.sb(f"g{i}", [64, 512], F32) for i in range(2)]
    e1 = [p.sb(f"e1_{i}", [128, 512], F32) for i in range(2)]
    DEP = 3
    NSP = DEP + 2
    NA = DEP + 3
    sp = [p.sb(f"sp{i}", [128, 512], BF16) for i in range(NSP)]
    NW = 3
    wt = [p.sb(f"wt{i}", [128, 512], BF16) for i in range(NW)]
    acc = p.sb("acc", [128, 512], BF16)
    sg = p.sb("sg", [64, 512], F32)
    ob = [p.sb(f"ob{i}", [64, 512], BF16) for i in range(2)]
    psA = [p.ps(f"psA{i}", [128, 512]) for i in range(NA)]
    psC = [p.ps(f"psC{i}", [128, 512]) for i in range(2)]

    for h in range(NH):
        p.dma("sp", k_sb[:, h, :], A_["kT"][h], writes=[f"k{h}"])
    p.dma("sp", v_sb[:, :, :], A_["v"].rearrange("(b p) f -> p b f", p=128), writes=["v_sb"])
    p.dma("sp", m_sb[:, :, :], A_["masks"].rearrange("m p t -> p m t"), writes=["m_sb"])
    p.dma("sp", u_sb[:, :], A_["ut"], writes=["u_sb"])

    sbs = sb_list()
    its = []
    for si, (b0, nb) in enumerate(sbs):
        for h in range(NH):
            kend = b0 + nb - 1
            for kb in range(kend, -1, -1):
                its.append((si, b0, nb, h, kb, kend))
    n_it = len(its)
    cnt = {"ph": -1}

    def mask_idx(b0, kb):
        if kb >= b0:
            m = kb - b0
            if kb == 0:
                return 5
            return m
        if kb == 0:
            return 4
        return None

    def stage1(i):
        si, b0, nb, h, kb, kend = its[i]
        N = nb * 128
        qb, qk = q_sb[si % 2], f"q{si % 2}"
        if h == 0 and kb == kend:
            p.dma("sp", qb[:, :, 0:N], A_["qT"][:, :, b0 * 128:b0 * 128 + N].rearrange("h d t -> d h t"),
                  writes=[qk])
        A, ak = psA[i % NA], f"psA{i % NA}"
        p.op("pe", lambda e: e.matmul(A[:, 0:N], lhsT=k_sb[:, h, kb * 128:(kb + 1) * 128], rhs=qb[:, h, 0:N],
                                      start=True, stop=False), reads=[f"k{h}", qk], writes=[ak])
        E, ek = e1[i % 2], f"e1_{i % 2}"
        p.op("act", lambda e: e.activation(out=E[:, 0:N], in_=A[:, 0:N], func=AF.Exp, scale=0.125),
             reads=[ak], writes=[ek])
        S, sk = sp[i % NSP], f"sp{i % NSP}"
        p.op("act", lambda e: e.activation(out=S[:, 0:N], in_=E[:, 0:N], func=AF.Ln, bias=1.0, scale=1.0),
             reads=[ek], writes=[sk])
        mi = mask_idx(b0, kb)
        if mi is not None:
            p.op("pool", lambda e: e.tensor_tensor(out=S[:, 0:N], in0=S[:, 0:N], in1=m_sb[:, mi, 0:N], op=ALU.mult),
                 reads=[sk, "m_sb"], writes=[sk])

    def stage2(i):
        si, b0, nb, h, kb, kend = its[i]
        N = nb * 128
        qb, qk = q_sb[si % 2], f"q{si % 2}"
        S, sk = sp[i % NSP], f"sp{i % NSP}"
        B, bk = psA[i % NA], f"psA{i % NA}"
        hi = si * NH + h
        C, ck = psC[hi % 2], f"psC{hi % 2}"
        first = (kb == kend)
        p.op("pe", lambda e: e.matmul(B[:, 0:N], lhsT=u_sb[:, 0:128], rhs=S[:, 0:N],
                                      start=False, stop=first), reads=[sk, "u_sb"], writes=[bk])
        if not first:
            p.op("pe", lambda e: e.matmul(B[:, 0:N], lhsT=u_sb[:, 128:256], rhs=acc[:, 0:N],
                                          start=False, stop=True), reads=["acc", "u_sb"], writes=[bk])
        W, wk = wt[i % NW], f"wt{i % NW}"
        p.op("act", lambda e: e.activation(out=W[:, 0:N], in_=B[:, 0:N], func=AF.Exp, scale=0.125),
             reads=[bk], writes=[wk])
        mi = mask_idx(b0, kb)
        if mi is not None:
            p.op("pool", lambda e: e.tensor_tensor(out=W[:, 0:N], in0=W[:, 0:N], in1=m_sb[:, mi, 0:N], op=ALU.mult),
                 reads=[wk, "m_sb"], writes=[wk])
        if first:
            p.op("dve", lambda e: e.tensor_copy(out=acc[:, 0:N], in_=S[:, 0:N]), reads=[sk], writes=["acc"])
        elif kb > 0:
            p.op("dve", lambda e: e.tensor_tensor(out=acc[:, 0:N], in0=acc[:, 0:N], in1=S[:, 0:N], op=ALU.add),
                 reads=[sk, "acc"], writes=["acc"])

    def stage2b(i):
        si, b0, nb, h, kb, kend = its[i]
        N = nb * 128
        hi = si * NH + h
        C, ck = psC[hi % 2], f"psC{hi % 2}"
        first = (kb == kend)
        W, wk = wt[i % NW], f"wt{i % NW}"
        p.op("pe", lambda e: e.matmul(C[0:64, 0:N], lhsT=v_sb[:, kb, h * 64:(h + 1) * 64], rhs=W[:, 0:N],
                                      start=first, stop=(kb == 0)), reads=[wk, "v_sb"], writes=[ck])
        if kb == 0:
            G, gk = g_sb[hi % 2], f"g{hi % 2}"
            p.dma("sp", G[:, 0:N], A_["gT"][h, :, b0 * 128:b0 * 128 + N], writes=[gk])
            p.op("act", lambda e: e.activation(out=sg[:, 0:N], in_=G[:, 0:N], func=AF.Exp, scale=-1.0),
                 reads=[gk], writes=["sg"])
            p.op("dve", lambda e: e.tensor_scalar(out=sg[:, 0:N], in0=sg[:, 0:N], scalar1=1.0, scalar2=None,
                                                  op0=ALU.add), reads=["sg"], writes=["sg"])
            p.op("dve", lambda e: e.reciprocal(out=sg[:, 0:N], in_=sg[:, 0:N]), reads=["sg"], writes=["sg"])
            p.op("dve", lambda e: e.tensor_tensor(out=sg[:, 0:N], in0=sg[:, 0:N], in1=G[:, 0:N], op=ALU.mult),
                 reads=["sg", gk], writes=["sg"])
            O, ok_ = ob[hi % 2], f"ob{hi % 2}"
            p.op("dve", lambda e: e.tensor_tensor(out=O[:, 0:N], in0=C[0:64, 0:N], in1=sg[:, 0:N], op=ALU.mult),
                 reads=[ck, "sg"], writes=[ok_])
            p.dma("pool", A_["oT"][h, :, b0 * 128:b0 * 128 + N], O[:, 0:N], reads=[ok_], skey=ok_)

    for i in range(n_it + DEP + 1):
        if i < n_it:
            stage1(i)
        if 0 <= i - DEP < n_it:
            stage2(i - DEP)
        if 0 <= i - DEP - 1 < n_it:
            stage2b(i - DEP - 1)
    if own:
        p.finish()
        return p.emit()


def d_consts():
    import ml_dtypes
    masks = np.zeros((6, 128, 512), np.float32)
    s = np.arange(128)[:, None]
    t = np.arange(512)[None, :]
    for m in range(4):
        j = t // 128
        tl = t % 128
        masks[m] = np.where(j > m, 1.0, np.where(j == m, (tl > s).astype(np.float32), 0.0))
    masks[4] = (s >= PADF).astype(np.float32) * np.ones((128, 512), np.float32)
    masks[5] = masks[0] * masks[4]
    ut = np.zeros((128, 256), np.float32)
    jj = np.arange(128)[:, None]
    ss = np.arange(128)[None, :]
    ut[:, 0:128] = np.where(jj >= ss, -8.0, 0.0)
    ut[:, 128:256] = -8.0
    return masks.astype(ml_dtypes.bfloat16), ut.astype(ml_dtypes.bfloat16)


def build_L(kind, p=None, io=None):
    own = p is None
    gla = (kind == "gla")
    if own:
        p = Prog()
        A_ = {}
        A_["qT"] = p.dram("qT", [128, PPAD], BF16, kind="ExternalInput").ap()
        A_["kT"] = p.dram("kT", [128, PPAD], BF16, kind="ExternalInput").ap()
        A_["v"] = p.dram("v", [PPAD, 256], BF16, kind="ExternalInput").ap()
        A_["gT"] = p.dram("gT", [256, PPAD], F32, kind="ExternalInput").ap()
        A_["cst"] = p.dram("cst", [128, 3, 256], F32, kind="ExternalInput").ap()
        if gla:
            A_["baT"] = p.dram("baT", [16, PPAD], F32, kind="ExternalInput").ap()
            A_["w2a"] = p.dram("w2a", [17, 128], F32, kind="ExternalInput").ap()
        else:
            A_["qsT"] = p.dram("qsT", [128, PPAD], BF16, kind="ExternalInput").ap()
            A_["ksT"] = p.dram("ksT", [128, PPAD], BF16, kind="ExternalInput").ap()
            A_["tabs"] = p.dram("tabs", [4, 128, PPAD], F32, kind="ExternalInput").ap()
            A_["decc"] = p.dram("decc", [64, 2], F32, kind="ExternalInput").ap()
        A_["oT"] = p.dram("oT", [256, PPAD], BF16, kind="ExternalOutput").ap()
    else:
        A_ = io

    cst_sb = p.sb("cst_sb", [128, 3, 256], F32)
    idb = p.sb("idb", [64, 64], BF16)
    m4 = p.sb("m4", [64, 256], BF16)
    ones = p.sb("ones", [128, 128], F32)
    S = p.sb("S", [64, 2, 128], F32)
    Sb = p.sb("Sb", [64, 2, 128], BF16)
    SbX = [Sb, p.sb("Sb1", [64, 2, 128], BF16)]
    kvd = p.sb("kvd", [64, 4, 128], F32)
    NB2 = 2
    NBI = 4
    q_in = [p.sb(f"q_in{i}", [64, 2, 128], BF16) for i in range(NBI)]
    k_in = [p.sb(f"k_in{i}", [64, 2, 128], BF16) for i in range(NBI)]
    v_in = [p.sb(f"v_in{i}", [64, 2, 256], BF16) for i in range(NBI)]
    g_in = [p.sb(f"g_in{i}", [128, 2, 128], F32) for i in range(NBI)]
    qt = [p.sb(f"qt{i}", [64, 2, 128], BF16) for i in range(NB2)]
    kt = [p.sb(f"kt{i}", [64, 2, 128], BF16) for i in range(NB2)]
    ktt = [p.sb(f"ktt{i}", [64, 256], BF16) for i in range(NB2)]
    dec = [p.sb(f"dec{i}", [64, 4], F32) for i in range(NB2)]
    attm = [p.sb(f"attm{i}", [64, 256], BF16) for i in range(NB2)]
    if gla:
        ba_in = [p.sb(f"ba_in{i}", [17, 128], F32) for i in range(NBI)]
        w2_sb = p.sb("w2_sb", [17, 128], F32)
        la = p.sb("la", [128, 128], F32)
        ep = p.sb("ep", [64, 256], F32)
        em = p.sb("em", [64, 256], F32)
    else:
        qs_in = [p.sb(f"qs_in{i}", [64, 2, 128], BF16) for i in range(NBI)]
        ks_in = [p.sb(f"ks_in{i}", [64, 2, 128], BF16) for i in range(NBI)]
        tb_in = [p.sb(f"tb_in{i}", [64, 4, 2, 128], F32) for i in range(NBI)]
        t1 = p.sb("t1", [64, 256], F32)
        t2 = p.sb("t2", [64, 256], F32)
        dec_c = p.sb("dec_c", [64, 2], F32)
    ofs = [p.sb(f"of{i}", [128, 256], F32) for i in range(2)]
    osq = p.sb("osq", [128, 256], F32)
    rs = p.sb("rs", [128, 256], F32)
    sg = p.sb("sg", [128, 256], F32)
    yb = [p.sb(f"yb{i}", [128, 2, 128], BF16) for i in range(2)]
    ln8t = p.sb("ln8t", [128, 1], F32)
    epst = p.sb("epst", [128, 1], F32)

    ps_x = p.ps("ps_x", [128, 512])
    ps_c = p.ps("ps_c", [128, 512])
    ps_t = p.ps("ps_t", [128, 512])
    ps_a = p.ps("ps_a", [128, 512])
    ps_o = p.ps("ps_o", [128, 512])
    ps_kv = p.ps("ps_kv", [128, 512])
    ps_n = p.ps("ps_n", [128, 512])

    p.dma("sp", cst_sb[:, :, :], A_["cst"], writes=["cst"])
    p.op("dve", lambda e: e.tensor_copy(out=m4[:, :], in_=cst_sb[0:64, 1, :]), reads=["cst"], writes=["m4"])
    p.op("dve", lambda e: e.tensor_copy(out=idb[:, :], in_=cst_sb[0:64, 2, 0:64]), reads=["cst"], writes=["idb"])
    p.op("pool", lambda e: e.memset(ones[:, :], 1.0 / 128.0), writes=["ones"])
    p.op("pool", lambda e: e.memset(S[:, :, :], 0.0), writes=["S0", "S1"])
    p.op("pool", lambda e: e.memset(SbX[0][:, :, :], 0.0), writes=["Sb0"])
    p.op("pool", lambda e: e.memset(SbX[1][:, :, :], 0.0), writes=["Sb1"])
    p.op("pool", lambda e: e.memset(ln8t[:, :], float(np.log(0.125))), writes=["ln8"])
    p.op("pool", lambda e: e.memset(epst[:, :], EPS), writes=["epsn"])
    if gla:
        if isinstance(A_["w2a"], tuple):
            p.dma("sp", w2_sb[0:16, :], A_["w2a"][0], writes=["w2"])
            p.dma("sp", w2_sb[16:17, :], A_["w2a"][1], writes=["w2"])
        else:
            p.dma("sp", w2_sb[:, :], A_["w2a"], writes=["w2"])
        for i in range(NBI):
            p.op("pool", lambda e, i=i: e.memset(ba_in[i][:, :], 1.0), writes=[f"ba_in{i}"])
    else:
        p.dma("sp", dec_c[:, :], A_["decc"], writes=["dec_c"])

    gview = A_["gT"].rearrange("(h e) t -> e h t", h=2)
    oview = A_["oT"].rearrange("(h e) t -> e h t", h=2)
    hd = lambda ap_: ap_.rearrange("(h d) t -> d h t", h=2)

    def load(b):
        i = b % NBI
        c0 = b * 128
        p.dma("sp", q_in[i][:, :, :], hd(A_["qT"])[:, :, c0:c0 + 128], writes=[f"q_in{i}"])
        p.dma("sp", k_in[i][:, :, :], hd(A_["kT"])[:, :, c0:c0 + 128], writes=[f"k_in{i}"])
        p.dma("sp", v_in[i][:, :, :], A_["v"][c0:c0 + 128, :].rearrange("(n s) f -> s n f", n=2), writes=[f"v_in{i}"])
        p.dma("sp", g_in[i][:, :, :], gview[:, :, c0:c0 + 128], writes=[f"g_in{i}"])
        if gla:
            p.dma("sp", ba_in[i][0:16, :], A_["baT"][:, c0:c0 + 128], writes=[f"ba_in{i}"])
        else:
            p.dma("sp", qs_in[i][:, :, :], hd(A_["qsT"])[:, :, c0:c0 + 128], writes=[f"qs_in{i}"])
            p.dma("sp", ks_in[i][:, :, :], hd(A_["ksT"])[:, :, c0:c0 + 128], writes=[f"ks_in{i}"])
            for f in range(4):
                p.dma("sp", tb_in[i][:, f, :, :], hd(A_["tabs"][f])[:, :, c0:c0 + 128], writes=[f"tb_in{i}"])

    fl = lambda t_: t_.rearrange("p h t -> p (h t)")

    def prep(b):
        i = b % NB2
        ii = b % NBI
        if gla:
            p.op("pe", lambda e: e.matmul(ps_x[:, 0:128], lhsT=ba_in[ii][:, :], rhs=w2_sb[:, :], start=True, stop=True),
                 reads=[f"ba_in{ii}", "w2"], writes=["ps_x"])
            p.op("act", lambda e: e.activation(out=la[:, :], in_=ps_x[:, 0:128], func=AF.Exp, scale=-1.0),
                 reads=["ps_x"], writes=["la"])
            p.op("act", lambda e: e.activation(out=la[:, :], in_=la[:, :], func=AF.Ln, bias=1.0),
                 reads=["la"], writes=["la"])
            for h in range(2):
                p.op("pe", lambda e, h=h: e.matmul(ps_c[0:64, h * 128:(h + 1) * 128], lhsT=la[:, h * 64:(h + 1) * 64],
                                                   rhs=cst_sb[:, 0, 0:128], start=True, stop=True),
                     reads=["la", "cst"], writes=["ps_c"])
            p.op("act", lambda e: e.activation(out=ep[:, :], in_=ps_c[0:64, 0:256], func=AF.Exp, bias=ln8t[0:64, 0:1]),
                 reads=["ps_c", "ln8"], writes=["ep"])
            p.op("act", lambda e: e.activation(out=em[:, :], in_=ps_c[0:64, 0:256], func=AF.Exp, scale=-1.0),
                 reads=["ps_c"], writes=["em"])
            p.op("act", lambda e: e.activation(out=dec[i][:, :], in_=ps_c[0:64, 63:256:64], func=AF.Exp),
                 reads=["ps_c"], writes=[f"dec{i}"])
            p.op("dve", lambda e: e.tensor_tensor(out=fl(qt[i][:, :, :]), in0=fl(q_in[ii][:, :, :]), in1=ep[:, :], op=ALU.mult),
                 reads=[f"q_in{ii}", "ep"], writes=[f"qt{i}"])
            p.op("dve", lambda e: e.tensor_tensor(out=fl(kt[i][:, :, :]), in0=fl(k_in[ii][:, :, :]), in1=em[:, :], op=ALU.mult),
                 reads=[f"k_in{ii}", "em"], writes=[f"kt{i}"])
        else:
            for (a_in, s_in, fa, fs, dst, dk_) in ((q_in, qs_in, 0, 1, qt, "qt"), (k_in, ks_in, 2, 3, kt, "kt")):
                p.op("dve", lambda e, a_in=a_in, fa=fa: e.tensor_tensor(
                    out=t1[:, :], in0=fl(a_in[ii][:, :, :]), in1=fl(tb_in[ii][:, fa, :, :]), op=ALU.mult),
                    reads=[f"q_in{ii}", f"k_in{ii}", f"tb_in{ii}"], writes=["t1"])
                p.op("pool", lambda e, s_in=s_in, fs=fs: e.tensor_tensor(
                    out=t2[:, :], in0=fl(s_in[ii][:, :, :]), in1=fl(tb_in[ii][:, fs, :, :]), op=ALU.mult),
                    reads=[f"qs_in{ii}", f"ks_in{ii}", f"tb_in{ii}"], writes=["t2"])
                p.op("dve", lambda e, dst=dst: e.tensor_tensor(out=fl(dst[i][:, :, :]), in0=t1[:, :], in1=t2[:, :], op=ALU.add),
                     reads=["t1", "t2"], writes=[f"{dk_}{i}"])
        for h in range(2):
            for n in range(2):
                j = h * 2 + n
                p.op("pe", lambda e, h=h, n=n, j=j: e.matmul(ps_t[0:64, j * 64:(j + 1) * 64], lhsT=kt[i][:, h, n * 64:(n + 1) * 64],
                                                            rhs=idb[:, :], start=True, stop=True),
                     reads=[f"kt{i}", "idb"], writes=["ps_t"])
        p.op("act", lambda e: e.activation(out=ktt[i][:, :], in_=ps_t[0:64, 0:256], func=AF.Copy),
             reads=["ps_t"], writes=[f"ktt{i}"])

    def core(b):
        i = b % NB2
        ii = b % NBI
        c0 = b * 128
        for h in range(2):
            for n in range(2):
                j = h * 2 + n
                p.op("pe", lambda e, h=h, n=n, j=j: e.matmul(
                    ps_a[0:64, j * 64:(j + 1) * 64], lhsT=kt[i][:, h, n * 64:(n + 1) * 64], rhs=qt[i][:, h, n * 64:(n + 1) * 64],
                    start=True, stop=True), reads=[f"kt{i}", f"qt{i}"], writes=["ps_a"])
        p.op("dve", lambda e: e.tensor_tensor(out=attm[i][:, :], in0=ps_a[0:64, 0:256], in1=m4[:, :], op=ALU.mult),
             reads=["ps_a", "m4"], writes=[f"attm{i}"])
        for n in range(2):
            for h in range(2):
                j = h * 2 + n
                j4 = n * 2 + h
                p.op("pe", lambda e, n=n, h=h, j=j, j4=j4: e.matmul(
                    ps_kv[0:64, j4 * 128:(j4 + 1) * 128], lhsT=ktt[i][:, j * 64:(j + 1) * 64], rhs=v_in[ii][:, n, h * 128:(h + 1) * 128],
                    start=True, stop=True), reads=[f"ktt{i}", f"v_in{ii}"], writes=["ps_kv"])
        dkey = f"dec{i}" if gla else "dec_c"
        for n in range(2):
            for h in range(2):
                j = h * 2 + n
                j4 = n * 2 + h
                dsc = dec[i][:, j:j + 1] if gla else dec_c[:, h:h + 1]
                p.op("act", lambda e, dsc=dsc, j4=j4: e.activation(out=kvd[:, j4, :], in_=ps_kv[0:64, j4 * 128:(j4 + 1) * 128],
                                                                   func=AF.Copy, scale=dsc),
                     reads=["ps_kv", dkey], writes=[f"kvd{j4}"])
        for n in range(2):
            cn = 2 * b + n
            SBc, sbk = SbX[cn % 2], f"Sb{cn % 2}"
            SBn, sbnk = SbX[(cn + 1) % 2], f"Sb{(cn + 1) % 2}"
            for h in range(2):
                j = h * 2 + n
                oc = slice(h * 128 + n * 64, h * 128 + (n + 1) * 64)
                p.op("pe", lambda e, n=n, h=h, j=j, oc=oc: e.matmul(
                    ps_o[:, oc], lhsT=v_in[ii][:, n, h * 128:(h + 1) * 128], rhs=attm[i][:, j * 64:(j + 1) * 64],
                    start=True, stop=False), reads=[f"v_in{ii}", f"attm{i}"], writes=["ps_o"])
                p.op("pe", lambda e, n=n, h=h, oc=oc, SBc=SBc: e.matmul(
                    ps_o[:, oc], lhsT=SBc[:, h, :], rhs=qt[i][:, h, n * 64:(n + 1) * 64], start=False, stop=True),
                    reads=[sbk, f"qt{i}"], writes=["ps_o"])
            for h in range(2):
                j = h * 2 + n
                j4 = n * 2 + h
                dsc = dec[i][:, j:j + 1] if gla else dec_c[:, h:h + 1]
                p.op("dve", lambda e, dsc=dsc, h=h, j4=j4: e.scalar_tensor_tensor(
                    out=S[:, h, :], in0=S[:, h, :], scalar=dsc, in1=kvd[:, j4, :],
                    op0=ALU.mult, op1=ALU.add), reads=[f"S{h}", f"kvd{j4}", dkey], writes=[f"S{h}"])
            p.op("act", lambda e, SBn=SBn: e.activation(out=fl(SBn[:, :, :]), in_=fl(S[:, :, :]), func=AF.Copy),
                 reads=["S0", "S1"], writes=[sbnk])
        OF, ofk = ofs[b % 2], f"of{b % 2}"
        p.op("act", lambda e: e.activation(out=OF[:, :], in_=ps_o[:, 0:256], func=AF.Copy), reads=["ps_o"], writes=[ofk])

    def tail(b):
        ii = b % NBI
        c0 = b * 128
        of, ofk = ofs[b % 2], f"of{b % 2}"
        if not gla:
            p.op("pe", lambda e: e.matmul(ps_n[:, 0:256], lhsT=ones[:, :], rhs=of[:, :], start=True, stop=True),
                 reads=[ofk, "ones"], writes=["ps_n"])
            p.op("dve", lambda e: e.tensor_tensor(out=of[:, :], in0=of[:, :], in1=ps_n[:, 0:256], op=ALU.subtract),
                 reads=[ofk, "ps_n"], writes=[ofk])
        p.op("act", lambda e: e.activation(out=osq[:, :], in_=of[:, :], func=AF.Square), reads=[ofk], writes=["osq"])
        p.op("pe", lambda e: e.matmul(ps_n[:, 256:512], lhsT=ones[:, :], rhs=osq[:, :], start=True, stop=True),
             reads=["osq", "ones"], writes=["ps_n"])
        p.op("act", lambda e: e.activation(out=rs[:, :], in_=ps_n[:, 256:512], func=AF.Sqrt, bias=epst[:, 0:1], scale=1.0),
             reads=["ps_n", "epsn"], writes=["rs"])
        p.op("dve", lambda e: e.reciprocal(out=rs[:, :], in_=rs[:, :]), reads=["rs"], writes=["rs"])
        p.op("dve", lambda e: e.tensor_tensor(out=of[:, :], in0=of[:, :], in1=rs[:, :], op=ALU.mult),
             reads=[ofk, "rs"], writes=[ofk])
        G = fl(g_in[ii][:, :, :])
        p.op("act", lambda e: e.activation(out=sg[:, :], in_=G, func=AF.Exp, scale=-1.0), reads=[f"g_in{ii}"], writes=["sg"])
        p.op("dve", lambda e: e.tensor_scalar(out=sg[:, :], in0=sg[:, :], scalar1=1.0, scalar2=None, op0=ALU.add),
             reads=["sg"], writes=["sg"])
        p.op("dve", lambda e: e.reciprocal(out=sg[:, :], in_=sg[:, :]), reads=["sg"], writes=["sg"])
        p.op("pool", lambda e: e.tensor_tensor(out=sg[:, :], in0=sg[:, :], in1=G, op=ALU.mult),
             reads=["sg", f"g_in{ii}"], writes=["sg"])
        Y, yk = yb[b % 2], f"yb{b % 2}"
        p.op("dve", lambda e: e.tensor_tensor(out=fl(Y[:, :, :]), in0=of[:, :], in1=sg[:, :], op=ALU.mult),
             reads=[ofk, "sg"], writes=[yk])
        p.dma("pool", oview[:, :, c0:c0 + 128], Y[:, :, :], reads=[yk], skey=yk)

    load(0)
    if NBLK > 1:
        load(1)
    prep(0)
    for b in range(NBLK):
        if b + 2 < NBLK:
            load(b + 2)
        if b + 1 < NBLK:
            prep(b + 1)
        core(b)
        if b >= 1:
            tail(b - 1)
    tail(NBLK - 1)
    if own:
        p.finish()
        return p.emit()


def l_consts():
    t = np.arange(128)
    same = (t[:, None] // 64) == (t[None, :] // 64)
    tri = np.where(same & (t[:, None] <= t[None, :]), -1.0 / 16.0, 0.0)
    cst = np.zeros((128, 3, 256), np.float32)
    cst[:, 0, 0:128] = tri
    s_ = np.arange(64)
    m64 = (s_[:, None] <= s_[None, :]).astype(np.float32)
    cst[0:64, 1, :] = np.tile(m64, (1, 4))
    cst[0:64, 2, :] = np.tile(np.eye(64, dtype=np.float32), (1, 4))
    return cst


def ret_tables(r):
    pos = np.arange(PPAD)
    inv = (10000.0 ** (-np.arange(32, dtype=np.float32) / 32)).astype(np.float32)
    ang = (pos.astype(np.float32)[:, None] * inv[None, :]).astype(np.float32).astype(np.float64)
    cos = np.cos(ang).T
    sin = np.sin(ang).T
    Cos = np.concatenate([cos, cos], 0)
    SinS = np.concatenate([-sin, sin], 0)
    c = (pos % 64).astype(np.float64)
    tabs = np.zeros((4, 128, PPAD), np.float64)
    decc = np.zeros((64, 2), np.float64)
    for hl in range(2):
        h = 2 * r + hl
        gam = 1.0 - 2.0 ** (-5.0 - h)
        xi = gam ** (c + 1.0)
        kf = gam ** (-(c + 1.0)) / 8.0
        sl = slice(hl * 64, (hl + 1) * 64)
        tabs[0, sl] = Cos * xi
        tabs[1, sl] = SinS * xi
        tabs[2, sl] = Cos * kf
        tabs[3, sl] = SinS * kf
        decc[:, hl] = gam ** 64
    return tabs.astype(np.float32), decc.astype(np.float32)


NQB = 33
NQ = NQB * 128


def a_groups():
    gs = []
    jj = 0
    while jj < NQB:
        n = min(4, NQB - jj)
        gs.append((jj, n))
        jj += n
    return gs


def build_A(p=None, io=None):
    own = p is None
    NH = 8
    if own:
        p = Prog()
        A_ = {}
        A_["aqT"] = p.dram("aqT", [NH, 64, NQ], BF16, kind="ExternalInput").ap().rearrange("h d (j t) -> h d j t", t=128)
        A_["agT"] = p.dram("agT", [NH, 64, NQ], F32, kind="ExternalInput").ap().rearrange("h d (j t) -> h d j t", t=128)
        A_["iqT"] = p.dram("iqT", [NH, 64, NQ], BF16, kind="ExternalInput").ap().rearrange("h d (j t) -> h d j t", t=128)
        A_["iw"] = p.dram("iw", [128, NQB, 8], F32, kind="ExternalInput").ap()
        A_["akT"] = p.dram("akT", [NH, 64, PPAD], BF16, kind="ExternalInput").ap()
        A_["avh"] = p.dram("avh", [NH, 128, NBLK, 64], BF16, kind="ExternalInput").ap()
        A_["ikT"] = p.dram("ikT", [64, PPAD], BF16, kind="ExternalInput").ap()
        A_["mskd"] = p.dram("mskd", [128, 3, 128], F32, kind="ExternalInput").ap()
        A_["btd"] = p.dram("btd", [128, 3, NH, 128], F32, kind="ExternalInput").ap()
        A_["b15d"] = p.dram("b15d", [128, NH], F32, kind="ExternalInput").ap()
        A_["cstd"] = p.dram("cstd", [128, 160], F32, kind="ExternalInput").ap()
        A_["oT"] = p.dram("oT", [NH, 64, NQ], BF16, kind="ExternalOutput").ap().rearrange("h d (j t) -> h d j t", t=128)
    else:
        A_ = io

    NIT = 26
    MAXKB = 64
    ik_sb = p.sb("ik_sb", [64, PPAD], BF16)
    k_sb = p.sb("k_sb", [64, PPAD], BF16)
    v_sb = p.sb("v_sb", [128, NBLK, 64], BF16)
    mT = p.sb("mT", [128, MAXKB * 512], BF16)
    score = p.sb("score", [128, PPAD], F32)
    junk = p.sb("junk", [128, PPAD], BF16)
    iq_sb = p.sb("iq_sb", [64, NH, 512], BF16)
    aq_sb = p.sb("aq_sb", [64, NH, 512], BF16)
    iw_sb = p.sb("iw_sb", [128, NQB, 8], F32)
    msk = p.sb("msk", [128, 3, 128], F32)
    bt = p.sb("bt", [128, 3, NH, 128], F32)
    b15 = p.sb("b15", [128, NH], F32)
    cst = p.sb("cst", [128, 160], F32)
    idb = p.sb("idb", [128, 128], BF16)
    onesb = p.sb("onesb", [128, 64], BF16)
    R = [p.sb(f"R{i}", [128, 512], F32) for i in range(2)]
    mk = [p.sb(f"mk{i}", [128, 512], BF16) for i in range(2)]
    PT = [p.sb(f"PT{i}", [128, 512], BF16) for i in range(3)]
    st = p.sb("st", [128, 8], F32)
    W = p.sb("W", [128, NIT], F32)
    sa = p.sb("sa", [128, 1], F32)
    g_sb = p.sb("g_sb", [64, 512], F32)
    sg = p.sb("sg", [64, 512], F32)
    rd = p.sb("rd", [64, 512], F32)
    ob = [p.sb(f"ob{i}", [64, 512], BF16) for i in range(2)]

    psI = [p.ps(f"psI{i}", [128, 512]) for i in range(2)]
    psT = p.ps("psT", [128, 512])
    psS = [p.ps(f"psS{i}", [128, 512]) for i in range(3)]
    psO = p.ps("psO", [128, 512])
    psD = p.ps("psD", [128, 512])

    p.dma("sp", ik_sb[:, :], A_["ikT"], writes=["ik"])
    p.dma("sp", iw_sb[:, :, :], A_["iw"], writes=["iw"])
    p.dma("sp", msk[:, :, :], A_["mskd"], writes=["msk"])
    p.dma("sp", bt[:, :, :, :], A_["btd"], writes=["bt"])
    p.dma("sp", b15[:, :], A_["b15d"], writes=["b15"])
    p.dma("sp", cst[:, :], A_["cstd"], writes=["cst"])
    p.op("dve", lambda e: e.tensor_copy(out=idb[:, :], in_=cst[:, 0:128]), reads=["cst"], writes=["idb"])
    p.op("pool", lambda e: e.memset(onesb[:, :], 1.0), writes=["onesb"])
    for h in range(NH):
        for w_ in range(3):
            p.op("dve", lambda e, h=h, w_=w_: e.tensor_scalar(out=bt[:, w_, h, :], in0=bt[:, w_, h, :], scalar1=b15[:, h:h + 1],
                                                              scalar2=None, op0=ALU.subtract), reads=["bt", "b15"], writes=["bt"])
    p.op("dve", lambda e: e.tensor_scalar(out=bt[:, :, :, :].rearrange("p a h t -> p (a h t)"),
                                          in0=bt[:, :, :, :].rearrange("p a h t -> p (a h t)"), scalar1=8.0, scalar2=None,
                                          op0=ALU.mult), reads=["bt"], writes=["bt"])

    evi = [0]
    for gi, (jj0, nq) in enumerate(a_groups()):
        N = nq * 128
        g8 = 8 * gi
        nkb = min(8 * gi + 8, NBLK) if nq == 4 else NBLK
        q0 = jj0 * 128
        for hh in range(NH):
            p.dma("sp", iq_sb[:, hh, 0:N].rearrange("d (j t) -> d j t", t=128), A_["iqT"][hh, :, jj0:jj0 + nq, :], writes=["iq"])
            p.dma("sp", aq_sb[:, hh, 0:N].rearrange("d (j t) -> d j t", t=128), A_["aqT"][hh, :, jj0:jj0 + nq, :], writes=["aq"])
        p.op("pool", lambda e, nkb=nkb, N=N: e.memset(mT[:, 0:nkb * N], 0.0), writes=["mT"])
        mTv = mT[:, 0:nkb * N].rearrange("p (k t) -> p k t", t=N)
        for j in range(nq):
            jj = jj0 + j
            nk = g8 + 2 * j + 2
            nkeys = nk * 128
            ntl = (nkeys + 511) // 512
            for kt_ in range(ntl):
                c0 = kt_ * 512
                w_ = min(512, nkeys - c0)
                for h in range(NH):
                    ii = evi[0] % 2
                    evi[0] += 1
                    p.op("pe", lambda e, ii=ii, h=h, j=j, c0=c0, w_=w_: e.matmul(
                        psI[ii][:, 0:w_], lhsT=iq_sb[:, h, j * 128:(j + 1) * 128], rhs=ik_sb[:, c0:c0 + w_],
                        start=True, stop=True), reads=["iq", "ik"], writes=[f"psI{ii}"])
                    p.op("act", lambda e, ii=ii, w_=w_: e.activation(out=R[ii][:, 0:w_], in_=psI[ii][:, 0:w_], func=AF.Relu),
                         reads=[f"psI{ii}"], writes=[f"R{ii}"])
                    if h == 0:
                        p.op("dve", lambda e, ii=ii, c0=c0, w_=w_, jj=jj: e.tensor_scalar(
                            out=score[:, c0:c0 + w_], in0=R[ii][:, 0:w_], scalar1=iw_sb[:, jj, 0:1], scalar2=None,
                            op0=ALU.mult), reads=[f"R{ii}", "iw"], writes=["score"])
                    else:
                        p.op("dve", lambda e, ii=ii, c0=c0, w_=w_, jj=jj, h=h: e.scalar_tensor_tensor(
                            out=score[:, c0:c0 + w_], in0=R[ii][:, 0:w_], scalar=iw_sb[:, jj, h:h + 1],
                            in1=score[:, c0:c0 + w_], op0=ALU.mult, op1=ALU.add),
                            reads=[f"R{ii}", "iw", "score"], writes=["score"])
            p.op("dve", lambda e, nkeys=nkeys: e.reduce_max(out=st[:, 0:1], in_=score[:, 0:nkeys], axis=AX.X,
                                                            apply_absolute_value=True), reads=["score"], writes=["stB"])
            p.op("dve", lambda e, nk=nk: e.tensor_tensor(out=score[:, (nk - 2) * 128:(nk - 1) * 128],
                                                         in0=score[:, (nk - 2) * 128:(nk - 1) * 128], in1=msk[:, 0, :], op=ALU.add),
                 reads=["score", "msk"], writes=["score"])
            p.op("dve", lambda e, nk=nk: e.tensor_tensor(out=score[:, (nk - 1) * 128:nk * 128],
                                                         in0=score[:, (nk - 1) * 128:nk * 128], in1=msk[:, 1, :], op=ALU.add),
                 reads=["score", "msk"], writes=["score"])
            p.op("dve", lambda e: e.tensor_tensor(out=score[:, 0:128], in0=score[:, 0:128], in1=msk[:, 2, :], op=ALU.add),
                 reads=["score", "msk"], writes=["score"])
            p.op("dve", lambda e: e.tensor_scalar(out=st[:, 1:2], in0=st[:, 0:1], scalar1=2.002, scalar2=1e-6,
                                                  op0=ALU.mult, op1=ALU.add), reads=["stB"], writes=["stw0"])
            p.op("dve", lambda e: e.tensor_scalar(out=st[:, 2:3], in0=st[:, 1:2], scalar1=-0.5, scalar2=None, op0=ALU.mult),
                 reads=["stw0"], writes=["stlo"])
            p.op("dve", lambda e: e.tensor_scalar(out=W[:, :], in0=cst[:, 128:128 + NIT], scalar1=st[:, 1:2], scalar2=None,
                                                  op0=ALU.mult), reads=["cst", "stw0"], writes=["W"])
            n1 = max(128, (nkeys // 2) // 128 * 128)
            n2 = nkeys - n1
            p.op("pool", lambda e, n2=n2: e.memset(st[:, 6:7], float(TOPK) - 0.5 - 0.5 * n2), writes=["k255"])
            for it in range(NIT):
                p.op("dve", lambda e, it=it: e.tensor_tensor(out=st[:, 3:4], in0=st[:, 2:3], in1=W[:, it:it + 1], op=ALU.add),
                     reads=["stlo", "W"], writes=["stmid"])
                p.op("dve", lambda e, n1=n1: e.tensor_scalar(out=junk[:, 0:n1], in0=score[:, 0:n1],
                                                             scalar1=st[:, 3:4], scalar2=0.0, op0=ALU.is_ge, op1=ALU.add,
                                                             accum_out=st[:, 4:5]),
                     reads=["score", "stmid"], writes=["junk", "stcnt"])
                p.op("act", lambda e, n1=n1, nkeys=nkeys: e.activation(out=junk[:, n1:nkeys], in_=score[:, n1:nkeys], func=AF.Sign,
                                                                       bias=st[:, 3:4], scale=-1.0, accum_out=sa[:, 0:1]),
                     reads=["score", "stmid"], writes=["junkB", "stS"])
                p.op("dve", lambda e: e.scalar_tensor_tensor(out=st[:, 4:5], in0=sa[:, 0:1], scalar=-0.5, in1=st[:, 4:5],
                                                             op0=ALU.mult, op1=ALU.add),
                     reads=["stS", "stcnt"], writes=["stcnt"])
                p.op("dve", lambda e, it=it: e.tensor_scalar(out=st[:, 5:6], in0=st[:, 4:5], scalar1=st[:, 6:7],
                                                             scalar2=W[:, it:it + 1], op0=ALU.is_gt, op1=ALU.mult),
                     reads=["stcnt", "k255", "W"], writes=["stt"])
                p.op("dve", lambda e: e.tensor_tensor(out=st[:, 2:3], in0=st[:, 2:3], in1=st[:, 5:6], op=ALU.add),
                     reads=["stlo", "stt"], writes=["stlo"])
            p.op("dve", lambda e: e.tensor_scalar(out=st[:, 7:8], in0=st[:, 2:3], scalar1=-1.0e29, scalar2=None, op0=ALU.max),
                 reads=["stlo"], writes=["stthr"])
            for kt_ in range(ntl):
                c0 = kt_ * 512
                w_ = min(512, nkeys - c0)
                nb_ = w_ // 128
                mi = kt_ % 2
                p.op("dve", lambda e, mi=mi, c0=c0, w_=w_: e.tensor_scalar(out=mk[mi][:, 0:w_], in0=score[:, c0:c0 + w_],
                                                                           scalar1=st[:, 7:8], scalar2=None, op0=ALU.is_ge),
                     reads=["score", "stthr"], writes=[f"mk{mi}"])
                for bb in range(nb_):
                    p.op("pe", lambda e, mi=mi, bb=bb: e.matmul(psT[:, bb * 128:(bb + 1) * 128], lhsT=mk[mi][:, bb * 128:(bb + 1) * 128],
                                                                rhs=idb[:, :], start=True, stop=True),
                         reads=[f"mk{mi}", "idb"], writes=["psT"])
                p.op("act", lambda e, kt_=kt_, nb_=nb_, j=j, w_=w_, mTv=mTv: e.activation(
                    out=mTv[:, kt_ * 4:kt_ * 4 + nb_, j * 128:(j + 1) * 128],
                    in_=psT[:, 0:w_].rearrange("p (k t) -> p k t", t=128), func=AF.Copy),
                    reads=["psT"], writes=["mT"])
        for h in range(NH):
            p.dma("sp", k_sb[:, 0:nkb * 128], A_["akT"][h, :, 0:nkb * 128], writes=["k_sb"])
            p.dma("sp", v_sb[:, 0:nkb, :], A_["avh"][h, :, 0:nkb, :], writes=["v_sb"])
            def s1(kb, h=h, N=N, nkb=nkb, mTv=mTv, nq=nq, g8=g8):
                si = kb % 3
                p.op("pe", lambda e: e.matmul(
                    psS[si][:, 0:N], lhsT=k_sb[:, kb * 128:(kb + 1) * 128], rhs=aq_sb[:, h, 0:N], start=True, stop=True),
                    reads=["k_sb", "aq"], writes=[f"psS{si}"])
                for j in range(nq):
                    which = kb - (g8 + 2 * j - 1)
                    if 0 <= which <= 2:
                        p.op("dve", lambda e, j=j, which=which: e.tensor_tensor(
                            out=psS[si][:, j * 128:(j + 1) * 128], in0=psS[si][:, j * 128:(j + 1) * 128],
                            in1=bt[:, which, h, :], op=ALU.add), reads=[f"psS{si}", "bt"], writes=[f"psS{si}"])
                p.op("act", lambda e: e.activation(out=PT[si][:, 0:N], in_=psS[si][:, 0:N], func=AF.Exp, scale=0.125),
                     reads=[f"psS{si}"], writes=[f"PT{si}"])
                p.op("dve", lambda e: e.tensor_tensor(out=PT[si][:, 0:N], in0=PT[si][:, 0:N], in1=mTv[:, kb, :], op=ALU.mult),
                     reads=[f"PT{si}", "mT"], writes=[f"PT{si}"])

            def s2(kb, h=h, N=N, nkb=nkb):
                si = kb % 3
                p.op("pe", lambda e: e.matmul(
                    psO[0:64, 0:N], lhsT=v_sb[:, kb, :], rhs=PT[si][:, 0:N], start=(kb == 0), stop=(kb == nkb - 1)),
                    reads=["v_sb", f"PT{si}"], writes=["psO"])
                p.op("pe", lambda e: e.matmul(
                    psD[0:64, 0:N], lhsT=onesb[:, :], rhs=PT[si][:, 0:N], start=(kb == 0), stop=(kb == nkb - 1)),
                    reads=["onesb", f"PT{si}"], writes=["psD"])

            for kb in range(nkb + 2):
                if kb < nkb:
                    s1(kb)
                if kb - 2 >= 0:
                    s2(kb - 2)
            p.dma("sp", g_sb[:, 0:N].rearrange("d (j t) -> d j t", t=128), A_["agT"][h, :, jj0:jj0 + nq, :], writes=["g_sb"])
            p.op("dve", lambda e, N=N: e.tensor_scalar(out=rd[:, 0:N], in0=psD[0:64, 0:N], scalar1=1e-30, scalar2=None,
                                                       op0=ALU.max), reads=["psD"], writes=["rd"])
            p.op("dve", lambda e, N=N: e.reciprocal(out=rd[:, 0:N], in_=rd[:, 0:N]), reads=["rd"], writes=["rd"])
            p.op("act", lambda e, N=N: e.activation(out=sg[:, 0:N], in_=g_sb[:, 0:N], func=AF.Exp, scale=-1.0),
                 reads=["g_sb"], writes=["sg"])
            p.op("dve", lambda e, N=N: e.tensor_scalar(out=sg[:, 0:N], in0=sg[:, 0:N], scalar1=1.0, scalar2=None, op0=ALU.add),
                 reads=["sg"], writes=["sg"])
            p.op("dve", lambda e, N=N: e.reciprocal(out=sg[:, 0:N], in_=sg[:, 0:N]), reads=["sg"], writes=["sg"])
            p.op("pool", lambda e, N=N: e.tensor_tensor(out=sg[:, 0:N], in0=sg[:, 0:N], in1=g_sb[:, 0:N], op=ALU.mult),
                 reads=["sg", "g_sb"], writes=["sg"])
            p.op("pool", lambda e, N=N: e.tensor_tensor(out=sg[:, 0:N], in0=sg[:, 0:N], in1=rd[:, 0:N], op=ALU.mult),
                 reads=["sg", "rd"], writes=["sg"])
            O, ok_ = ob[h % 2], f"ob{h % 2}"
            p.op("dve", lambda e, N=N, O=O: e.tensor_tensor(out=O[:, 0:N], in0=psO[0:64, 0:N], in1=sg[:, 0:N], op=ALU.mult),
                 reads=["psO", "sg"], writes=[ok_])
            p.dma("pool", A_["oT"][h, :, jj0:jj0 + nq, :], O[:, 0:N].rearrange("d (j t) -> d j t", t=128), reads=[ok_], skey=ok_)
    if own:
        p.finish()
        return p.emit()


def rel_bucket_np(rel):
    n = -rel
    ret = np.where(n < 0, 16, 0)
    n = np.abs(n)
    nf = np.maximum(n, 1).astype(np.float32)
    large = 8 + (np.log(nf / np.float32(8)) / np.float32(np.log(16.0)) * np.float32(8)).astype(np.int32)
    large = np.minimum(large, 15)
    return ret + np.where(n < 8, n, large)


def a_consts(r, rel_bias):
    s = np.arange(128)
    t = np.arange(128)
    diag = np.where((s[None, :] < 64) | (t[:, None] >= 64), 0.0, NEG)
    if r == 0:
        mA, mB = diag, np.full((128, 128), NEG)
    else:
        mA, mB = np.zeros((128, 128)), diag
    padm = np.where(s[None, :] >= PADF, 0.0, NEG) * np.ones((128, 1))
    mskd = np.stack([mA, mB, padm], axis=1).astype(np.float32)
    btd = np.zeros((128, 3, 8, 128), np.float32)
    for which in range(3):
        dblk = (r + 1 - which)
        rel = (s[:, None] - (t[None, :] + dblk * 128)).astype(np.int32)
        bk = rel_bucket_np(rel)
        btd[:, which, :, :] = np.transpose(rel_bias[bk], (0, 2, 1))
    b15d = np.broadcast_to(rel_bias[15][None, :], (128, 8)).astype(np.float32).copy()
    cstd = np.zeros((128, 160), np.float32)
    cstd[:, 0:128] = np.eye(128)
    cstd[:, 128:128 + 26] = (2.0 ** -(np.arange(26) + 1.0))[None, :]
    return mskd, btd, b15d, cstd


def a_pack(aqT, agT, iqT, iw, akT, av, ikT, r, rel_bias):
    blks = np.arange(NQB) * 2 + r
    cols = (blks[:, None] * 128 + np.arange(128)[None, :]).reshape(-1)
    mskd, btd, b15d, cstd = a_consts(r, rel_bias)
    return {
        "aqT": np.ascontiguousarray(aqT[:, :, cols]),
        "agT": np.ascontiguousarray(agT[:, :, cols]),
        "iqT": np.ascontiguousarray(iqT[:, :, cols]),
        "iw": np.ascontiguousarray(iw[cols].reshape(NQB, 128, 8).transpose(1, 0, 2)),
        "akT": np.ascontiguousarray(akT),
        "avh": np.ascontiguousarray(av.reshape(NBLK, 128, 8, 64).transpose(2, 1, 0, 3)),
        "ikT": np.ascontiguousarray(ikT),
        "mskd": mskd, "btd": btd, "b15d": b15d, "cstd": cstd,
    }


T_CORE = PPAD // 2

AB_FB = ([(0 + i * 128, 128) for i in range(4)] + [(512 + i * 128, 128) for i in range(4)]
         + [(2048 + i * 128, 128) for i in range(4)] + [(2560, 64)]
         + [(2632, 128), (2760, 128), (2888, 128), (3016, 128)])
AB_FF = [(1536 + i * 128, 128) for i in range(4)] + [(3656 + i * 128, 128) for i in range(4)] + [(4168, 16)]
AB_TB = [(1024, 512), (3144, 512)]
AB_TF = [(2624, 8)]
W_CDX = W_CD + 512
CD_FB = ([(0, 128), (128, 128), (256, 128), (384, 128), (3584, 128), (3712, 128), (3840, 128), (3968, 128)]
         + [(1536 + i * 128, 128) for i in range(4)] + [(2048 + i * 128, 128) for i in range(4)])
CD_FF = [(1024 + i * 128, 128) for i in range(4)] + [(3072 + i * 128, 128) for i in range(4)]
CD_TB = [(512, 512), (2560, 512)]
CD_TF = []

_PROGS = {}


def _prog(name):
    if name not in _PROGS:
        if name == "P_AB0":
            _PROGS[name] = build_P(T_CORE, W_AB, AB_FB, AB_FF, AB_TB, AB_TF, with_out=False)
        elif name == "P_AB":
            _PROGS[name] = build_P(T_CORE, W_AB, AB_FB, AB_FF, AB_TB, AB_TF, with_out=True)
        elif name == "P_CD":
            _PROGS[name] = build_P(T_CORE, W_CDX, CD_FB, CD_FF, CD_TB, CD_TF, with_out=True)
        elif name == "F":
            _PROGS[name] = build_P(T_CORE, 0, [], [], [], [], with_out=True, final=True)
        elif name == "A":
            _PROGS[name] = build_A()
        elif name == "D":
            _PROGS[name] = build_D()
        elif name == "LG":
            _PROGS[name] = build_L("gla")
        elif name == "LR":
            _PROGS[name] = build_L("ret")
    return _PROGS[name]


def _run(name, maps):
    res = run_bass_kernel_spmd(_prog(name), maps, core_ids=list(range(NCORES)))
    return res.results


def _g_layout(g):
    return np.ascontiguousarray(np.asarray(g, np.float32).reshape(8, 128).T)


def _seq(results, key, axis):
    return [np.concatenate([results[2 * b][key], results[2 * b + 1][key]], axis=axis) for b in range(BATCH)]


def _cd_weights(w):
    w = np.asarray(w, np.float32)
    d = np.arange(64)
    sw = (d + 32) % 64
    cq_sw = np.concatenate([h * 64 + sw for h in range(4)])
    ck_sw = 256 + cq_sw
    return np.ascontiguousarray(np.concatenate([w, w[:, cq_sw], w[:, ck_sw]], axis=1))


def kernel(x, meta_tokens, rel_bias, norm_g, final_g, w_in_ab, gla_gate_w2, gla_gate_b,
           w_out_ab, w_in_cd, w_out_cd):
    x = np.asarray(x, np.float32)
    rel_bias = np.asarray(rel_bias, np.float32)
    h = np.zeros((BATCH, PPAD, D_MODEL), np.float32)
    h[:, PADF:128] = np.asarray(meta_tokens, np.float32)[None]
    h[:, 128:128 + SEQ] = x
    hT = [np.ascontiguousarray(h[b].T) for b in range(BATCH)]
    del h
    mixT = None
    T = T_CORE
    dmasks, dut = d_consts()
    lcst = l_consts()
    for layer in range(4):
        j = layer // 2
        ab = (layer % 2 == 0)
        maps = []
        for c in range(NCORES):
            b, r = divmod(c, 2)
            m = {"hT": np.ascontiguousarray(hT[b][:, r * T:(r + 1) * T]), "g": _g_layout(norm_g[layer])}
            if ab:
                m["w"] = np.ascontiguousarray(np.asarray(w_in_ab[j], np.float32))
            else:
                m["w"] = _cd_weights(w_in_cd[j])
            if layer > 0:
                m["mixT"] = np.ascontiguousarray(mixT[b][:, r * T:(r + 1) * T])
                m["wo"] = np.ascontiguousarray(np.asarray(w_out_cd[j - 1] if ab else w_out_ab[j], np.float32))
            maps.append(m)
        pname = "P_AB0" if layer == 0 else ("P_AB" if ab else "P_CD")
        res = _run(pname, maps)
        del maps
        if layer > 0:
            hT = _seq(res, "hT_new", 1)
        of_bf = _seq(res, "of_bf", 1)
        of_f32 = _seq(res, "of_f32", 1)
        ot_bf = _seq(res, "ot_bf", 0)
        ot_f32 = _seq(res, "ot_f32", 0)
        del res
        import ml_dtypes
        mixT = [np.zeros((D_MODEL, PPAD), ml_dtypes.bfloat16) for _ in range(BATCH)]
        if ab:
            maps = []
            for c in range(NCORES):
                b, r = divmod(c, 2)
                fb, ff = of_bf[b], of_f32[b]
                maps.append(a_pack(fb[0:512].reshape(8, 64, PPAD), ff[0:512].reshape(8, 64, PPAD),
                                   fb[1024:1536].reshape(8, 64, PPAD), ot_f32[b], fb[512:1024].reshape(8, 64, PPAD),
                                   ot_bf[b][:, 0:512], fb[1536:1600], r, rel_bias))
            res = _run("A", maps)
            del maps
            for c in range(NCORES):
                b, r = divmod(c, 2)
                o = res[c]["oT"].reshape(512, NQB, 128)
                blks = np.arange(NQB) * 2 + r
                keep = blks < NBLK - (1 if r == 1 else 0) if False else blks < NBLK
                mv = mixT[b][0:512].reshape(512, NBLK, 128)
                mv[:, blks[keep], :] = o[:, keep, :]
            del res
            maps = []
            for c in range(NCORES):
                b, r = divmod(c, 2)
                fb, ff = of_bf[b], of_f32[b]
                w2a = np.concatenate([np.asarray(gla_gate_w2[j], np.float32)[:, r * 128:(r + 1) * 128],
                                      np.asarray(gla_gate_b[j], np.float32)[None, r * 128:(r + 1) * 128]], axis=0)
                maps.append({"qT": np.ascontiguousarray(fb[(13 + r) * 128:(14 + r) * 128]),
                             "kT": np.ascontiguousarray(fb[(15 + r) * 128:(16 + r) * 128]),
                             "v": np.ascontiguousarray(ot_bf[b][:, 512 + r * 256:512 + (r + 1) * 256]),
                             "gT": np.ascontiguousarray(ff[512 + r * 256:512 + (r + 1) * 256]),
                             "baT": np.ascontiguousarray(ff[1024:1040]), "w2a": np.ascontiguousarray(w2a), "cst": lcst})
            res = _run("LG", maps)
            del maps
            for c in range(NCORES):
                b, r = divmod(c, 2)
                mixT[b][512 + r * 256:512 + (r + 1) * 256] = res[c]["oT"]
            del res
        else:
            maps = []
            for c in range(NCORES):
                b, r = divmod(c, 2)
                fb, ff = of_bf[b], of_f32[b]
                tabs, decc = ret_tables(r)
                maps.append({"qT": np.ascontiguousarray(fb[(0 + r) * 128:(1 + r) * 128]),
                             "kT": np.ascontiguousarray(fb[(2 + r) * 128:(3 + r) * 128]),
                             "qsT": np.ascontiguousarray(fb[(4 + r) * 128:(5 + r) * 128]),
                             "ksT": np.ascontiguousarray(fb[(6 + r) * 128:(7 + r) * 128]),
                             "v": np.ascontiguousarray(ot_bf[b][:, r * 256:(r + 1) * 256]),
                             "gT": np.ascontiguousarray(ff[r * 256:(r + 1) * 256]),
                             "tabs": tabs, "decc": decc, "cst": lcst})
            res = _run("LR", maps)
            del maps
            for c in range(NCORES):
                b, r = divmod(c, 2)
                mixT[b][r * 256:(r + 1) * 256] = res[c]["oT"]
            del res
            maps = []
            for c in range(NCORES):
                b, r = divmod(c, 2)
                fb, ff = of_bf[b], of_f32[b]
                maps.append({"qT": np.ascontiguousarray(fb[8 * 128 + r * 256:8 * 128 + (r + 1) * 256].reshape(4, 64, PPAD)),
                             "kT": np.ascontiguousarray(fb[12 * 128 + r * 256:12 * 128 + (r + 1) * 256].reshape(4, 64, PPAD)),
                             "v": np.ascontiguousarray(ot_bf[b][:, 512 + r * 256:512 + (r + 1) * 256]),
                             "gT": np.ascontiguousarray(ff[512 + r * 256:512 + (r + 1) * 256].reshape(4, 64, PPAD)),
                             "masks": dmasks, "ut": dut})
            res = _run("D", maps)
            del maps
            for c in range(NCORES):
                b, r = divmod(c, 2)
                mixT[b][512 + r * 256:512 + (r + 1) * 256] = res[c]["oT"].reshape(256, PPAD)
            del res
        del of_bf, of_f32, ot_bf, ot_f32
    maps = []
    for c in range(NCORES):
        b, r = divmod(c, 2)
        maps.append({"hT": np.ascontiguousarray(hT[b][:, r * T:(r + 1) * T]), "g": _g_layout(final_g),
                     "mixT": np.ascontiguousarray(mixT[b][:, r * T:(r + 1) * T]),
                     "wo": np.ascontiguousarray(np.asarray(w_out_cd[1], np.float32))})
    res = _run("F", maps)
    yT = _seq(res, "y", 1)
    out = np.stack([np.ascontiguousarray(yT[b][:, 128:128 + SEQ].T) for b in range(BATCH)], axis=0)
    return out.astype(np.float32)


def build_fused():
    p = Prog()
    p.fused = True
    EI = "ExternalInput"
    hT0 = p.dram("hT0", [D_MODEL, PPAD], F32, kind=EI)
    gs = p.dram("gs", [5, 128, 8], F32, kind=EI)
    w_ab = p.dram("w_ab", [2, D_MODEL, W_AB], F32, kind=EI)
    w_cdx = p.dram("w_cdx", [2, D_MODEL, W_CDX], F32, kind=EI)
    wo_ab = p.dram("wo_ab", [2, D_MODEL, D_MODEL], F32, kind=EI)
    wo_cd = p.dram("wo_cd", [2, D_MODEL, D_MODEL], F32, kind=EI)
    w2 = p.dram("w2", [2, 16, 256], F32, kind=EI)
    gb = p.dram("gb", [2, 1, 256], F32, kind=EI)
    mskd = p.dram("mskd", [2, 128, 3, 128], F32, kind=EI)
    btd = p.dram("btd", [2, 128, 3, 8, 128], F32, kind=EI)
    b15d = p.dram("b15d", [128, 8], F32, kind=EI)
    cstd = p.dram("cstd", [128, 160], F32, kind=EI)
    lcst = p.dram("lcst", [128, 3, 256], F32, kind=EI)
    tabs = p.dram("tabs", [2, 4, 128, PPAD], F32, kind=EI)
    decc = p.dram("decc", [2, 64, 2], F32, kind=EI)
    dmasks = p.dram("dmasks", [6, 128, 512], BF16, kind=EI)
    dut = p.dram("dut", [128, 256], BF16, kind=EI)
    y = p.dram("y", [D_MODEL, PPAD], F32, kind="ExternalOutput")
    of_bf = p.dram("s_of_bf", [17 * 128, PPAD], BF16)
    of_f32 = p.dram("s_of_f32", [9 * 128, PPAD], F32)
    ot_bf = p.dram("s_ot_bf", [PPAD, 1024], BF16)
    ot_f32 = p.dram("s_ot_f32", [PPAD, 8], F32)
    avh = p.dram("s_avh", [8, 128, NBLK, 64], BF16)
    mixT = p.dram("s_mixT", [D_MODEL, PPAD], BF16)
    hA = p.dram("s_hA", [D_MODEL, PPAD], F32)
    hB = p.dram("s_hB", [D_MODEL, PPAD], F32)

    fb = of_bf.ap()
    ff = of_f32.ap()
    hd3 = lambda ap_: ap_.rearrange("(h d) t -> h d t", d=64)

    def qsel(ap_, r):
        return ap_.rearrange("(h d) (j r t) -> h d j r t", d=64, r=2, t=128)[:, :, :, r, :]

    h_cur = hT0.ap()
    h_bufs = [hA.ap(), hB.ap()]
    for layer in range(4):
        j = layer // 2
        ab = (layer % 2 == 0)
        p.phase(f"P{layer}")
        io = {"hT": h_cur, "g": gs.ap()[layer], "of_bf": fb, "of_f32": ff, "ot_bf": ot_bf.ap(), "ot_f32": ot_f32.ap()}
        io["w"] = w_ab.ap()[j] if ab else w_cdx.ap()[j]
        if layer > 0:
            io["mixT"] = mixT.ap()
            io["wo"] = wo_cd.ap()[j - 1] if ab else wo_ab.ap()[j]
            io["hT_new"] = h_bufs[(layer - 1) % 2]
        if ab:
            tdst = {1024: (lambda blk: avh.ap()[:, :, blk, :].rearrange("h s d -> s h d"))}
            build_P(PPAD, W_AB, AB_FB, AB_FF, AB_TB, AB_TF, with_out=(layer > 0), p=p, io=io, tdst=tdst)
        else:
            build_P(PPAD, W_CDX, CD_FB, CD_FF, CD_TB, CD_TF, with_out=True, p=p, io=io)
        if layer > 0:
            h_cur = h_bufs[(layer - 1) % 2]
        if ab:
            for r in range(2):
                p.phase(f"A{layer}{r}")
                build_A(p=p, io={
                    "aqT": qsel(fb[0:512], r), "agT": qsel(ff[0:512], r), "iqT": qsel(fb[1024:1536], r),
                    "iw": ot_f32.ap().rearrange("(j r t) c -> t j r c", r=2, t=128)[:, :, r, :],
                    "akT": hd3(fb[512:1024]), "avh": avh.ap(), "ikT": fb[1536:1600],
                    "mskd": mskd.ap()[r], "btd": btd.ap()[r], "b15d": b15d.ap(), "cstd": cstd.ap(),
                    "oT": qsel(mixT.ap()[0:512], r)})
            for r in range(2):
                p.phase(f"G{layer}{r}")
                build_L("gla", p=p, io={
                    "qT": fb[(13 + r) * 128:(14 + r) * 128], "kT": fb[(15 + r) * 128:(16 + r) * 128],
                    "v": ot_bf.ap()[:, 512 + r * 256:512 + (r + 1) * 256],
                    "gT": ff[512 + r * 256:512 + (r + 1) * 256], "baT": ff[1024:1040],
                    "w2a": (w2.ap()[j][:, r * 128:(r + 1) * 128], gb.ap()[j][:, r * 128:(r + 1) * 128]),
                    "cst": lcst.ap(), "oT": mixT.ap()[512 + r * 256:512 + (r + 1) * 256]})
        else:
            for r in range(2):
                p.phase(f"R{layer}{r}")
                build_L("ret", p=p, io={
                    "qT": fb[(0 + r) * 128:(1 + r) * 128], "kT": fb[(2 + r) * 128:(3 + r) * 128],
                    "qsT": fb[(4 + r) * 128:(5 + r) * 128], "ksT": fb[(6 + r) * 128:(7 + r) * 128],
                    "v": ot_bf.ap()[:, r * 256:(r + 1) * 256], "gT": ff[r * 256:(r + 1) * 256],
                    "tabs": tabs.ap()[r], "decc": decc.ap()[r], "cst": lcst.ap(),
                    "oT": mixT.ap()[r * 256:(r + 1) * 256]})
            for r in range(2):
                p.phase(f"D{layer}{r}")
                build_D(p=p, io={
                    "qT": hd3(fb[8 * 128 + r * 256:8 * 128 + (r + 1) * 256]),
                    "kT": hd3(fb[12 * 128 + r * 256:12 * 128 + (r + 1) * 256]),
                    "v": ot_bf.ap()[:, 512 + r * 256:512 + (r + 1) * 256],
                    "gT": hd3(ff[512 + r * 256:512 + (r + 1) * 256]),
                    "masks": dmasks.ap(), "ut": dut.ap(),
                    "oT": hd3(mixT.ap()[512 + r * 256:512 + (r + 1) * 256])})
    p.phase("F")
    build_P(PPAD, 0, [], [], [], [], with_out=True, final=True, p=p,
            io={"hT": h_cur, "g": gs.ap()[4], "mixT": mixT.ap(), "wo": wo_cd.ap()[1], "hT_new": None, "y": y.ap()})
    p.finish()
    return p.emit()


def kernel_unfused(**kw):
    return _kernel_unfused(**kw)


_kernel_unfused = kernel


def kernel(x, meta_tokens, rel_bias, norm_g, final_g, w_in_ab, gla_gate_w2, gla_gate_b,
           w_out_ab, w_in_cd, w_out_cd):
    f32 = lambda a: np.ascontiguousarray(np.asarray(a, np.float32))
    x = f32(x)
    rel_bias = f32(rel_bias)
    if "FUSED" not in _PROGS:
        _PROGS["FUSED"] = build_fused()
    nc = _PROGS["FUSED"]
    gs = np.stack([_g_layout(norm_g[l]) for l in range(4)] + [_g_layout(final_g)], axis=0)
    ac = [a_consts(r, rel_bias) for r in range(2)]
    rt = [ret_tables(r) for r in range(2)]
    dmasks, dut = d_consts()
    shared = {
        "gs": gs, "w_ab": f32(w_in_ab), "w_cdx": np.stack([_cd_weights(w_in_cd[j]) for j in range(2)], 0),
        "wo_ab": f32(w_out_ab), "wo_cd": f32(w_out_cd), "w2": f32(gla_gate_w2),
        "gb": f32(gla_gate_b).reshape(2, 1, 256),
        "mskd": np.stack([ac[0][0], ac[1][0]], 0), "btd": np.stack([ac[0][1], ac[1][1]], 0),
        "b15d": ac[0][2], "cstd": ac[0][3], "lcst": l_consts(),
        "tabs": np.stack([rt[0][0], rt[1][0]], 0), "decc": np.stack([rt[0][1], rt[1][1]], 0),
        "dmasks": dmasks, "dut": dut,
    }
    maps = []
    for c in range(NCORES):
        b = c // 2
        h = np.zeros((PPAD, D_MODEL), np.float32)
        h[PADF:128] = np.asarray(meta_tokens, np.float32)
        h[128:128 + SEQ] = x[b]
        m = dict(shared)
        m["hT0"] = np.ascontiguousarray(h.T)
        maps.append(m)
    res = run_bass_kernel_spmd(nc, maps, core_ids=list(range(NCORES))).results
    out = np.stack([np.ascontiguousarray(res[2 * b]["y"][:, 128:128 + SEQ].T) for b in range(BATCH)], axis=0)
    return out.astype(np.float32)
```

```python
import contextlib
import numpy as np
import concourse.bass as bass
import concourse.mybir as mybir
from concourse.bass_utils import run_bass_kernel_spmd

F32 = mybir.dt.float32
BF16 = mybir.dt.bfloat16
ALU = mybir.AluOpType
AF = mybir.ActivationFunctionType
AX = mybir.AxisListType

D_MODEL = 1024
BATCH = 4
SEQ = 8192
PADF = 112
NMETA = 16
PTOK = SEQ + 128
NBLK = 66
PPAD = NBLK * 128
NCORES = 8
W_AB = 4184
W_CD = 3584
TOPK = 256
NEG = -1.0e30
LDBG = 9
EPS = 1e-6
SEM_ROLL = 30000

ENGS = ("pe", "act", "dve", "pool", "sp")


class Prog:
    def __init__(self, num_devices=None):
        if num_devices:
            self.nc = bass.Bass("TRN2", target_bir_lowering=False, num_devices=num_devices)
        else:
            self.nc = bass.Bass("TRN2", target_bir_lowering=False)
        self.q = {e: [] for e in ENGS}
        self.cnt = {e: 0 for e in ENGS}
        self.esem = {e: None for e in ENGS}
        self.keys = {}
        self.waited = {e: {} for e in ENGS}
        self.dsem = {}
        self.all_dma = []
        self.nsem = 0
        self.allsems = []
        self.fused = False
        self.prefix = ""
        self.sb_off = 16640
        self.banks = None
        self.bank_i = 0

    def new_sem(self, name):
        self.nsem += 1
        s = self.nc.alloc_semaphore(f"{name}_{self.nsem}")
        self.allsems.append(s)
        return s

    def dram(self, name, shape, dtype, kind="Internal"):
        return self.nc.dram_tensor(name, list(shape), dtype, kind=kind)

    def sb(self, name, shape, dtype):
        if not self.fused:
            return self.nc.alloc_sbuf_tensor(name, list(shape), dtype)
        nbytes = int(np.prod(shape[1:])) * (2 if dtype is BF16 else 4)
        nbytes = (nbytes + 31) // 32 * 32
        off = self.sb_off
        self.sb_off += nbytes
        assert self.sb_off <= 229120, (name, self.sb_off)
        self.sb_max = max(getattr(self, 'sb_max', 0), self.sb_off)
        return self.nc.alloc_sbuf_tensor_at(self.prefix + name, list(shape), dtype, offset=off)

    def ps(self, name, shape, dtype=F32):
        if not self.fused:
            return self.nc.alloc_psum_tensor(name, list(shape), dtype)
        if self.banks is None:
            self.banks = [self.nc.alloc_psum_tensor(f"bank{i}", [128, 512], F32) for i in range(8)]
        b = self.banks[self.bank_i]
        self.bank_i += 1
        assert self.bank_i <= 8
        return b

    def phase(self, name):
        toks = []
        for e in ENGS:
            if self.esem[e] is not None and self.cnt[e] > 0:
                toks.append((self.esem[e], self.cnt[e]))
        for ent in self.dsem.values():
            toks.append((ent[0], ent[1]))
        for e in ENGS:
            waits = []
            for sem, val in toks:
                if e == "pe" and sem is self.esem["pe"]:
                    continue
                if self.waited[e].get(id(sem), 0) >= val:
                    continue
                self.waited[e][id(sem)] = val
                waits.append((sem, val))
            if waits:
                self.q[e].append((waits, None, None, 0))
        self.prefix = name + "_"
        self.sb_off = 16640
        self.bank_i = 0

    def _deps(self, eng, reads, writes):
        deps = []
        for k in reads:
            st = self.keys.get(k)
            if st and st["w"]:
                deps.append(st["w"])
        for k in writes:
            st = self.keys.get(k)
            if st:
                if st["w"]:
                    deps.append(st["w"])
                deps.extend(st["r"])
        best = {}
        for sem, val in deps:
            if eng == "pe" and sem is self.esem["pe"]:
                continue
            sid = id(sem)
            if sid not in best or best[sid][1] < val:
                best[sid] = (sem, val)
        out = []
        for sid, (sem, val) in best.items():
            if self.waited[eng].get(sid, 0) >= val:
                continue
            self.waited[eng][sid] = val
            out.append((sem, val))
        return out

    def _mark(self, reads, writes, tok):
        for k in reads:
            st = self.keys.setdefault(k, {"w": None, "r": []})
            st["r"].append(tok)
        for k in writes:
            self.keys[k] = {"w": tok, "r": []}

    def op(self, eng, fn, reads=(), writes=()):
        waits = self._deps(eng, reads, writes)
        if self.esem[eng] is None or self.cnt[eng] >= SEM_ROLL:
            self.esem[eng] = self.new_sem("e" + eng)
            self.cnt[eng] = 0
        self.cnt[eng] += 1
        tok = (self.esem[eng], self.cnt[eng])
        self.q[eng].append((waits, fn, tok[0], 1))
        self._mark(reads, writes, tok)

    def dma(self, eng, out_ap, in_ap, reads=(), writes=(), skey=None):
        waits = self._deps(eng, reads, writes)
        if skey is None:
            skey = (list(writes) + list(reads))[0]
        ent = self.dsem.get(skey)
        if ent is None:
            ent = [self.new_sem("d"), 0]
            self.dsem[skey] = ent
        ent[1] += 16
        tok = (ent[0], ent[1])
        self.q[eng].append((waits, lambda e: e.dma_start(out=out_ap, in_=in_ap), tok[0], 16))
        self._mark(reads, writes, tok)

    def finish(self):
        fin = [(ent[0], ent[1]) for ent in self.dsem.values()]
        self.q["pool"].append((fin, None, None, 0))
        self.q["sp"].append((fin, None, None, 0))

    def emit(self):
        nc = self.nc

        def run(e, lst):
            for waits, fn, sem, inc in lst:
                for s, v in waits:
                    e.wait_ge(s, v)
                if fn is not None:
                    fn(e).then_inc(sem, inc)

        with nc.Block() as block:
            @block.tensor
            def _(e):
                run(e, self.q["pe"])

            @block.scalar
            def _(e):
                run(e, self.q["act"])

            @block.vector
            def _(e):
                run(e, self.q["dve"])

            @block.gpsimd
            def _(e):
                run(e, self.q["pool"])

            @block.sync
            def _(e):
                run(e, self.q["sp"])
        return nc


def build_P(T, WT, fchunks_bf, fchunks_f32, tgroups_bf, tgroups_f32, with_out, final=False, p=None, io=None,
            tdst=None):
    own = p is None
    NT = 384
    assert T % NT == 0
    ntile = T // NT
    KC = 8
    tdst = tdst or {}
    if own:
        p = Prog()
        A = {}
        A["hT"] = p.dram("hT", [D_MODEL, T], F32, kind="ExternalInput").ap()
        A["g"] = p.dram("g", [128, KC], F32, kind="ExternalInput").ap()
        if not final:
            A["w"] = p.dram("w", [D_MODEL, WT], F32, kind="ExternalInput").ap()
        if with_out:
            A["mixT"] = p.dram("mixT", [D_MODEL, T], BF16, kind="ExternalInput").ap()
            A["wo"] = p.dram("wo", [D_MODEL, D_MODEL], F32, kind="ExternalInput").ap()
            A["hT_new"] = p.dram("hT_new", [D_MODEL, T], F32, kind="ExternalOutput").ap()
        if final:
            A["y"] = p.dram("y", [D_MODEL, T], F32, kind="ExternalOutput").ap()
        else:
            nfb, nff = len(fchunks_bf), len(fchunks_f32)
            ctb = sum(n for _, n in tgroups_bf)
            ctf = sum(n for _, n in tgroups_f32)
            A["of_bf"] = p.dram("of_bf", [max(nfb, 1) * 128, T], BF16, kind="ExternalOutput").ap()
            A["of_f32"] = p.dram("of_f32", [max(nff, 1) * 128, T], F32, kind="ExternalOutput").ap()
            A["ot_bf"] = p.dram("ot_bf", [T, max(ctb, 1)], BF16, kind="ExternalOutput").ap()
            A["ot_f32"] = p.dram("ot_f32", [T, max(ctf, 1)], F32, kind="ExternalOutput").ap()
    else:
        A = io
    of_bf, of_f32, ot_bf, ot_f32 = A.get("of_bf"), A.get("of_f32"), A.get("ot_bf"), A.get("ot_f32")

    g_sb = p.sb("g_sb", [128, KC], F32)
    ones = p.sb("ones", [128, 128], F32)
    if not final:
        w_sb = p.sb("w_sb", [128, KC, WT], BF16)
        stg = [p.sb(f"stg{i}", [128, KC, 512], F32) for i in range(2)]
    if with_out:
        wo_sb = p.sb("wo_sb", [128, KC, D_MODEL], BF16)
        mix_sb = [p.sb(f"mix{i}", [128, KC, NT], BF16) for i in range(2)]
        if final:
            stg = [p.sb(f"stg{i}", [128, KC, 512], F32) for i in range(2)]
    h_sb = [p.sb(f"h{i}", [128, KC, NT], F32) for i in range(2)]
    sq_sb = p.sb("sq", [128, NT], F32)
    rstd = p.sb("rstd", [128, NT], F32)
    hn_sb = [p.sb(f"hn{i}", [128, KC, NT], BF16) for i in range(2)]
    NEV = 4
    ev_bf = [p.sb(f"evb{i}", [128, 512], BF16) for i in range(NEV)]
    ev_f = [p.sb(f"evf{i}", [128, 512], F32) for i in range(NEV)]
    pst = [p.ps(f"ps{i}", [128, 512], F32) for i in range(6)]
    ps_ss = p.ps("ps_ss", [128, 512], F32)

    p.dma("sp", g_sb[:, :], A["g"], writes=["g_sb"])
    p.op("pool", lambda e: e.memset(ones[:, :], 1.0), writes=["ones"])

    def load_w(dram_w, sb_w, ncols, tag):
        view = dram_w.rearrange("(c p) w -> p c w", p=128)
        npc = (ncols + 511) // 512
        for i in range(npc):
            a = i * 512
            b = min(ncols, a + 512)
            s = stg[i % 2]
            sk = f"stg{i % 2}"
            p.dma("sp", s[:, :, 0:b - a], view[:, :, a:b], writes=[sk])
            eng = "dve" if i % 2 == 0 else "pool"
            p.op(eng, lambda e, s=s, a=a, b=b: e.tensor_copy(out=sb_w[:, :, a:b], in_=s[:, :, 0:b - a]),
                 reads=[sk], writes=[f"{tag}_{i}"])
        return [f"{tag}_{i}" for i in range(npc)]

    wkeys = []
    if with_out:
        wokeys = load_w(A["wo"], wo_sb, D_MODEL, "wo")
    if not final:
        wkeys = load_w(A["w"], w_sb, WT, "w")

    hview = A["hT"].rearrange("(c p) t -> p c t", p=128)
    if with_out:
        mview = A["mixT"].rearrange("(c p) t -> p c t", p=128)
        if not final:
            hnview = A["hT_new"].rearrange("(c p) t -> p c t", p=128)
    if final:
        yview = A["y"].rearrange("(c p) t -> p c t", p=128)

    evi = [0]
    psi = [0]

    def next_ps():
        i = psi[0] % len(pst)
        psi[0] += 1
        return pst[i], f"ps{i}"

    def evac(ps_ap, pkey, nparts, ncols, dtype, dram_ap):
        i = evi[0] % NEV
        evi[0] += 1
        if dtype is BF16:
            t, tk = ev_bf[i], f"evb{i}"
        else:
            t, tk = ev_f[i], f"evf{i}"
        if evi[0] % 2 == 0:
            p.op("act", lambda e: e.activation(out=t[0:nparts, 0:ncols], in_=ps_ap, func=AF.Copy),
                 reads=[pkey], writes=[tk])
        else:
            p.op("dve", lambda e: e.tensor_copy(out=t[0:nparts, 0:ncols], in_=ps_ap),
                 reads=[pkey], writes=[tk])
        src = t[0:nparts, 0:ncols]
        if len(dram_ap.shape) == 3:
            src = src.rearrange("p (h d) -> p h d", h=dram_ap.shape[1])
        p.dma("pool", dram_ap, src, reads=[tk], skey=tk)

    for ti in range(ntile):
        t0 = ti * NT
        hb, hk = h_sb[ti % 2], f"h{ti % 2}"
        hnb, hnk = hn_sb[ti % 2], f"hn{ti % 2}"
        p.dma("sp", hb[:, :, :], hview[:, :, t0:t0 + NT], writes=[hk])
        if with_out:
            mb, mk = mix_sb[ti % 2], f"mix{ti % 2}"
            p.dma("sp", mb[:, :, :], mview[:, :, t0:t0 + NT], writes=[mk])
            for oc in range(KC):
                ps, pk = next_ps()
                for c in range(KC):
                    p.op("pe", lambda e, ps=ps, c=c, oc=oc, mb=mb: e.matmul(
                        ps[:, 0:NT], lhsT=wo_sb[:, c, oc * 128:(oc + 1) * 128], rhs=mb[:, c, :],
                        start=(c == 0), stop=(c == KC - 1)),
                        reads=[mk] + wokeys, writes=[pk])
                p.op("dve", lambda e, ps=ps, oc=oc, hb=hb: e.tensor_tensor(
                    out=hb[:, oc, :], in0=hb[:, oc, :], in1=ps[:, 0:NT], op=ALU.add),
                    reads=[pk, hk], writes=[hk])
            if not final:
                p.dma("pool", hnview[:, :, t0:t0 + NT], hb[:, :, :], reads=[hk], skey=hk + "_st")
        for c in range(KC):
            p.op("act", lambda e, c=c, hb=hb: e.activation(out=sq_sb[:, :], in_=hb[:, c, :], func=AF.Square),
                 reads=[hk], writes=["sq"])
            p.op("pe", lambda e, c=c: e.matmul(ps_ss[:, 0:NT], lhsT=ones[:, :], rhs=sq_sb[:, :],
                                                 start=(c == 0), stop=(c == KC - 1)),
                 reads=["sq", "ones"], writes=["ps_ss"])
        p.op("act", lambda e: e.activation(out=rstd[:, :], in_=ps_ss[:, 0:NT], func=AF.Sqrt,
                                           bias=EPS, scale=1.0 / D_MODEL),
             reads=["ps_ss"], writes=["rstd"])
        p.op("dve", lambda e: e.reciprocal(out=rstd[:, :], in_=rstd[:, :]),
             reads=["rstd"], writes=["rstd"])
        if final:
            for c in range(KC):
                p.op("dve", lambda e, c=c, hb=hb: e.scalar_tensor_tensor(
                    out=hb[:, c, :], in0=hb[:, c, :], scalar=g_sb[:, c:c + 1], in1=rstd[:, :],
                    op0=ALU.mult, op1=ALU.mult), reads=[hk, "rstd", "g_sb"], writes=[hk])
            p.dma("pool", yview[:, :, t0:t0 + NT], hb[:, :, :], reads=[hk], skey=hk + "_st")
            continue
        for c in range(KC):
            p.op("dve", lambda e, c=c, hb=hb, hnb=hnb: e.scalar_tensor_tensor(
                out=hnb[:, c, :], in0=hb[:, c, :], scalar=g_sb[:, c:c + 1], in1=rstd[:, :],
                op0=ALU.mult, op1=ALU.mult), reads=[hk, "rstd", "g_sb"], writes=[hnk])
        for lst, dt_, dram_o in ((fchunks_bf, BF16, of_bf), (fchunks_f32, F32, of_f32)):
            for ci, (c0, ncol) in enumerate(lst):
                ps, pk = next_ps()
                for c in range(KC):
                    p.op("pe", lambda e, ps=ps, c=c, c0=c0, ncol=ncol, hnb=hnb: e.matmul(
                        ps[0:ncol, 0:NT], lhsT=w_sb[:, c, c0:c0 + ncol], rhs=hnb[:, c, :],
                        start=(c == 0), stop=(c == KC - 1)), reads=[hnk] + wkeys, writes=[pk])
                evac(ps[0:ncol, 0:NT], pk, ncol, NT, dt_, dram_o[ci * 128:ci * 128 + ncol, t0:t0 + NT])
        for blk in range(NT // 128):
            for lst, dt_, dram_o in ((tgroups_bf, BF16, ot_bf), (tgroups_f32, F32, ot_f32)):
                oc0 = 0
                for (c0, ncol) in lst:
                    ps, pk = next_ps()
                    for c in range(KC):
                        p.op("pe", lambda e, ps=ps, c=c, c0=c0, ncol=ncol, hnb=hnb, blk=blk: e.matmul(
                            ps[:, 0:ncol], lhsT=hnb[:, c, blk * 128:(blk + 1) * 128], rhs=w_sb[:, c, c0:c0 + ncol],
                            start=(c == 0), stop=(c == KC - 1)), reads=[hnk] + wkeys, writes=[pk])
                    if (dt_ is BF16) and (c0 in tdst):
                        dst_ap = tdst[c0](ti * (NT // 128) + blk)
                    else:
                        dst_ap = dram_o[t0 + blk * 128:t0 + (blk + 1) * 128, oc0:oc0 + ncol]
                    evac(ps[:, 0:ncol], pk, 128, ncol, dt_, dst_ap)
                    oc0 += ncol
    if own:
        p.finish()
        return p.emit()


def sb_list():
    sbs = []
    b = 0
    while b < NBLK:
        nb = min(4, NBLK - b)
        sbs.append((b, nb))
        b += nb
    return sbs


def build_D(p=None, io=None):
    own = p is None
    NH = 4
    if own:
        p = Prog()
        qT = p.dram("qT", [NH, 64, PPAD], BF16, kind="ExternalInput")
        kT = p.dram("kT", [NH, 64, PPAD], BF16, kind="ExternalInput")
        v = p.dram("v", [PPAD, NH * 64], BF16, kind="ExternalInput")
        gT = p.dram("gT", [NH, 64, PPAD], F32, kind="ExternalInput")
        masks = p.dram("masks", [6, 128, 512], BF16, kind="ExternalInput")
        ut = p.dram("ut", [128, 256], BF16, kind="ExternalInput")
        oT = p.dram("oT", [NH, 64, PPAD], BF16, kind="ExternalOutput")
        A_ = {"qT": qT.ap(), "kT": kT.ap(), "v": v.ap(), "gT": gT.ap(), "masks": masks.ap(), "ut": ut.ap(),
              "oT": oT.ap()}
    else:
        A_ = io

    k_sb = p.sb("k_sb", [64, NH, PPAD], BF16)
    v_sb = p.sb("v_sb", [128, NBLK, NH * 64], BF16)
    m_sb = p.sb("m_sb", [128, 6, 512], BF16)
    u_sb = p.sb("u_sb", [128, 256], BF16)
    q_sb = [p.sb(f"q{i}", [64, NH, 512], BF16) for i in range(2)]
    g_sb = [p.sb(f"g{i}", [64, 512], F32) for i in range(2)]
    e1 = [p.sb(f"e1_{i}", [128, 512], F32) for i in range(2)]
    DEP = 3
    NSP = DEP + 2
    NA = DEP + 3
    sp = [p.sb(f"sp{i}", [128, 512], BF16) for i in range(NSP)]
    NW = 3
    wt = [p.sb(f"wt{i}", [128, 512], BF16) for i in range(NW)]
    acc = p.sb("acc", [128, 512], BF16)
    sg = p.sb("sg", [64, 512], F32)
    ob = [p.sb(f"ob{i}", [64, 512], BF16) for i in range(2)]
    psA = [p.ps(f"psA{i}", [128, 512]) for i in range(NA)]
    psC = [p.ps(f"psC{i}", [128, 512]) for i in range(2)]

    for h in range(NH):
        p.dma("sp", k_sb[:, h, :], A_["kT"][h], writes=[f"k{h}"])
    p.dma("sp", v_sb[:, :, :], A_["v"].rearrange("(b p) f -> p b f", p=128), writes=["v_sb"])
    p.dma("sp", m_sb[:, :, :], A_["masks"].rearrange("m p t -> p m t"), writes=["m_sb"])
    p.dma("sp", u_sb[:, :], A_["ut"], writes=["u_sb"])

    sbs = sb_list()
    its = []
    for si, (b0, nb) in enumerate(sbs):
        for h in range(NH):
            kend = b0 + nb - 1
            for kb in range(kend, -1, -1):
                its.append((si, b0, nb, h, kb, kend))
    n_it = len(its)
    cnt = {"ph": -1}

    def mask_idx(b0, kb):
        if kb >= b0:
            m = kb - b0
            if kb == 0:
                return 5
            return m
        if kb == 0:
            return 4
        return None

    def stage1(i):
        si, b0, nb, h, kb, kend = its[i]
        N = nb * 128
        qb, qk = q_sb[si % 2], f"q{si % 2}"
        if h == 0 and kb == kend:
            p.dma("sp", qb[:, :, 0:N], A_["qT"][:, :, b0 * 128:b0 * 128 + N].rearrange("h d t -> d h t"),
                  writes=[qk])
        A, ak = psA[i % NA], f"psA{i % NA}"
        p.op("pe", lambda e: e.matmul(A[:, 0:N], lhsT=k_sb[:, h, kb * 128:(kb + 1) * 128], rhs=qb[:, h, 0:N],
                                      start=True, stop=False), reads=[f"k{h}", qk], writes=[ak])
        E, ek = e1[i % 2], f"e1_{i % 2}"
        p.op("act", lambda e: e.activation(out=E[:, 0:N], in_=A[:, 0:N], func=AF.Exp, scale=0.125),
             reads=[ak], writes=[ek])
        S, sk = sp[i % NSP], f"sp{i % NSP}"
        p.op("act", lambda e: e.activation(out=S[:, 0:N], in_=E[:, 0:N], func=AF.Ln, bias=1.0, scale=1.0),
             reads=[ek], writes=[sk])
        mi = mask_idx(b0, kb)
        if mi is not None:
            p.op("pool", lambda e: e.tensor_tensor(out=S[:, 0:N], in0=S[:, 0:N], in1=m_sb[:, mi, 0:N], op=ALU.mult),
                 reads=[sk, "m_sb"], writes=[sk])

    def stage2(i):
        si, b0, nb, h, kb, kend = its[i]
        N = nb * 128
        qb, qk = q_sb[si % 2], f"q{si % 2}"
        S, sk = sp[i % NSP], f"sp{i % NSP}"
        B, bk = psA[i % NA], f"psA{i % NA}"
        hi = si * NH + h
        C, ck = psC[hi % 2], f"psC{hi % 2}"
        first = (kb == kend)
        p.op("pe", lambda e: e.matmul(B[:, 0:N], lhsT=u_sb[:, 0:128], rhs=S[:, 0:N],
                                      start=False, stop=first), reads=[sk, "u_sb"], writes=[bk])
        if not first:
            p.op("pe", lambda e: e.matmul(B[:, 0:N], lhsT=u_sb[:, 128:256], rhs=acc[:, 0:N],
                                          start=False, stop=True), reads=["acc", "u_sb"], writes=[bk])
        W, wk = wt[i % NW], f"wt{i % NW}"
        p.op("act", lambda e: e.activation(out=W[:, 0:N], in_=B[:, 0:N], func=AF.Exp, scale=0.125),
             reads=[bk], writes=[wk])
        mi = mask_idx(b0, kb)
        if mi is not None:
            p.op("pool", lambda e: e.tensor_tensor(out=W[:, 0:N], in0=W[:, 0:N], in1=m_sb[:, mi, 0:N], op=ALU.mult),
                 reads=[wk, "m_sb"], writes=[wk])
        if first:
            p.op("dve", lambda e: e.tensor_copy(out=acc[:, 0:N], in_=S[:, 0:N]), reads=[sk], writes=["acc"])
        elif kb > 0:
            p.op("dve", lambda e: e.tensor_tensor(out=acc[:, 0:N], in0=acc[:, 0:N], in1=S[:, 0:N], op=ALU.add),
                 reads=[sk, "acc"], writes=["acc"])

    def stage2b(i):
        si, b0, nb, h, kb, kend = its[i]
        N = nb * 128
        hi = si * NH + h
        C, ck = psC[hi % 2], f"psC{hi % 2}"
        first = (kb == kend)
        W, wk = wt[i % NW], f"wt{i % NW}"
        p.op("pe", lambda e: e.matmul(C[0:64, 0:N], lhsT=v_sb[:, kb, h * 64:(h + 1) * 64], rhs=W[:, 0:N],
                                      start=first, stop=(kb == 0)), reads=[wk, "v_sb"], writes=[ck])
        if kb == 0:
            G, gk = g_sb[hi % 2], f"g{hi % 2}"
            p.dma("sp", G[:, 0:N], A_["gT"][h, :, b0 * 128:b0 * 128 + N], writes=[gk])
            p.op("act", lambda e: e.activation(out=sg[:, 0:N], in_=G[:, 0:N], func=AF.Exp, scale=-1.0),
                 reads=[gk], writes=["sg"])
            p.op("dve", lambda e: e.tensor_scalar(out=sg[:, 0:N], in0=sg[:, 0:N], scalar1=1.0, scalar2=None,
                                                  op0=ALU.add), reads=["sg"], writes=["sg"])
            p.op("dve", lambda e: e.reciprocal(out=sg[:, 0:N], in_=sg[:, 0:N]), reads=["sg"], writes=["sg"])
            p.op("dve", lambda e: e.tensor_tensor(out=sg[:, 0:N], in0=sg[:, 0:N], in1=G[:, 0:N], op=ALU.mult),
                 reads=["sg", gk], writes=["sg"])
            O, ok_ = ob[hi % 2], f"ob{hi % 2}"
            p.op("dve", lambda e: e.tensor_tensor(out=O[:, 0:N], in0=C[0:64, 0:N], in1=sg[:, 0:N], op=ALU.mult),
                 reads=[ck, "sg"], writes=[ok_])
            p.dma("pool", A_["oT"][h, :, b0 * 128:b0 * 128 + N], O[:, 0:N], reads=[ok_], skey=ok_)

    for i in range(n_it + DEP + 1):
        if i < n_it:
            stage1(i)
        if 0 <= i - DEP < n_it:
            stage2(i - DEP)
        if 0 <= i - DEP - 1 < n_it:
            stage2b(i - DEP - 1)
    if own:
        p.finish()
        return p.emit()


def d_consts():
    import ml_dtypes
    masks = np.zeros((6, 128, 512), np.float32)
    s = np.arange(128)[:, None]
    t = np.arange(512)[None, :]
    for m in range(4):
        j = t // 128
        tl = t % 128
        masks[m] = np.where(j > m, 1.0, np.where(j == m, (tl > s).astype(np.float32), 0.0))
    masks[4] = (s >= PADF).astype(np.float32) * np.ones((128, 512), np.float32)
    masks[5] = masks[0] * masks[4]
    ut = np.zeros((128, 256), np.float32)
    jj = np.arange(128)[:, None]
    ss = np.arange(128)[None, :]
    ut[:, 0:128] = np.where(jj >= ss, -8.0, 0.0)
    ut[:, 128:256] = -8.0
    return masks.astype(ml_dtypes.bfloat16), ut.astype(ml_dtypes.bfloat16)


def build_L(kind, p=None, io=None):
    own = p is None
    gla = (kind == "gla")
    if own:
        p = Prog()
        A_ = {}
        A_["qT"] = p.dram("qT", [128, PPAD], BF16, kind="ExternalInput").ap()
        A_["kT"] = p.dram("kT", [128, PPAD], BF16, kind="ExternalInput").ap()
        A_["v"] = p.dram("v", [PPAD, 256], BF16, kind="ExternalInput").ap()
        A_["gT"] = p.dram("gT", [256, PPAD], F32, kind="ExternalInput").ap()
        A_["cst"] = p.dram("cst", [128, 3, 256], F32, kind="ExternalInput").ap()
        if gla:
            A_["baT"] = p.dram("baT", [16, PPAD], F32, kind="ExternalInput").ap()
            A_["w2a"] = p.dram("w2a", [17, 128], F32, kind="ExternalInput").ap()
        else:
            A_["qsT"] = p.dram("qsT", [128, PPAD], BF16, kind="ExternalInput").ap()
            A_["ksT"] = p.dram("ksT", [128, PPAD], BF16, kind="ExternalInput").ap()
            A_["tabs"] = p.dram("tabs", [4, 128, PPAD], F32, kind="ExternalInput").ap()
            A_["decc"] = p.dram("decc", [64, 2], F32, kind="ExternalInput").ap()
        A_["oT"] = p.dram("oT", [256, PPAD], BF16, kind="ExternalOutput").ap()
    else:
        A_ = io

    cst_sb = p.sb("cst_sb", [128, 3, 256], F32)
    idb = p.sb("idb", [64, 64], BF16)
    m4 = p.sb("m4", [64, 256], BF16)
    ones = p.sb("ones", [128, 128], F32)
    S = p.sb("S", [64, 2, 128], F32)
    Sb = p.sb("Sb", [64, 2, 128], BF16)
    SbX = [Sb, p.sb("Sb1", [64, 2, 128], BF16)]
    kvd = p.sb("kvd", [64, 4, 128], F32)
    NB2 = 2
    NBI = 4
    q_in = [p.sb(f"q_in{i}", [64, 2, 128], BF16) for i in range(NBI)]
    k_in = [p.sb(f"k_in{i}", [64, 2, 128], BF16) for i in range(NBI)]
    v_in = [p.sb(f"v_in{i}", [64, 2, 256], BF16) for i in range(NBI)]
    g_in = [p.sb(f"g_in{i}", [128, 2, 128], F32) for i in range(NBI)]
    qt = [p.sb(f"qt{i}", [64, 2, 128], BF16) for i in range(NB2)]
    kt = [p.sb(f"kt{i}", [64, 2, 128], BF16) for i in range(NB2)]
    ktt = [p.sb(f"ktt{i}", [64, 256], BF16) for i in range(NB2)]
    dec = [p.sb(f"dec{i}", [64, 4], F32) for i in range(NB2)]
    attm = [p.sb(f"attm{i}", [64, 256], BF16) for i in range(NB2)]
    if gla:
        ba_in = [p.sb(f"ba_in{i}", [17, 128], F32) for i in range(NBI)]
        w2_sb = p.sb("w2_sb", [17, 128], F32)
        la = p.sb("la", [128, 128], F32)
        ep = p.sb("ep", [64, 256], F32)
        em = p.sb("em", [64, 256], F32)
    else:
        qs_in = [p.sb(f"qs_in{i}", [64, 2, 128], BF16) for i in range(NBI)]
        ks_in = [p.sb(f"ks_in{i}", [64, 2, 128], BF16) for i in range(NBI)]
        tb_in = [p.sb(f"tb_in{i}", [64, 4, 2, 128], F32) for i in range(NBI)]
        t1 = p.sb("t1", [64, 256], F32)
        t2 = p.sb("t2", [64, 256], F32)
        dec_c = p.sb("dec_c", [64, 2], F32)
    ofs = [p.sb(f"of{i}", [128, 256], F32) for i in range(2)]
    osq = p.sb("osq", [128, 256], F32)
    rs = p.sb("rs", [128, 256], F32)
    sg = p.sb("sg", [128, 256], F32)
    yb = [p.sb(f"yb{i}", [128, 2, 128], BF16) for i in range(2)]
    ln8t = p.sb("ln8t", [128, 1], F32)
    epst = p.sb("epst", [128, 1], F32)

    ps_x = p.ps("ps_x", [128, 512])
    ps_c = p.ps("ps_c", [128, 512])
    ps_t = p.ps("ps_t", [128, 512])
    ps_a = p.ps("ps_a", [128, 512])
    ps_o = p.ps("ps_o", [128, 512])
    ps_kv = p.ps("ps_kv", [128, 512])
    ps_n = p.ps("ps_n", [128, 512])

    p.dma("sp", cst_sb[:, :, :], A_["cst"], writes=["cst"])
    p.op("dve", lambda e: e.tensor_copy(out=m4[:, :], in_=cst_sb[0:64, 1, :]), reads=["cst"], writes=["m4"])
    p.op("dve", lambda e: e.tensor_copy(out=idb[:, :], in_=cst_sb[0:64, 2, 0:64]), reads=["cst"], writes=["idb"])
    p.op("pool", lambda e: e.memset(ones[:, :], 1.0 / 128.0), writes=["ones"])
    p.op("pool", lambda e: e.memset(S[:, :, :], 0.0), writes=["S0", "S1"])
    p.op("pool", lambda e: e.memset(SbX[0][:, :, :], 0.0), writes=["Sb0"])
    p.op("pool", lambda e: e.memset(SbX[1][:, :, :], 0.0), writes=["Sb1"])
    p.op("pool", lambda e: e.memset(ln8t[:, :], float(np.log(0.125))), writes=["ln8"])
    p.op("pool", lambda e: e.memset(epst[:, :], EPS), writes=["epsn"])
    if gla:
        if isinstance(A_["w2a"], tuple):
            p.dma("sp", w2_sb[0:16, :], A_["w2a"][0], writes=["w2"])
            p.dma("sp", w2_sb[16:17, :], A_["w2a"][1], writes=["w2"])
        else:
            p.dma("sp", w2_sb[:, :], A_["w2a"], writes=["w2"])
        for i in range(NBI):
            p.op("pool", lambda e, i=i: e.memset(ba_in[i][:, :], 1.0), writes=[f"ba_in{i}"])
    else:
        p.dma("sp", dec_c[:, :], A_["decc"], writes=["dec_c"])

    gview = A_["gT"].rearrange("(h e) t -> e h t", h=2)
    oview = A_["oT"].rearrange("(h e) t -> e h t", h=2)
    hd = lambda ap_: ap_.rearrange("(h d) t -> d h t", h=2)

    def load(b):
        i = b % NBI
        c0 = b * 128
        p.dma("sp", q_in[i][:, :, :], hd(A_["qT"])[:, :, c0:c0 + 128], writes=[f"q_in{i}"])
        p.dma("sp", k_in[i][:, :, :], hd(A_["kT"])[:, :, c0:c0 + 128], writes=[f"k_in{i}"])
        p.dma("sp", v_in[i][:, :, :], A_["v"][c0:c0 + 128, :].rearrange("(n s) f -> s n f", n=2), writes=[f"v_in{i}"])
        p.dma("sp", g_in[i][:, :, :], gview[:, :, c0:c0 + 128], writes=[f"g_in{i}"])
        if gla:
            p.dma("sp", ba_in[i][0:16, :], A_["baT"][:, c0:c0 + 128], writes=[f"ba_in{i}"])
        else:
            p.dma("sp", qs_in[i][:, :, :], hd(A_["qsT"])[:, :, c0:c0 + 128], writes=[f"qs_in{i}"])
            p.dma("sp", ks_in[i][:, :, :], hd(A_["ksT"])[:, :, c0:c0 + 128], writes=[f"ks_in{i}"])
            for f in range(4):
                p.dma("sp", tb_in[i][:, f, :, :], hd(A_["tabs"][f])[:, :, c0:c0 + 128], writes=[f"tb_in{i}"])

    fl = lambda t_: t_.rearrange("p h t -> p (h t)")

    def prep(b):
        i = b % NB2
        ii = b % NBI
        if gla:
            p.op("pe", lambda e: e.matmul(ps_x[:, 0:128], lhsT=ba_in[ii][:, :], rhs=w2_sb[:, :], start=True, stop=True),
                 reads=[f"ba_in{ii}", "w2"], writes=["ps_x"])
            p.op("act", lambda e: e.activation(out=la[:, :], in_=ps_x[:, 0:128], func=AF.Exp, scale=-1.0),
                 reads=["ps_x"], writes=["la"])
            p.op("act", lambda e: e.activation(out=la[:, :], in_=la[:, :], func=AF.Ln, bias=1.0),
                 reads=["la"], writes=["la"])
            for h in range(2):
                p.op("pe", lambda e, h=h: e.matmul(ps_c[0:64, h * 128:(h + 1) * 128], lhsT=la[:, h * 64:(h + 1) * 64],
                                                   rhs=cst_sb[:, 0, 0:128], start=True, stop=True),
                     reads=["la", "cst"], writes=["ps_c"])
            p.op("act", lambda e: e.activation(out=ep[:, :], in_=ps_c[0:64, 0:256], func=AF.Exp, bias=ln8t[0:64, 0:1]),
                 reads=["ps_c", "ln8"], writes=["ep"])
            p.op("act", lambda e: e.activation(out=em[:, :], in_=ps_c[0:64, 0:256], func=AF.Exp, scale=-1.0),
                 reads=["ps_c"], writes=["em"])
            p.op("act", lambda e: e.activation(out=dec[i][:, :], in_=ps_c[0:64, 63:256:64], func=AF.Exp),
                 reads=["ps_c"], writes=[f"dec{i}"])
            p.op("dve", lambda e: e.tensor_tensor(out=fl(qt[i][:, :, :]), in0=fl(q_in[ii][:, :, :]), in1=ep[:, :], op=ALU.mult),
                 reads=[f"q_in{ii}", "ep"], writes=[f"qt{i}"])
            p.op("dve", lambda e: e.tensor_tensor(out=fl(kt[i][:, :, :]), in0=fl(k_in[ii][:, :, :]), in1=em[:, :], op=ALU.mult),
                 reads=[f"k_in{ii}", "em"], writes=[f"kt{i}"])
        else:
            for (a_in, s_in, fa, fs, dst, dk_) in ((q_in, qs_in, 0, 1, qt, "qt"), (k_in, ks_in, 2, 3, kt, "kt")):
                p.op("dve", lambda e, a_in=a_in, fa=fa: e.tensor_tensor(
                    out=t1[:, :], in0=fl(a_in[ii][:, :, :]), in1=fl(tb_in[ii][:, fa, :, :]), op=ALU.mult),
                    reads=[f"q_in{ii}", f"k_in{ii}", f"tb_in{ii}"], writes=["t1"])
                p.op("pool", lambda e, s_in=s_in, fs=fs: e.tensor_tensor(
                    out=t2[:, :], in0=fl(s_in[ii][:, :, :]), in1=fl(tb_in[ii][:, fs, :, :]), op=ALU.mult),
                    reads=[f"qs_in{ii}", f"ks_in{ii}", f"tb_in{ii}"], writes=["t2"])
                p.op("dve", lambda e, dst=dst: e.tensor_tensor(out=fl(dst[i][:, :, :]), in0=t1[:, :], in1=t2[:, :], op=ALU.add),
                     reads=["t1", "t2"], writes=[f"{dk_}{i}"])
        for h in range(2):
            for n in range(2):
                j = h * 2 + n
                p.op("pe", lambda e, h=h, n=n, j=j: e.matmul(ps_t[0:64, j * 64:(j + 1) * 64], lhsT=kt[i][:, h, n * 64:(n + 1) * 64],
                                                            rhs=idb[:, :], start=True, stop=True),
                     reads=[f"kt{i}", "idb"], writes=["ps_t"])
        p.op("act", lambda e: e.activation(out=ktt[i][:, :], in_=ps_t[0:64, 0:256], func=AF.Copy),
             reads=["ps_t"], writes=[f"ktt{i}"])

    def core(b):
        i = b % NB2
        ii = b % NBI
        c0 = b * 128
        for h in range(2):
            for n in range(2):
                j = h * 2 + n
                p.op("pe", lambda e, h=h, n=n, j=j: e.matmul(
                    ps_a[0:64, j * 64:(j + 1) * 64], lhsT=kt[i][:, h, n * 64:(n + 1) * 64], rhs=qt[i][:, h, n * 64:(n + 1) * 64],
                    start=True, stop=True), reads=[f"kt{i}", f"qt{i}"], writes=["ps_a"])
        p.op("dve", lambda e: e.tensor_tensor(out=attm[i][:, :], in0=ps_a[0:64, 0:256], in1=m4[:, :], op=ALU.mult),
             reads=["ps_a", "m4"], writes=[f"attm{i}"])
        for n in range(2):
            for h in range(2):
                j = h * 2 + n
                j4 = n * 2 + h
                p.op("pe", lambda e, n=n, h=h, j=j, j4=j4: e.matmul(
                    ps_kv[0:64, j4 * 128:(j4 + 1) * 128], lhsT=ktt[i][:, j * 64:(j + 1) * 64], rhs=v_in[ii][:, n, h * 128:(h + 1) * 128],
                    start=True, stop=True), reads=[f"ktt{i}", f"v_in{ii}"], writes=["ps_kv"])
        dkey = f"dec{i}" if gla else "dec_c"
        for n in range(2):
            for h in range(2):
                j = h * 2 + n
                j4 = n * 2 + h
                dsc = dec[i][:, j:j + 1] if gla else dec_c[:, h:h + 1]
                p.op("act", lambda e, dsc=dsc, j4=j4: e.activation(out=kvd[:, j4, :], in_=ps_kv[0:64, j4 * 128:(j4 + 1) * 128],
                                                                   func=AF.Copy, scale=dsc),
                     reads=["ps_kv", dkey], writes=[f"kvd{j4}"])
        for n in range(2):
            cn = 2 * b + n
            SBc, sbk = SbX[cn % 2], f"Sb{cn % 2}"
            SBn, sbnk = SbX[(cn + 1) % 2], f"Sb{(cn + 1) % 2}"
            for h in range(2):
                j = h * 2 + n
                oc = slice(h * 128 + n * 64, h * 128 + (n + 1) * 64)
                p.op("pe", lambda e, n=n, h=h, j=j, oc=oc: e.matmul(
                    ps_o[:, oc], lhsT=v_in[ii][:, n, h * 128:(h + 1) * 128], rhs=attm[i][:, j * 64:(j + 1) * 64],
                    start=True, stop=False), reads=[f"v_in{ii}", f"attm{i}"], writes=["ps_o"])
                p.op("pe", lambda e, n=n, h=h, oc=oc, SBc=SBc: e.matmul(
                    ps_o[:, oc], lhsT=SBc[:, h, :], rhs=qt[i][:, h, n * 64:(n + 1) * 64], start=False, stop=True),
                    reads=[sbk, f"qt{i}"], writes=["ps_o"])
            for h in range(2):
                j = h * 2 + n
                j4 = n * 2 + h
                dsc = dec[i][:, j:j + 1] if gla else dec_c[:, h:h + 1]
                p.op("dve", lambda e, dsc=dsc, h=h, j4=j4: e.scalar_tensor_tensor(
                    out=S[:, h, :], in0=S[:, h, :], scalar=dsc, in1=kvd[:, j4, :],
                    op0=ALU.mult, op1=ALU.add), reads=[f"S{h}", f"kvd{j4}", dkey], writes=[f"S{h}"])
            p.op("act", lambda e, SBn=SBn: e.activation(out=fl(SBn[:, :, :]), in_=fl(S[:, :, :]), func=AF.Copy),
                 reads=["S0", "S1"], writes=[sbnk])
        OF, ofk = ofs[b % 2], f"of{b % 2}"
        p.op("act", lambda e: e.activation(out=OF[:, :], in_=ps_o[:, 0:256], func=AF.Copy), reads=["ps_o"], writes=[ofk])

    def tail(b):
        ii = b % NBI
        c0 = b * 128
        of, ofk = ofs[b % 2], f"of{b % 2}"
        if not gla:
            p.op("pe", lambda e: e.matmul(ps_n[:, 0:256], lhsT=ones[:, :], rhs=of[:, :], start=True, stop=True),
                 reads=[ofk, "ones"], writes=["ps_n"])
            p.op("dve", lambda e: e.tensor_tensor(out=of[:, :], in0=of[:, :], in1=ps_n[:, 0:256], op=ALU.subtract),
                 reads=[ofk, "ps_n"], writes=[ofk])
        p.op("act", lambda e: e.activation(out=osq[:, :], in_=of[:, :], func=AF.Square), reads=[ofk], writes=["osq"])
        p.op("pe", lambda e: e.matmul(ps_n[:, 256:512], lhsT=ones[:, :], rhs=osq[:, :], start=True, stop=True),
             reads=["osq", "ones"], writes=["ps_n"])
        p.op("act", lambda e: e.activation(out=rs[:, :], in_=ps_n[:, 256:512], func=AF.Sqrt, bias=epst[:, 0:1], scale=1.0),
             reads=["ps_n", "epsn"], writes=["rs"])
        p.op("dve", lambda e: e.reciprocal(out=rs[:, :], in_=rs[:, :]), reads=["rs"], writes=["rs"])
        p.op("dve", lambda e: e.tensor_tensor(out=of[:, :], in0=of[:, :], in1=rs[:, :], op=ALU.mult),
             reads=[ofk, "rs"], writes=[ofk])
        G = fl(g_in[ii][:, :, :])
        p.op("act", lambda e: e.activation(out=sg[:, :], in_=G, func=AF.Exp, scale=-1.0), reads=[f"g_in{ii}"], writes=["sg"])
        p.op("dve", lambda e: e.tensor_scalar(out=sg[:, :], in0=sg[:, :], scalar1=1.0, scalar2=None, op0=ALU.add),
             reads=["sg"], writes=["sg"])
        p.op("dve", lambda e: e.reciprocal(out=sg[:, :], in_=sg[:, :]), reads=["sg"], writes=["sg"])
        p.op("pool", lambda e: e.tensor_tensor(out=sg[:, :], in0=sg[:, :], in1=G, op=ALU.mult),
             reads=["sg", f"g_in{ii}"], writes=["sg"])
        Y, yk = yb[b % 2], f"yb{b % 2}"
        p.op("dve", lambda e: e.tensor_tensor(out=fl(Y[:, :, :]), in0=of[:, :], in1=sg[:, :], op=ALU.mult),
             reads=[ofk, "sg"], writes=[yk])
        p.dma("pool", oview[:, :, c0:c0 + 128], Y[:, :, :], reads=[yk], skey=yk)

    load(0)
    if NBLK > 1:
        load(1)
    prep(0)
    for b in range(NBLK):
        if b + 2 < NBLK:
            load(b + 2)
        if b + 1 < NBLK:
            prep(b + 1)
        core(b)
        if b >= 1:
            tail(b - 1)
    tail(NBLK - 1)
    if own:
        p.finish()
        return p.emit()


def l_consts():
    t = np.arange(128)
    same = (t[:, None] // 64) == (t[None, :] // 64)
    tri = np.where(same & (t[:, None] <= t[None, :]), -1.0 / 16.0, 0.0)
    cst = np.zeros((128, 3, 256), np.float32)
    cst[:, 0, 0:128] = tri
    s_ = np.arange(64)
    m64 = (s_[:, None] <= s_[None, :]).astype(np.float32)
    cst[0:64, 1, :] = np.tile(m64, (1, 4))
    cst[0:64, 2, :] = np.tile(np.eye(64, dtype=np.float32), (1, 4))
    return cst


def ret_tables(r):
    pos = np.arange(PPAD)
    inv = (10000.0 ** (-np.arange(32, dtype=np.float32) / 32)).astype(np.float32)
    ang = (pos.astype(np.float32)[:, None] * inv[None, :]).astype(np.float32).astype(np.float64)
    cos = np.cos(ang).T
    sin = np.sin(ang).T
    Cos = np.concatenate([cos, cos], 0)
    SinS = np.concatenate([-sin, sin], 0)
    c = (pos % 64).astype(np.float64)
    tabs = np.zeros((4, 128, PPAD), np.float64)
    decc = np.zeros((64, 2), np.float64)
    for hl in range(2):
        h = 2 * r + hl
        gam = 1.0 - 2.0 ** (-5.0 - h)
        xi = gam ** (c + 1.0)
        kf = gam ** (-(c + 1.0)) / 8.0
        sl = slice(hl * 64, (hl + 1) * 64)
        tabs[0, sl] = Cos * xi
        tabs[1, sl] = SinS * xi
        tabs[2, sl] = Cos * kf
        tabs[3, sl] = SinS * kf
        decc[:, hl] = gam ** 64
    return tabs.astype(np.float32), decc.astype(np.float32)


NQB = 33
NQ = NQB * 128


def a_groups():
    gs = []
    jj = 0
    while jj < NQB:
        n = min(4, NQB - jj)
        gs.append((jj, n))
        jj += n
    return gs


def build_A(p=None, io=None):
    own = p is None
    NH = 8
    if own:
        p = Prog()
        A_ = {}
        A_["aqT"] = p.dram("aqT", [NH, 64, NQ], BF16, kind="ExternalInput").ap().rearrange("h d (j t) -> h d j t", t=128)
        A_["agT"] = p.dram("agT", [NH, 64, NQ], F32, kind="ExternalInput").ap().rearrange("h d (j t) -> h d j t", t=128)
        A_["iqT"] = p.dram("iqT", [NH, 64, NQ], BF16, kind="ExternalInput").ap().rearrange("h d (j t) -> h d j t", t=128)
        A_["iw"] = p.dram("iw", [128, NQB, 8], F32, kind="ExternalInput").ap()
        A_["akT"] = p.dram("akT", [NH, 64, PPAD], BF16, kind="ExternalInput").ap()
        A_["avh"] = p.dram("avh", [NH, 128, NBLK, 64], BF16, kind="ExternalInput").ap()
        A_["ikT"] = p.dram("ikT", [64, PPAD], BF16, kind="ExternalInput").ap()
        A_["mskd"] = p.dram("mskd", [128, 3, 128], F32, kind="ExternalInput").ap()
        A_["btd"] = p.dram("btd", [128, 3, NH, 128], F32, kind="ExternalInput").ap()
        A_["b15d"] = p.dram("b15d", [128, NH], F32, kind="ExternalInput").ap()
        A_["cstd"] = p.dram("cstd", [128, 160], F32, kind="ExternalInput").ap()
        A_["oT"] = p.dram("oT", [NH, 64, NQ], BF16, kind="ExternalOutput").ap().rearrange("h d (j t) -> h d j t", t=128)
    else:
        A_ = io

    NIT = 26
    MAXKB = 64
    ik_sb = p.sb("ik_sb", [128, PPAD], BF16)
    k_sb = p.sb("k_sb", [128, PPAD], BF16)
    v_sb = p.sb("v_sb", [128, NBLK, 128], BF16)
    mT = p.sb("mT", [128, MAXKB * 512], BF16)
    score = p.sb("score", [128, PPAD], F32)
    junk = p.sb("junk", [128, PPAD], BF16)
    q_sb = p.sb("q_sb", [128, NH, 512], BF16)
    iq_sb = q_sb
    aq_sb = q_sb
    iw_sb = p.sb("iw_sb", [128, NQB, 8], F32)
    msk = p.sb("msk", [128, 3, 128], F32)
    bt = p.sb("bt", [128, 3, NH, 128], F32)
    b15 = p.sb("b15", [128, NH], F32)
    cst = p.sb("cst", [128, 160], F32)
    idb = p.sb("idb", [128, 128], BF16)
    onesb = p.sb("onesb", [128, 128], BF16)
    R = [p.sb(f"R{i}", [128, 512], F32) for i in range(2)]
    mk = [p.sb(f"mk{i}", [128, 512], BF16) for i in range(2)]
    PT = [p.sb(f"PT{i}", [128, 512], BF16) for i in range(3)]
    st = p.sb("st", [128, 8], F32)
    W = p.sb("W", [128, NIT], F32)
    sa = p.sb("sa", [128, 1], F32)
    g_sb = p.sb("g_sb", [64, 512], F32)
    sg = p.sb("sg", [64, 512], F32)
    rd = p.sb("rd", [64, 512], F32)
    ob = [p.sb(f"ob{i}", [64, 512], BF16) for i in range(2)]

    psI = [p.ps(f"psI{i}", [128, 512]) for i in range(2)]
    psT = p.ps("psT", [128, 512])
    psS = [p.ps(f"psS{i}", [128, 512]) for i in range(3)]
    psO = p.ps("psO", [128, 512])
    psD = p.ps("psD", [128, 512])

    p.op("pool", lambda e: e.memset(ik_sb[64:128, :], 0.0), writes=["ik_hi"])
    p.op("pool", lambda e: e.memset(k_sb[64:128, :], 0.0), writes=["k_hi"])
    p.op("pool", lambda e: e.memset(q_sb[64:128, :, :], 0.0), writes=["q_hi"])
    p.op("pool", lambda e: e.memset(v_sb[:, :, :], 0.0), writes=["v_sb"])
    p.dma("sp", ik_sb[0:64, :], A_["ikT"], writes=["ik"])
    p.dma("sp", iw_sb[:, :, :], A_["iw"], writes=["iw"])
    p.dma("sp", msk[:, :, :], A_["mskd"], writes=["msk"])
    p.dma("sp", bt[:, :, :, :], A_["btd"], writes=["bt"])
    p.dma("sp", b15[:, :], A_["b15d"], writes=["b15"])
    p.dma("sp", cst[:, :], A_["cstd"], writes=["cst"])
    p.op("dve", lambda e: e.tensor_copy(out=idb[:, :], in_=cst[:, 0:128]), reads=["cst"], writes=["idb"])
    p.op("pool", lambda e: e.memset(onesb[:, :], 1.0), writes=["onesb"])
    for h in range(NH):
        for w_ in range(3):
            p.op("dve", lambda e, h=h, w_=w_: e.tensor_scalar(out=bt[:, w_, h, :], in0=bt[:, w_, h, :], scalar1=b15[:, h:h + 1],
                                                              scalar2=None, op0=ALU.subtract), reads=["bt", "b15"], writes=["bt"])
    p.op("dve", lambda e: e.tensor_scalar(out=bt[:, :, :, :].rearrange("p a h t -> p (a h t)"),
                                          in0=bt[:, :, :, :].rearrange("p a h t -> p (a h t)"), scalar1=8.0, scalar2=None,
                                          op0=ALU.mult), reads=["bt"], writes=["bt"])

    evi = [0]
    for gi, (jj0, nq) in enumerate(a_groups()):
        N = nq * 128
        g8 = 8 * gi
        nkb = min(8 * gi + 8, NBLK) if nq == 4 else NBLK
        q0 = jj0 * 128
        for hh in range(NH):
            p.dma("sp", q_sb[0:64, hh, 0:N].rearrange("d (j t) -> d j t", t=128), A_["iqT"][hh, :, jj0:jj0 + nq, :], writes=["qsh"])
        p.op("pool", lambda e, nkb=nkb, N=N: e.memset(mT[:, 0:nkb * N], 0.0), writes=["mT"])
        mTv = mT[:, 0:nkb * N].rearrange("p (k t) -> p k t", t=N)
        for j in range(nq):
            jj = jj0 + j
            nk = g8 + 2 * j + 2
            nkeys = nk * 128
            ntl = (nkeys + 511) // 512
            for kt_ in range(ntl):
                c0 = kt_ * 512
                w_ = min(512, nkeys - c0)
                for h in range(NH):
                    ii = evi[0] % 2
                    evi[0] += 1
                    p.op("pe", lambda e, ii=ii, h=h, j=j, c0=c0, w_=w_: e.matmul(
                        psI[ii][:, 0:w_], lhsT=q_sb[:, h, j * 128:(j + 1) * 128], rhs=ik_sb[:, c0:c0 + w_],
                        start=True, stop=True), reads=["qsh", "ik", "ik_hi", "q_hi"], writes=[f"psI{ii}"])
                    p.op("act", lambda e, ii=ii, w_=w_: e.activation(out=R[ii][:, 0:w_], in_=psI[ii][:, 0:w_], func=AF.Relu),
                         reads=[f"psI{ii}"], writes=[f"R{ii}"])
                    if h == 0:
                        p.op("dve", lambda e, ii=ii, c0=c0, w_=w_, jj=jj: e.tensor_scalar(
                            out=score[:, c0:c0 + w_], in0=R[ii][:, 0:w_], scalar1=iw_sb[:, jj, 0:1], scalar2=None,
                            op0=ALU.mult), reads=[f"R{ii}", "iw"], writes=["score"])
                    else:
                        p.op("dve", lambda e, ii=ii, c0=c0, w_=w_, jj=jj, h=h: e.scalar_tensor_tensor(
                            out=score[:, c0:c0 + w_], in0=R[ii][:, 0:w_], scalar=iw_sb[:, jj, h:h + 1],
                            in1=score[:, c0:c0 + w_], op0=ALU.mult, op1=ALU.add),
                            reads=[f"R{ii}", "iw", "score"], writes=["score"])
            p.op("dve", lambda e, nkeys=nkeys: e.reduce_max(out=st[:, 0:1], in_=score[:, 0:nkeys], axis=AX.X,
                                                            apply_absolute_value=True), reads=["score"], writes=["stB"])
            p.op("dve", lambda e, nk=nk: e.tensor_tensor(out=score[:, (nk - 2) * 128:(nk - 1) * 128],
                                                         in0=score[:, (nk - 2) * 128:(nk - 1) * 128], in1=msk[:, 0, :], op=ALU.add),
                 reads=["score", "msk"], writes=["score"])
            p.op("dve", lambda e, nk=nk: e.tensor_tensor(out=score[:, (nk - 1) * 128:nk * 128],
                                                         in0=score[:, (nk - 1) * 128:nk * 128], in1=msk[:, 1, :], op=ALU.add),
                 reads=["score", "msk"], writes=["score"])
            p.op("dve", lambda e: e.tensor_tensor(out=score[:, 0:128], in0=score[:, 0:128], in1=msk[:, 2, :], op=ALU.add),
                 reads=["score", "msk"], writes=["score"])
            p.op("dve", lambda e: e.tensor_scalar(out=st[:, 1:2], in0=st[:, 0:1], scalar1=2.002, scalar2=1e-6,
                                                  op0=ALU.mult, op1=ALU.add), reads=["stB"], writes=["stw0"])
            p.op("dve", lambda e: e.tensor_scalar(out=st[:, 2:3], in0=st[:, 1:2], scalar1=-0.5, scalar2=None, op0=ALU.mult),
                 reads=["stw0"], writes=["stlo"])
            p.op("dve", lambda e: e.tensor_scalar(out=W[:, :], in0=cst[:, 128:128 + NIT], scalar1=st[:, 1:2], scalar2=None,
                                                  op0=ALU.mult), reads=["cst", "stw0"], writes=["W"])
            n1 = max(128, (nkeys // 2) // 128 * 128)
            n2 = nkeys - n1
            p.op("pool", lambda e, n2=n2: e.memset(st[:, 6:7], float(TOPK) - 0.5 - 0.5 * n2), writes=["k255"])
            for it in range(NIT):
                p.op("dve", lambda e, it=it: e.tensor_tensor(out=st[:, 3:4], in0=st[:, 2:3], in1=W[:, it:it + 1], op=ALU.add),
                     reads=["stlo", "W"], writes=["stmid"])
                p.op("dve", lambda e, n1=n1: e.tensor_scalar(out=junk[:, 0:n1], in0=score[:, 0:n1],
                                                             scalar1=st[:, 3:4], scalar2=0.0, op0=ALU.is_ge, op1=ALU.add,
                                                             accum_out=st[:, 4:5]),
                     reads=["score", "stmid"], writes=["junk", "stcnt"])
                p.op("act", lambda e, n1=n1, nkeys=nkeys: e.activation(out=junk[:, n1:nkeys], in_=score[:, n1:nkeys], func=AF.Sign,
                                                                       bias=st[:, 3:4], scale=-1.0, accum_out=sa[:, 0:1]),
                     reads=["score", "stmid"], writes=["junkB", "stS"])
                p.op("dve", lambda e: e.scalar_tensor_tensor(out=st[:, 4:5], in0=sa[:, 0:1], scalar=-0.5, in1=st[:, 4:5],
                                                             op0=ALU.mult, op1=ALU.add),
                     reads=["stS", "stcnt"], writes=["stcnt"])
                p.op("dve", lambda e, it=it: e.tensor_scalar(out=st[:, 5:6], in0=st[:, 4:5], scalar1=st[:, 6:7],
                                                             scalar2=W[:, it:it + 1], op0=ALU.is_gt, op1=ALU.mult),
                     reads=["stcnt", "k255", "W"], writes=["stt"])
                p.op("dve", lambda e: e.tensor_tensor(out=st[:, 2:3], in0=st[:, 2:3], in1=st[:, 5:6], op=ALU.add),
                     reads=["stlo", "stt"], writes=["stlo"])
            p.op("dve", lambda e: e.tensor_scalar(out=st[:, 7:8], in0=st[:, 2:3], scalar1=-1.0e29, scalar2=None, op0=ALU.max),
                 reads=["stlo"], writes=["stthr"])
            for kt_ in range(ntl):
                c0 = kt_ * 512
                w_ = min(512, nkeys - c0)
                nb_ = w_ // 128
                mi = kt_ % 2
                p.op("dve", lambda e, mi=mi, c0=c0, w_=w_: e.tensor_scalar(out=mk[mi][:, 0:w_], in0=score[:, c0:c0 + w_],
                                                                           scalar1=st[:, 7:8], scalar2=None, op0=ALU.is_ge),
                     reads=["score", "stthr"], writes=[f"mk{mi}"])
                for bb in range(nb_):
                    p.op("pe", lambda e, mi=mi, bb=bb: e.matmul(psT[:, bb * 128:(bb + 1) * 128], lhsT=mk[mi][:, bb * 128:(bb + 1) * 128],
                                                                rhs=idb[:, :], start=True, stop=True),
                         reads=[f"mk{mi}", "idb"], writes=["psT"])
                p.op("act", lambda e, kt_=kt_, nb_=nb_, j=j, w_=w_, mTv=mTv: e.activation(
                    out=mTv[:, kt_ * 4:kt_ * 4 + nb_, j * 128:(j + 1) * 128],
                    in_=psT[:, 0:w_].rearrange("p (k t) -> p k t", t=128), func=AF.Copy),
                    reads=["psT"], writes=["mT"])
        for hh in range(NH):
            p.dma("sp", q_sb[0:64, hh, 0:N].rearrange("d (j t) -> d j t", t=128), A_["aqT"][hh, :, jj0:jj0 + nq, :], writes=["qsh"])
        for h in range(NH):
            p.dma("sp", k_sb[0:64, 0:nkb * 128], A_["akT"][h, :, 0:nkb * 128], writes=["k_sb"])
            p.dma("sp", v_sb[:, 0:nkb, 0:64], A_["avh"][h, :, 0:nkb, :], writes=["v_sb"])
            def s1(kb, h=h, N=N, nkb=nkb, mTv=mTv, nq=nq, g8=g8):
                si = kb % 3
                p.op("pe", lambda e: e.matmul(
                    psS[si][:, 0:N], lhsT=k_sb[:, kb * 128:(kb + 1) * 128], rhs=aq_sb[:, h, 0:N], start=True, stop=True),
                    reads=["k_sb", "k_hi", "qsh", "q_hi"], writes=[f"psS{si}"])
                for j in range(nq):
                    which = kb - (g8 + 2 * j - 1)
                    if 0 <= which <= 2:
                        p.op("dve", lambda e, j=j, which=which: e.tensor_tensor(
                            out=psS[si][:, j * 128:(j + 1) * 128], in0=psS[si][:, j * 128:(j + 1) * 128],
                            in1=bt[:, which, h, :], op=ALU.add), reads=[f"psS{si}", "bt"], writes=[f"psS{si}"])
                p.op("act", lambda e: e.activation(out=PT[si][:, 0:N], in_=psS[si][:, 0:N], func=AF.Exp, scale=0.125),
                     reads=[f"psS{si}"], writes=[f"PT{si}"])
                p.op("dve", lambda e: e.tensor_tensor(out=PT[si][:, 0:N], in0=PT[si][:, 0:N], in1=mTv[:, kb, :], op=ALU.mult),
                     reads=[f"PT{si}", "mT"], writes=[f"PT{si}"])

            def s2(kb, h=h, N=N, nkb=nkb):
                si = kb % 3
                p.op("pe", lambda e: e.matmul(
                    psO[:, 0:N], lhsT=v_sb[:, kb, :], rhs=PT[si][:, 0:N], start=(kb == 0), stop=(kb == nkb - 1)),
                    reads=["v_sb", f"PT{si}"], writes=["psO"])
                p.op("pe", lambda e: e.matmul(
                    psD[:, 0:N], lhsT=onesb[:, :], rhs=PT[si][:, 0:N], start=(kb == 0), stop=(kb == nkb - 1)),
                    reads=["onesb", f"PT{si}"], writes=["psD"])

            for kb in range(nkb + 2):
                if kb < nkb:
                    s1(kb)
                if kb - 2 >= 0:
                    s2(kb - 2)
            p.dma("sp", g_sb[:, 0:N].rearrange("d (j t) -> d j t", t=128), A_["agT"][h, :, jj0:jj0 + nq, :], writes=["g_sb"])
            p.op("dve", lambda e, N=N: e.tensor_scalar(out=rd[:, 0:N], in0=psD[0:64, 0:N], scalar1=1e-30, scalar2=None,
                                                       op0=ALU.max), reads=["psD"], writes=["rd"])
            p.op("dve", lambda e, N=N: e.reciprocal(out=rd[:, 0:N], in_=rd[:, 0:N]), reads=["rd"], writes=["rd"])
            p.op("act", lambda e, N=N: e.activation(out=sg[:, 0:N], in_=g_sb[:, 0:N], func=AF.Exp, scale=-1.0),
                 reads=["g_sb"], writes=["sg"])
            p.op("dve", lambda e, N=N: e.tensor_scalar(out=sg[:, 0:N], in0=sg[:, 0:N], scalar1=1.0, scalar2=None, op0=ALU.add),
                 reads=["sg"], writes=["sg"])
            p.op("dve", lambda e, N=N: e.reciprocal(out=sg[:, 0:N], in_=sg[:, 0:N]), reads=["sg"], writes=["sg"])
            p.op("pool", lambda e, N=N: e.tensor_tensor(out=sg[:, 0:N], in0=sg[:, 0:N], in1=g_sb[:, 0:N], op=ALU.mult),
                 reads=["sg", "g_sb"], writes=["sg"])
            p.op("pool", lambda e, N=N: e.tensor_tensor(out=sg[:, 0:N], in0=sg[:, 0:N], in1=rd[:, 0:N], op=ALU.mult),
                 reads=["sg", "rd"], writes=["sg"])
            O, ok_ = ob[h % 2], f"ob{h % 2}"
            p.op("dve", lambda e, N=N, O=O: e.tensor_tensor(out=O[:, 0:N], in0=psO[0:64, 0:N], in1=sg[:, 0:N], op=ALU.mult),
                 reads=["psO", "sg"], writes=[ok_])
            p.dma("pool", A_["oT"][h, :, jj0:jj0 + nq, :], O[:, 0:N].rearrange("d (j t) -> d j t", t=128), reads=[ok_], skey=ok_)
    if own:
        p.finish()
        return p.emit()


def rel_bucket_np(rel):
    n = -rel
    ret = np.where(n < 0, 16, 0)
    n = np.abs(n)
    nf = np.maximum(n, 1).astype(np.float32)
    large = 8 + (np.log(nf / np.float32(8)) / np.float32(np.log(16.0)) * np.float32(8)).astype(np.int32)
    large = np.minimum(large, 15)
    return ret + np.where(n < 8, n, large)


def a_consts(r, rel_bias):
    s = np.arange(128)
    t = np.arange(128)
    diag = np.where((s[None, :] < 64) | (t[:, None] >= 64), 0.0, NEG)
    if r == 0:
        mA, mB = diag, np.full((128, 128), NEG)
    else:
        mA, mB = np.zeros((128, 128)), diag
    padm = np.where(s[None, :] >= PADF, 0.0, NEG) * np.ones((128, 1))
    mskd = np.stack([mA, mB, padm], axis=1).astype(np.float32)
    btd = np.zeros((128, 3, 8, 128), np.float32)
    for which in range(3):
        dblk = (r + 1 - which)
        rel = (s[:, None] - (t[None, :] + dblk * 128)).astype(np.int32)
        bk = rel_bucket_np(rel)
        btd[:, which, :, :] = np.transpose(rel_bias[bk], (0, 2, 1))
    b15d = np.broadcast_to(rel_bias[15][None, :], (128, 8)).astype(np.float32).copy()
    cstd = np.zeros((128, 160), np.float32)
    cstd[:, 0:128] = np.eye(128)
    cstd[:, 128:128 + 26] = (2.0 ** -(np.arange(26) + 1.0))[None, :]
    return mskd, btd, b15d, cstd


def a_pack(aqT, agT, iqT, iw, akT, av, ikT, r, rel_bias):
    blks = np.arange(NQB) * 2 + r
    cols = (blks[:, None] * 128 + np.arange(128)[None, :]).reshape(-1)
    mskd, btd, b15d, cstd = a_consts(r, rel_bias)
    return {
        "aqT": np.ascontiguousarray(aqT[:, :, cols]),
        "agT": np.ascontiguousarray(agT[:, :, cols]),
        "iqT": np.ascontiguousarray(iqT[:, :, cols]),
        "iw": np.ascontiguousarray(iw[cols].reshape(NQB, 128, 8).transpose(1, 0, 2)),
        "akT": np.ascontiguousarray(akT),
        "avh": np.ascontiguousarray(av.reshape(NBLK, 128, 8, 64).transpose(2, 1, 0, 3)),
        "ikT": np.ascontiguousarray(ikT),
        "mskd": mskd, "btd": btd, "b15d": b15d, "cstd": cstd,
    }


T_CORE = PPAD // 2

AB_FB = ([(0 + i * 128, 128) for i in range(4)] + [(512 + i * 128, 128) for i in range(4)]
         + [(2048 + i * 128, 128) for i in range(4)] + [(2560, 64)]
         + [(2632, 128), (2760, 128), (2888, 128), (3016, 128)])
AB_FF = [(1536 + i * 128, 128) for i in range(4)] + [(3656 + i * 128, 128) for i in range(4)] + [(4168, 16)]
AB_TB = [(1024, 512), (3144, 512)]
AB_TF = [(2624, 8)]
W_CDX = W_CD + 512
CD_FB = ([(0, 128), (128, 128), (256, 128), (384, 128), (3584, 128), (3712, 128), (3840, 128), (3968, 128)]
         + [(1536 + i * 128, 128) for i in range(4)] + [(2048 + i * 128, 128) for i in range(4)])
CD_FF = [(1024 + i * 128, 128) for i in range(4)] + [(3072 + i * 128, 128) for i in range(4)]
CD_TB = [(512, 512), (2560, 512)]
CD_TF = []

_PROGS = {}


def _prog(name):
    if name not in _PROGS:
        if name == "P_AB0":
            _PROGS[name] = build_P(T_CORE, W_AB, AB_FB, AB_FF, AB_TB, AB_TF, with_out=False)
        elif name == "P_AB":
            _PROGS[name] = build_P(T_CORE, W_AB, AB_FB, AB_FF, AB_TB, AB_TF, with_out=True)
        elif name == "P_CD":
            _PROGS[name] = build_P(T_CORE, W_CDX, CD_FB, CD_FF, CD_TB, CD_TF, with_out=True)
        elif name == "F":
            _PROGS[name] = build_P(T_CORE, 0, [], [], [], [], with_out=True, final=True)
        elif name == "A":
            _PROGS[name] = build_A()
        elif name == "D":
            _PROGS[name] = build_D()
        elif name == "LG":
            _PROGS[name] = build_L("gla")
        elif name == "LR":
            _PROGS[name] = build_L("ret")
    return _PROGS[name]


def _run(name, maps):
    res = run_bass_kernel_spmd(_prog(name), maps, core_ids=list(range(NCORES)))
    return res.results


def _g_layout(g):
    return np.ascontiguousarray(np.asarray(g, np.float32).reshape(8, 128).T)


def _seq(results, key, axis):
    return [np.concatenate([results[2 * b][key], results[2 * b + 1][key]], axis=axis) for b in range(BATCH)]


def _cd_weights(w):
    w = np.asarray(w, np.float32)
    d = np.arange(64)
    sw = (d + 32) % 64
    cq_sw = np.concatenate([h * 64 + sw for h in range(4)])
    ck_sw = 256 + cq_sw
    return np.ascontiguousarray(np.concatenate([w, w[:, cq_sw], w[:, ck_sw]], axis=1))


def kernel(x, meta_tokens, rel_bias, norm_g, final_g, w_in_ab, gla_gate_w2, gla_gate_b,
           w_out_ab, w_in_cd, w_out_cd):
    x = np.asarray(x, np.float32)
    rel_bias = np.asarray(rel_bias, np.float32)
    h = np.zeros((BATCH, PPAD, D_MODEL), np.float32)
    h[:, PADF:128] = np.asarray(meta_tokens, np.float32)[None]
    h[:, 128:128 + SEQ] = x
    hT = [np.ascontiguousarray(h[b].T) for b in range(BATCH)]
    del h
    mixT = None
    T = T_CORE
    dmasks, dut = d_consts()
    lcst = l_consts()
    for layer in range(4):
        j = layer // 2
        ab = (layer % 2 == 0)
        maps = []
        for c in range(NCORES):
            b, r = divmod(c, 2)
            m = {"hT": np.ascontiguousarray(hT[b][:, r * T:(r + 1) * T]), "g": _g_layout(norm_g[layer])}
            if ab:
                m["w"] = np.ascontiguousarray(np.asarray(w_in_ab[j], np.float32))
            else:
                m["w"] = _cd_weights(w_in_cd[j])
            if layer > 0:
                m["mixT"] = np.ascontiguousarray(mixT[b][:, r * T:(r + 1) * T])
                m["wo"] = np.ascontiguousarray(np.asarray(w_out_cd[j - 1] if ab else w_out_ab[j], np.float32))
            maps.append(m)
        pname = "P_AB0" if layer == 0 else ("P_AB" if ab else "P_CD")
        res = _run(pname, maps)
        del maps
        if layer > 0:
            hT = _seq(res, "hT_new", 1)
        of_bf = _seq(res, "of_bf", 1)
        of_f32 = _seq(res, "of_f32", 1)
        ot_bf = _seq(res, "ot_bf", 0)
        ot_f32 = _seq(res, "ot_f32", 0)
        del res
        import ml_dtypes
        mixT = [np.zeros((D_MODEL, PPAD), ml_dtypes.bfloat16) for _ in range(BATCH)]
        if ab:
            maps = []
            for c in range(NCORES):
                b, r = divmod(c, 2)
                fb, ff = of_bf[b], of_f32[b]
                maps.append(a_pack(fb[0:512].reshape(8, 64, PPAD), ff[0:512].reshape(8, 64, PPAD),
                                   fb[1024:1536].reshape(8, 64, PPAD), ot_f32[b], fb[512:1024].reshape(8, 64, PPAD),
                                   ot_bf[b][:, 0:512], fb[1536:1600], r, rel_bias))
            res = _run("A", maps)
            del maps
            for c in range(NCORES):
                b, r = divmod(c, 2)
                o = res[c]["oT"].reshape(512, NQB, 128)
                blks = np.arange(NQB) * 2 + r
                keep = blks < NBLK - (1 if r == 1 else 0) if False else blks < NBLK
                mv = mixT[b][0:512].reshape(512, NBLK, 128)
                mv[:, blks[keep], :] = o[:, keep, :]
            del res
            maps = []
            for c in range(NCORES):
                b, r = divmod(c, 2)
                fb, ff = of_bf[b], of_f32[b]
                w2a = np.concatenate([np.asarray(gla_gate_w2[j], np.float32)[:, r * 128:(r + 1) * 128],
                                      np.asarray(gla_gate_b[j], np.float32)[None, r * 128:(r + 1) * 128]], axis=0)
                maps.append({"qT": np.ascontiguousarray(fb[(13 + r) * 128:(14 + r) * 128]),
                             "kT": np.ascontiguousarray(fb[(15 + r) * 128:(16 + r) * 128]),
                             "v": np.ascontiguousarray(ot_bf[b][:, 512 + r * 256:512 + (r + 1) * 256]),
                             "gT": np.ascontiguousarray(ff[512 + r * 256:512 + (r + 1) * 256]),
                             "baT": np.ascontiguousarray(ff[1024:1040]), "w2a": np.ascontiguousarray(w2a), "cst": lcst})
            res = _run("LG", maps)
            del maps
            for c in range(NCORES):
                b, r = divmod(c, 2)
                mixT[b][512 + r * 256:512 + (r + 1) * 256] = res[c]["oT"]
            del res
        else:
            maps = []
            for c in range(NCORES):
                b, r = divmod(c, 2)
                fb, ff = of_bf[b], of_f32[b]
                tabs, decc = ret_tables(r)
                maps.append({"qT": np.ascontiguousarray(fb[(0 + r) * 128:(1 + r) * 128]),
                             "kT": np.ascontiguousarray(fb[(2 + r) * 128:(3 + r) * 128]),
                             "qsT": np.ascontiguousarray(fb[(4 + r) * 128:(5 + r) * 128]),
                             "ksT": np.ascontiguousarray(fb[(6 + r) * 128:(7 + r) * 128]),
                             "v": np.ascontiguousarray(ot_bf[b][:, r * 256:(r + 1) * 256]),
                             "gT": np.ascontiguousarray(ff[r * 256:(r + 1) * 256]),
                             "tabs": tabs, "decc": decc, "cst": lcst})
            res = _run("LR", maps)
            del maps
            for c in range(NCORES):
                b, r = divmod(c, 2)
                mixT[b][r * 256:(r + 1) * 256] = res[c]["oT"]
            del res
            maps = []
            for c in range(NCORES):
                b, r = divmod(c, 2)
                fb, ff = of_bf[b], of_f32[b]
                maps.append({"qT": np.ascontiguousarray(fb[8 * 128 + r * 256:8 * 128 + (r + 1) * 256].reshape(4, 64, PPAD)),
                             "kT": np.ascontiguousarray(fb[12 * 128 + r * 256:12 * 128 + (r + 1) * 256].reshape(4, 64, PPAD)),
                             "v": np.ascontiguousarray(ot_bf[b][:, 512 + r * 256:512 + (r + 1) * 256]),
                             "gT": np.ascontiguousarray(ff[512 + r * 256:512 + (r + 1) * 256].reshape(4, 64, PPAD)),
                             "masks": dmasks, "ut": dut})
            res = _run("D", maps)
            del maps
            for c in range(NCORES):
                b, r = divmod(c, 2)
                mixT[b][512 + r * 256:512 + (r + 1) * 256] = res[c]["oT"].reshape(256, PPAD)
            del res
        del of_bf, of_f32, ot_bf, ot_f32
    maps = []
    for c in range(NCORES):
        b, r = divmod(c, 2)
        maps.append({"hT": np.ascontiguousarray(hT[b][:, r * T:(r + 1) * T]), "g": _g_layout(final_g),
                     "mixT": np.ascontiguousarray(mixT[b][:, r * T:(r + 1) * T]),
                     "wo": np.ascontiguousarray(np.asarray(w_out_cd[1], np.float32))})
    res = _run("F", maps)
    yT = _seq(res, "y", 1)
    out = np.stack([np.ascontiguousarray(yT[b][:, 128:128 + SEQ].T) for b in range(BATCH)], axis=0)
    return out.astype(np.float32)


def build_fused():
    p = Prog()
    p.fused = True
    EI = "ExternalInput"
    hT0 = p.dram("hT0", [D_MODEL, PPAD], F32, kind=EI)
    gs = p.dram("gs", [5, 128, 8], F32, kind=EI)
    w_ab = p.dram("w_ab", [2, D_MODEL, W_AB], F32, kind=EI)
    w_cdx = p.dram("w_cdx", [2, D_MODEL, W_CDX], F32, kind=EI)
    wo_ab = p.dram("wo_ab", [2, D_MODEL, D_MODEL], F32, kind=EI)
    wo_cd = p.dram("wo_cd", [2, D_MODEL, D_MODEL], F32, kind=EI)
    w2 = p.dram("w2", [2, 16, 256], F32, kind=EI)
    gb = p.dram("gb", [2, 1, 256], F32, kind=EI)
    mskd = p.dram("mskd", [2, 128, 3, 128], F32, kind=EI)
    btd = p.dram("btd", [2, 128, 3, 8, 128], F32, kind=EI)
    b15d = p.dram("b15d", [128, 8], F32, kind=EI)
    cstd = p.dram("cstd", [128, 160], F32, kind=EI)
    lcst = p.dram("lcst", [128, 3, 256], F32, kind=EI)
    tabs = p.dram("tabs", [2, 4, 128, PPAD], F32, kind=EI)
    decc = p.dram("decc", [2, 64, 2], F32, kind=EI)
    dmasks = p.dram("dmasks", [6, 128, 512], BF16, kind=EI)
    dut = p.dram("dut", [128, 256], BF16, kind=EI)
    y = p.dram("y", [D_MODEL, PPAD], F32, kind="ExternalOutput")
    of_bf = p.dram("s_of_bf", [17 * 128, PPAD], BF16)
    of_f32 = p.dram("s_of_f32", [9 * 128, PPAD], F32)
    ot_bf = p.dram("s_ot_bf", [PPAD, 1024], BF16)
    ot_f32 = p.dram("s_ot_f32", [PPAD, 8], F32)
    avh = p.dram("s_avh", [8, 128, NBLK, 64], BF16)
    mixT = p.dram("s_mixT", [D_MODEL, PPAD], BF16)
    hA = p.dram("s_hA", [D_MODEL, PPAD], F32)
    hB = p.dram("s_hB", [D_MODEL, PPAD], F32)

    fb = of_bf.ap()
    ff = of_f32.ap()
    hd3 = lambda ap_: ap_.rearrange("(h d) t -> h d t", d=64)

    def qsel(ap_, r):
        return ap_.rearrange("(h d) (j r t) -> h d j r t", d=64, r=2, t=128)[:, :, :, r, :]

    h_cur = hT0.ap()
    h_bufs = [hA.ap(), hB.ap()]
    for layer in range(4):
        j = layer // 2
        ab = (layer % 2 == 0)
        p.phase(f"P{layer}")
        io = {"hT": h_cur, "g": gs.ap()[layer], "of_bf": fb, "of_f32": ff, "ot_bf": ot_bf.ap(), "ot_f32": ot_f32.ap()}
        io["w"] = w_ab.ap()[j] if ab else w_cdx.ap()[j]
        if layer > 0:
            io["mixT"] = mixT.ap()
            io["wo"] = wo_cd.ap()[j - 1] if ab else wo_ab.ap()[j]
            io["hT_new"] = h_bufs[(layer - 1) % 2]
        if ab:
            tdst = {1024: (lambda blk: avh.ap()[:, :, blk, :].rearrange("h s d -> s h d"))}
            build_P(PPAD, W_AB, AB_FB, AB_FF, AB_TB, AB_TF, with_out=(layer > 0), p=p, io=io, tdst=tdst)
        else:
            build_P(PPAD, W_CDX, CD_FB, CD_FF, CD_TB, CD_TF, with_out=True, p=p, io=io)
        if layer > 0:
            h_cur = h_bufs[(layer - 1) % 2]
        if ab:
            for r in range(2):
                p.phase(f"A{layer}{r}")
                build_A(p=p, io={
                    "aqT": qsel(fb[0:512], r), "agT": qsel(ff[0:512], r), "iqT": qsel(fb[1024:1536], r),
                    "iw": ot_f32.ap().rearrange("(j r t) c -> t j r c", r=2, t=128)[:, :, r, :],
                    "akT": hd3(fb[512:1024]), "avh": avh.ap(), "ikT": fb[1536:1600],
                    "mskd": mskd.ap()[r], "btd": btd.ap()[r], "b15d": b15d.ap(), "cstd": cstd.ap(),
                    "oT": qsel(mixT.ap()[0:512], r)})
            for r in range(2):
                p.phase(f"G{layer}{r}")
                build_L("gla", p=p, io={
                    "qT": fb[(13 + r) * 128:(14 + r) * 128], "kT": fb[(15 + r) * 128:(16 + r) * 128],
                    "v": ot_bf.ap()[:, 512 + r * 256:512 + (r + 1) * 256],
                    "gT": ff[512 + r * 256:512 + (r + 1) * 256], "baT": ff[1024:1040],
                    "w2a": (w2.ap()[j][:, r * 128:(r + 1) * 128], gb.ap()[j][:, r * 128:(r + 1) * 128]),
                    "cst": lcst.ap(), "oT": mixT.ap()[512 + r * 256:512 + (r + 1) * 256]})
        else:
            for r in range(2):
                p.phase(f"R{layer}{r}")
                build_L("ret", p=p, io={
                    "qT": fb[(0 + r) * 128:(1 + r) * 128], "kT": fb[(2 + r) * 128:(3 + r) * 128],
                    "qsT": fb[(4 + r) * 128:(5 + r) * 128], "ksT": fb[(6 + r) * 128:(7 + r) * 128],
                    "v": ot_bf.ap()[:, r * 256:(r + 1) * 256], "gT": ff[r * 256:(r + 1) * 256],
                    "tabs": tabs.ap()[r], "decc": decc.ap()[r], "cst": lcst.ap(),
                    "oT": mixT.ap()[r * 256:(r + 1) * 256]})
            for r in range(2):
                p.phase(f"D{layer}{r}")
                build_D(p=p, io={
                    "qT": hd3(fb[8 * 128 + r * 256:8 * 128 + (r + 1) * 256]),
                    "kT": hd3(fb[12 * 128 + r * 256:12 * 128 + (r + 1) * 256]),
                    "v": ot_bf.ap()[:, 512 + r * 256:512 + (r + 1) * 256],
                    "gT": hd3(ff[512 + r * 256:512 + (r + 1) * 256]),
                    "masks": dmasks.ap(), "ut": dut.ap(),
                    "oT": hd3(mixT.ap()[512 + r * 256:512 + (r + 1) * 256])})
    p.phase("F")
    build_P(PPAD, 0, [], [], [], [], with_out=True, final=True, p=p,
            io={"hT": h_cur, "g": gs.ap()[4], "mixT": mixT.ap(), "wo": wo_cd.ap()[1], "hT_new": None, "y": y.ap()})
    p.finish()
    return p.emit()


def kernel_unfused(**kw):
    return _kernel_unfused(**kw)


_kernel_unfused = kernel


def kernel(x, meta_tokens, rel_bias, norm_g, final_g, w_in_ab, gla_gate_w2, gla_gate_b,
           w_out_ab, w_in_cd, w_out_cd):
    f32 = lambda a: np.ascontiguousarray(np.asarray(a, np.float32))
    x = f32(x)
    rel_bias = f32(rel_bias)
    if "FUSED" not in _PROGS:
        _PROGS["FUSED"] = build_fused()
    nc = _PROGS["FUSED"]
    gs = np.stack([_g_layout(norm_g[l]) for l in range(4)] + [_g_layout(final_g)], axis=0)
    ac = [a_consts(r, rel_bias) for r in range(2)]
    rt = [ret_tables(r) for r in range(2)]
    dmasks, dut = d_consts()
    shared = {
        "gs": gs, "w_ab": f32(w_in_ab), "w_cdx": np.stack([_cd_weights(w_in_cd[j]) for j in range(2)], 0),
        "wo_ab": f32(w_out_ab), "wo_cd": f32(w_out_cd), "w2": f32(gla_gate_w2),
        "gb": f32(gla_gate_b).reshape(2, 1, 256),
        "mskd": np.stack([ac[0][0], ac[1][0]], 0), "btd": np.stack([ac[0][1], ac[1][1]], 0),
        "b15d": ac[0][2], "cstd": ac[0][3], "lcst": l_consts(),
        "tabs": np.stack([rt[0][0], rt[1][0]], 0), "decc": np.stack([rt[0][1], rt[1][1]], 0),
        "dmasks": dmasks, "dut": dut,
    }
    maps = []
    for c in range(NCORES):
        b = c // 2
        h = np.zeros((PPAD, D_MODEL), np.float32)
        h[PADF:128] = np.asarray(meta_tokens, np.float32)
        h[128:128 + SEQ] = x[b]
        m = dict(shared)
        m["hT0"] = np.ascontiguousarray(h.T)
        maps.append(m)
    res = run_bass_kernel_spmd(nc, maps, core_ids=list(range(NCORES))).results
    out = np.stack([np.ascontiguousarray(res[2 * b]["y"][:, 128:128 + SEQ].T) for b in range(BATCH)], axis=0)
    return out.astype(np.float32)
```
